# Optimizing a Trainium2 kernel written in Bass

```python
import math
import jax, jax.numpy as jnp
from jax import lax
import numpy as np

D_MODEL = 1024
BATCH = 4
SEQ = 8192
DEPTH = 1

GDN_HEADS = 4
GDN_DK = 128
GDN_DV = 128
CONV_K = 4
CHUNK = 64
NSA_HEADS = 8
NSA_KV_HEADS = 2
NSA_GROUP = NSA_HEADS // NSA_KV_HEADS
NSA_DH = 64
CMP_LEN = 32
CMP_STRIDE = 16
CMP_HIDDEN = 128
SEL_BLOCK = 64
SEL_TOP_N = 16
WINDOW = 512
Q_BLOCK = 128
FORCE_BONUS = 1e4
N_BUCKETS = 32
MAX_DISTANCE = 1024
EPS = 1e-6
NEG = -1e30

GDN_QK_W = GDN_HEADS * GDN_DK
GDN_W = GDN_HEADS * GDN_DV
GDN_CONV_W = 2 * GDN_QK_W + GDN_W
NSA_W = NSA_HEADS * NSA_DH
KV_W = NSA_KV_HEADS * NSA_DH
MIX_W = GDN_W + NSA_W
PROJ_SIZES = (GDN_QK_W, GDN_QK_W, GDN_W, GDN_W, GDN_HEADS, GDN_HEADS,
              NSA_W, KV_W, KV_W, KV_W, KV_W, KV_W, KV_W, 3 * NSA_HEADS, NSA_W)
PROJ_W = sum(PROJ_SIZES)

kernel_name = "hymba_gdn_nsa_hybrid"


def rmsnorm(x, w):
    xf = x.astype(jnp.float32)
    y = xf * lax.rsqrt(jnp.mean(xf * xf, axis=-1, keepdims=True) + EPS)
    return (y * w.astype(jnp.float32)).astype(x.dtype)


def l2norm(t):
    return t * lax.rsqrt(jnp.sum(t * t, axis=-1, keepdims=True) + EPS)


def causal_conv(x, w):
    c = x.shape[-1]
    return lax.conv_general_dilated(x, w.astype(x.dtype)[:, None, :], window_strides=(1,),
                                    padding=[(CONV_K - 1, 0)],
                                    dimension_numbers=('NWC', 'WIO', 'NWC'),
                                    feature_group_count=c)


def rel_bucket(dist):
    n = jnp.maximum(dist, 0)
    max_exact = N_BUCKETS // 2
    large = max_exact + (jnp.log(jnp.maximum(n, max_exact).astype(jnp.float32) / max_exact)
                         / math.log(MAX_DISTANCE / max_exact) * (N_BUCKETS - max_exact)).astype(jnp.int32)
    return jnp.where(n < max_exact, n, jnp.minimum(large, N_BUCKETS - 1))


def gated_delta_rule(q, k, v, beta, g):
    B, S, H, dk = q.shape
    dv = v.shape[-1]
    N = S // CHUNK

    def chunk(t):
        return t.reshape((B, N, CHUNK, H) + t.shape[3:]).swapaxes(2, 3)

    q, k, v, beta, g = chunk(q), chunk(k), chunk(v), chunk(beta), chunk(g)
    g = jnp.cumsum(g, axis=-1)
    idx = jnp.arange(CHUNK)
    tril = idx[:, None] >= idx[None, :]
    tril_strict = idx[:, None] > idx[None, :]
    diff = g[..., :, None] - g[..., None, :]
    decay = jnp.where(tril, jnp.exp(jnp.where(tril, diff, 0.0)), 0.0)
    kb = k * beta[..., None]
    L = jnp.where(tril_strict, jnp.einsum('bnhcd,bnhed->bnhce', kb, k) * decay, 0.0)
    A = L + jnp.eye(CHUNK, dtype=L.dtype)
    rhs = jnp.concatenate([v * beta[..., None], kb * jnp.exp(g)[..., None]], axis=-1)
    sol = lax.linalg.triangular_solve(A, rhs, left_side=True, lower=True, unit_diagonal=True)
    u, w = sol[..., :dv], sol[..., dv:]
    attn_qk = jnp.where(tril, jnp.einsum('bnhcd,bnhed->bnhce', q, k) * decay, 0.0)
    g_last = g[..., -1]
    k_dec = k * jnp.exp(g_last[..., None] - g)[..., None]
    q_dec = q * jnp.exp(g)[..., None]

    def step(state, inp):
        qd, kd, u_, w_, a_, gl = inp
        v_new = u_ - jnp.einsum('bhcd,bhde->bhce', w_, state)
        o = jnp.einsum('bhcd,bhde->bhce', qd, state) + jnp.einsum('bhce,bhef->bhcf', a_, v_new)
        state = state * jnp.exp(gl)[..., None, None] + jnp.einsum('bhcd,bhce->bhde', kd, v_new)
        return state, o

    xs = tuple(jnp.moveaxis(t, 1, 0) for t in (q_dec, k_dec, u, w, attn_qk, g_last))
    s0 = jnp.zeros((B, H, dk, dv), jnp.float32)
    _, o = lax.scan(step, s0, xs)
    return jnp.moveaxis(o, 0, 1).swapaxes(2, 3).reshape(B, S, H, dv)


def compress(raw, pe, w1, w2):
    S = raw.shape[1]
    n_cmp = (S - CMP_LEN) // CMP_STRIDE + 1
    idx = CMP_STRIDE * np.arange(n_cmp)[:, None] + np.arange(CMP_LEN)[None, :]
    blocks = raw[:, idx] + pe[None, None, :, None, :]
    h = jax.nn.silu(jnp.einsum('bnlhd,ldf->bnhf', blocks, w1))
    return jnp.einsum('bnhf,fd->bhnd', h, w2)


def nsa_mixer(nq, kc, vc, ks, vs, kw, vw, ng, pe_k, pe_v, kw1, kw2, vw1, vw2, rel_bias):
    f32 = jnp.float32
    B, S, _ = nq.shape
    Hk, G, dh = NSA_KV_HEADS, NSA_GROUP, NSA_DH
    n_sel = S // SEL_BLOCK
    n_top = min(SEL_TOP_N, n_sel)
    n_cmp = (S - CMP_LEN) // CMP_STRIDE + 1
    q = nq.astype(f32).reshape(B, S, Hk, G, dh).transpose(0, 2, 3, 1, 4) * dh ** -0.5
    gates = jax.nn.sigmoid(ng.astype(f32)).reshape(B, S, 3, Hk, G).transpose(0, 3, 4, 1, 2)

    def heads(t):
        return t.astype(f32).reshape(B, S, Hk, dh)

    k_cmp = compress(heads(kc), pe_k.astype(f32), kw1.astype(f32), kw2.astype(f32))
    v_cmp = compress(heads(vc), pe_v.astype(f32), vw1.astype(f32), vw2.astype(f32))
    k_slc = heads(ks).transpose(0, 2, 1, 3).reshape(B, Hk, n_sel, SEL_BLOCK, dh)
    v_slc = heads(vs).transpose(0, 2, 1, 3).reshape(B, Hk, n_sel, SEL_BLOCK, dh)
    pad = ((0, 0), (0, 0), (WINDOW, 0), (0, 0))
    k_win = jnp.pad(heads(kw).transpose(0, 2, 1, 3), pad)
    v_win = jnp.pad(heads(vw).transpose(0, 2, 1, 3), pad)
    rel_tab = rel_bias.astype(f32).reshape(N_BUCKETS, Hk, G)

    cs = CMP_STRIDE * np.arange(n_cmp)
    ss = SEL_BLOCK * np.arange(n_sel)
    overlap = jnp.asarray(((cs[:, None] < ss[None, :] + SEL_BLOCK) &
                           (cs[:, None] + CMP_LEN > ss[None, :])).astype(np.float32))
    cmp_end = jnp.asarray((cs + CMP_LEN - 1).astype(np.int32))
    b_ix = jnp.arange(B)[:, None, None, None]
    h_ix = jnp.arange(Hk)[None, :, None, None]
    h_ix5 = jnp.arange(Hk)[None, :, None, None, None]
    g_ix5 = jnp.arange(G)[None, None, :, None, None]
    j = jnp.arange(n_sel)

    def block(qb):
        q0 = qb * Q_BLOCK
        t = q0 + jnp.arange(Q_BLOCK)
        qblk = lax.dynamic_slice_in_dim(q, q0, Q_BLOCK, axis=3)
        gblk = lax.dynamic_slice_in_dim(gates, q0, Q_BLOCK, axis=3)
        mask_c = cmp_end[None, :] <= t[:, None]
        bias_c = rel_tab[rel_bucket(t[:, None] - cmp_end[None, :])].transpose(2, 3, 0, 1)
        s_c = jnp.einsum('bhgtd,bhnd->bhgtn', qblk, k_cmp) + bias_c
        p_c = jax.nn.softmax(jnp.where(mask_c, s_c, NEG), axis=-1) * mask_c.astype(f32)
        o_c = jnp.einsum('bhgtn,bhnd->bhgtd', p_c, v_cmp)
        imp = jnp.einsum('bhgtn,nj->bhtj', p_c, overlap)
        cb = t // SEL_BLOCK
        valid = j[None, :] <= cb[:, None]
        forced = (j[None, :] == 0) | (j[None, :] == cb[:, None]) | (j[None, :] == cb[:, None] - 1)
        score = jnp.where(valid, imp + jnp.where(forced, FORCE_BONUS, 0.0), NEG)
        _, sel = lax.top_k(score, n_top)
        kb = k_slc[b_ix, h_ix, sel].reshape(B, Hk, Q_BLOCK, n_top * SEL_BLOCK, dh)
        vb = v_slc[b_ix, h_ix, sel].reshape(B, Hk, Q_BLOCK, n_top * SEL_BLOCK, dh)
        pos = (sel[..., None] * SEL_BLOCK + jnp.arange(SEL_BLOCK)).reshape(B, Hk, Q_BLOCK, n_top * SEL_BLOCK)
        dist_s = t[None, None, :, None] - pos
        bias_s = rel_tab[rel_bucket(dist_s)[:, :, None], h_ix5, g_ix5]
        s_s = jnp.einsum('bhgtd,bhtkd->bhgtk', qblk, kb) + bias_s
        p_s = jax.nn.softmax(jnp.where((dist_s >= 0)[:, :, None], s_s, NEG), axis=-1)
        o_s = jnp.einsum('bhgtk,bhtkd->bhgtd', p_s, vb)
        kwb = lax.dynamic_slice_in_dim(k_win, q0, WINDOW + Q_BLOCK, axis=2)
        vwb = lax.dynamic_slice_in_dim(v_win, q0, WINDOW + Q_BLOCK, axis=2)
        kpos = q0 - WINDOW + jnp.arange(WINDOW + Q_BLOCK)
        dist_w = t[:, None] - kpos[None, :]
        mask_w = (dist_w >= 0) & (dist_w < WINDOW) & (kpos[None, :] >= 0)
        bias_w = rel_tab[rel_bucket(dist_w)].transpose(2, 3, 0, 1)
        s_w = jnp.einsum('bhgtd,bhkd->bhgtk', qblk, kwb) + bias_w
        p_w = jax.nn.softmax(jnp.where(mask_w, s_w, NEG), axis=-1)
        o_w = jnp.einsum('bhgtk,bhkd->bhgtd', p_w, vwb)
        return gblk[..., 0:1] * o_c + gblk[..., 1:2] * o_s + gblk[..., 2:3] * o_w

    outs = lax.map(block, jnp.arange(S // Q_BLOCK))
    return outs.transpose(1, 0, 4, 2, 3, 5).reshape(B, S, NSA_W)


def setup_inputs(seed: int = 0) -> dict:
    key = jax.random.key(seed)
    ks = jax.random.split(key, 18)
    nrm = jax.random.normal
    D = D_MODEL
    dt = jnp.exp(jax.random.uniform(ks[5], (DEPTH, GDN_HEADS)) * (math.log(0.1) - math.log(0.001)) + math.log(0.001))
    return {
        "x": nrm(ks[0], (BATCH, SEQ, D), jnp.float32),
        "norm_w": 1.0 + 0.02 * nrm(ks[1], (DEPTH, D)),
        "w_in": nrm(ks[2], (DEPTH, D, PROJ_W)) * D ** -0.5,
        "conv_w": nrm(ks[3], (DEPTH, CONV_K, GDN_CONV_W)) * CONV_K ** -0.5,
        "a_log": jnp.log(jax.random.uniform(ks[4], (DEPTH, GDN_HEADS), minval=1.0, maxval=16.0)),
        "dt_bias": dt + jnp.log(-jnp.expm1(-dt)),
        "gdn_norm_w": 1.0 + 0.02 * nrm(ks[6], (DEPTH, GDN_DV)),
        "cmp_pe_k": 0.1 * nrm(ks[7], (DEPTH, CMP_LEN, NSA_DH)),
        "cmp_pe_v": 0.1 * nrm(ks[8], (DEPTH, CMP_LEN, NSA_DH)),
        "cmp_k_w1": nrm(ks[9], (DEPTH, CMP_LEN, NSA_DH, CMP_HIDDEN)) * (CMP_LEN * NSA_DH) ** -0.5,
        "cmp_k_w2": nrm(ks[10], (DEPTH, CMP_HIDDEN, NSA_DH)) * CMP_HIDDEN ** -0.5,
        "cmp_v_w1": nrm(ks[11], (DEPTH, CMP_LEN, NSA_DH, CMP_HIDDEN)) * (CMP_LEN * NSA_DH) ** -0.5,
        "cmp_v_w2": nrm(ks[12], (DEPTH, CMP_HIDDEN, NSA_DH)) * CMP_HIDDEN ** -0.5,
        "w_out": nrm(ks[13], (DEPTH, MIX_W, D)) * MIX_W ** -0.5,
        "rel_bias": 0.5 * nrm(ks[14], (N_BUCKETS, NSA_HEADS)),
        "final_norm_w": 1.0 + 0.02 * nrm(ks[15], (D,)),
    }


def reference(x, norm_w, w_in, conv_w, a_log, dt_bias, gdn_norm_w, cmp_pe_k, cmp_pe_v,
              cmp_k_w1, cmp_k_w2, cmp_v_w1, cmp_v_w2, w_out, rel_bias, final_norm_w):
    f32 = jnp.float32
    B, S, _ = x.shape
    split_points = [int(p) for p in np.cumsum(PROJ_SIZES)[:-1]]
    h = x
    for layer in range(DEPTH):
        u = rmsnorm(h, norm_w[layer])
        proj = u @ w_in[layer]
        (gq, gk, gv, gz, gb, ga, nq, kc, vc, ks, vs, kw, vw, ng, nz) = jnp.split(proj, split_points, axis=-1)
        qkv = jax.nn.silu(causal_conv(jnp.concatenate([gq, gk, gv], axis=-1), conv_w[layer])).astype(f32)
        q, k, v = jnp.split(qkv, [GDN_QK_W, 2 * GDN_QK_W], axis=-1)
        q = l2norm(q.reshape(B, S, GDN_HEADS, GDN_DK)) * GDN_DK ** -0.5
        k = l2norm(k.reshape(B, S, GDN_HEADS, GDN_DK))
        v = v.reshape(B, S, GDN_HEADS, GDN_DV)
        beta = jax.nn.sigmoid(gb.astype(f32))
        g = -jnp.exp(a_log[layer].astype(f32)) * jax.nn.softplus(ga.astype(f32) + dt_bias[layer].astype(f32))
        o_g = gated_delta_rule(q, k, v, beta, g)
        o_g = o_g * lax.rsqrt(jnp.mean(o_g * o_g, axis=-1, keepdims=True) + EPS) * gdn_norm_w[layer].astype(f32)
        o_g = o_g.reshape(B, S, GDN_W) * jax.nn.silu(gz.astype(f32))
        o_n = nsa_mixer(nq, kc, vc, ks, vs, kw, vw, ng, cmp_pe_k[layer], cmp_pe_v[layer],
                        cmp_k_w1[layer], cmp_k_w2[layer], cmp_v_w1[layer], cmp_v_w2[layer], rel_bias)
        o_n = o_n * jax.nn.silu(nz.astype(f32))
        mix = jnp.concatenate([o_g, o_n], axis=-1).astype(x.dtype)
        h = h + mix @ w_out[layer]
    return rmsnorm(h, final_norm_w)
```

```python
import numpy as np
import concourse.bass as bass
import concourse.mybir as mybir
from concourse.bass_utils import run_bass_kernel_spmd

F32 = mybir.dt.float32
BF16 = mybir.dt.bfloat16
ALU = mybir.AluOpType
AF = mybir.ActivationFunctionType

S = 8192
NT = 64
D = 1024
EPS = 1e-6
NEG = -1e30
MNEG = -30000.0


class Buf:
    __slots__ = ("w", "r", "bank")

    def __init__(self, bank=None):
        self.w = None
        self.r = {}
        self.bank = bank


class Tile:
    def __init__(self, t):
        self.t = t
        self.b = Buf()


class DSem:
    def __init__(self, nc, name):
        self.h = nc.alloc_semaphore(name)
        self.count = 0
        self.key = ("dma", name)


class KB:
    def __init__(self, nc):
        self.nc = nc
        self.eng = {"pe": nc.tensor, "dve": nc.vector, "act": nc.scalar, "pool": nc.gpsimd, "sp": nc.sync}
        self.semh = {}
        self.cnt = {}
        self.seen = {}
        for e in self.eng:
            self.semh[("eng", e)] = nc.alloc_semaphore("c_" + e)
            self.cnt[e] = 0
            self.seen[e] = {}
        self.dsems = []
        self.nd = 0
        self.nops = 0
        self.pe_bank_excl = True
        import os
        self.limit = int(os.environ.get("KBLIMIT", "1000000000"))

    def dsem(self, name=None):
        self.nd += 1
        d = DSem(self.nc, name or f"d{self.nd}")
        self.semh[d.key] = d.h
        self.dsems.append(d)
        return d

    def _wait(self, e, dep):
        key, val = dep
        if key == ("eng", "pe") and e == "pe":
            return
        if self.seen[e].get(key, 0) >= val:
            return
        self.eng[e].wait_ge(self.semh[key], val)
        self.seen[e][key] = val

    def _deps(self, e, R, W):
        for b in R:
            if b.w is not None:
                self._wait(e, b.w)
        for b in W:
            if b.w is not None:
                self._wait(e, b.w)
            for d in b.r.values():
                self._wait(e, d)

    def op(self, e, fn, R=(), W=()):
        self.nops += 1
        if self.nops > self.limit:
            return
        R = [x.b if isinstance(x, Tile) else x for x in R]
        W = [x.b if isinstance(x, Tile) else x for x in W]
        self._deps(e, R, W)
        if e != "pe":
            for b in R:
                if b.bank is not None:
                    for e2, tk in b.bank.items():
                        if e2 != e:
                            self._wait(e, tk)
        elif self.pe_bank_excl:
            for b in W:
                if b.bank is not None:
                    for e2, tk in b.bank.items():
                        self._wait(e, tk)
        ins = fn(self.eng[e])
        self.cnt[e] += 1
        ins.then_inc(self.semh[("eng", e)], 1)
        tok = (("eng", e), self.cnt[e])
        for b in R:
            b.r[tok[0]] = tok
            if b.bank is not None and e != "pe":
                b.bank[e] = tok
        for b in W:
            b.w = tok
            b.r = {}

    def dma(self, q, out, in_, R=(), W=(), sem=None):
        self.nops += 1
        if self.nops > self.limit:
            return
        R = [x.b if isinstance(x, Tile) else x for x in R]
        W = [x.b if isinstance(x, Tile) else x for x in W]
        self._deps(q, R, W)
        ins = self.eng[q].dma_start(out=out, in_=in_)
        sem.count += 16
        ins.then_inc(sem.h, 16)
        tok = (sem.key, sem.count)
        for b in R:
            b.r[tok[0]] = tok
        for b in W:
            b.w = tok
            b.r = {}

    def seal(self, sem, tiles):
        for t in tiles:
            b = t.b if isinstance(t, Tile) else t
            b.w = (sem.key, sem.count)

    def barrier(self):
        for e in self.eng:
            for e2 in self.eng:
                if e2 != e and self.cnt[e2] > 0:
                    self._wait(e, (("eng", e2), self.cnt[e2]))
            for d in self.dsems:
                if d.count > 0:
                    self._wait(e, (d.key, d.count))

    def final_wait(self, sems):
        for d in sems:
            if d.count > 0:
                self._wait("sp", (d.key, d.count))


CIDX = {}
for _ct in range(4):
    for _i in range(32):
        if 2 * _i + 1 - 16 * _ct >= 0 and 2 * _i - 16 * _ct < 23:
            CIDX[(_i, _ct)] = len(CIDX)


def build(PH=99, DBG=(), NBLK=None, GT=None, NSL=None, DSLOT=-1):
    nc = bass.Bass("TRN2", target_bir_lowering=False)
    try:
        nc.allow_low_precision("bf16 matmul operands, fp32 accumulation")
        nc.allow_non_contiguous_dma("strided scratch layouts")
    except Exception:
        pass
    kb = KB(nc)
    V, A, P, PE = "dve", "act", "pool", "pe"

    def din(name, shape, dt=F32):
        return nc.dram_tensor(name, list(shape), dt, kind="ExternalInput").ap()

    def dscr(name, shape, dt):
        return nc.dram_tensor(name, list(shape), dt).ap()

    from contextlib import ExitStack
    stk = [None]

    def sb(name, shape, dt=F32):
        if stk[0] is None:
            return Tile(nc.alloc_sbuf_tensor(name, list(shape), dt))
        return Tile(stk[0].enter_context(nc.sbuf_tensor(name, list(shape), dt)))

    xT = din("xT", [D, S])
    xr = din("xr", [S // 2, D])
    w_in = din("w_in", [D, 3872])
    nw = din("nw", [128, 8])
    convw = din("convw", [128, 12, 4])
    alog_b = din("alog_b", [128, 256])
    dtb_b = din("dtb_b", [128, 256])
    gnw_b = din("gnw_b", [128, 512])
    fnw_b = din("fnw_b", [128, D])
    w_out = din("w_out", [D, D])
    w1k = din("w1k", [64, 32, 128]); w1v = din("w1v", [64, 32, 128])
    pek = din("pek", [64, 32]); pev = din("pev", [64, 32])
    w2k = din("w2k", [128, 64]); w2v = din("w2v", [128, 64])
    tabS = din("tabS", [2, 9, 128, 512])
    tabWx = din("tabWx", [2, 2, 128, 512])
    tabCfar = din("tabCfar", [2, 128, 512])
    tabC = din("tabC", [2, 44, 128, 512])
    cexp = din("cexp", [128, 8])
    ovl = din("ovl", [128, 4, 128])
    adjT = din("adjT", [128, 256])
    j0b = din("j0b", [128, 32])
    pw = din("pw", [128, 2])
    consts = din("consts", [128, 1024])
    estk = din("estk", [64, S])
    y = nc.dram_tensor("y", [S // 2, D], F32, kind="ExternalOutput").ap()
    dbg_out = {}
    for name, shape, dt in DBG:
        dbg_out[name] = nc.dram_tensor("dbg_" + name, list(shape), dt, kind="ExternalOutput").ap()

    GQKV = dscr("GQKV", [12, 128, S], BF16)
    NQ = dscr("NQ", [8, 64, S], BF16)
    KVT = dscr("KVT", [4, 128, S], BF16)
    TM = dscr("TM", [S, 1312], F32)
    MIXG = dscr("MIXG", [S, 512], BF16)
    scr_b = {n: Buf() for n in ("GQKV", "NQ", "KVT", "TM", "MIXG")}

    cst = sb("cst", [128, 1024])
    ident = cst.t[:, 0:128]
    mask2 = cst.t[:, 128:384]
    tri2 = cst.t[:, 256:384]
    blk2 = cst.t[:, 384:512]
    sel0 = cst.t[:, 512:640]
    sel1 = cst.t[:, 640:768]
    nvalid = cst.t[:, 768:772]
    ones_f = cst.t[:, 772:900]
    epsc = cst.t[:, 900:901]
    identb = sb("identb", [128, 128], BF16)
    onesb = sb("onesb", [128, 128], BF16)
    gba = sb("gba", [128, 64, 8])
    ld = kb.dsem("ld0")
    kb.dma("sp", cst.t[:], consts, W=[cst], sem=ld)
    kb.op(V, lambda e: e.tensor_copy(out=identb.t[:], in_=ident), R=[cst], W=[identb])
    kb.op(V, lambda e: e.tensor_copy(out=onesb.t[:], in_=ones_f), R=[cst], W=[onesb])

    psum = [Tile(nc.alloc_psum_tensor(f"ps{i}", [128, 512], F32)) for i in range(7)]
    psb = Tile(nc.alloc_psum_tensor("psb", [128, 1024], BF16))
    for p_ in psum + [psb]:
        p_.b.bank = {}
    out_sems = []

    def dbg_dump(name, ap_src, tile, rows=128):
        if name in dbg_out:
            d = kb.dsem()
            out_sems.append(d)
            kb.dma("sp", dbg_out[name], ap_src, R=[tile], sem=d)

    stk[0] = ExitStack()
    Wb = sb("Wb", [128, 8, 3872], BF16)
    cw = sb("cw", [128, 12, 4])
    nwt = sb("nwt", [128, 8])
    ld1 = kb.dsem("ld1")
    kb.dma("sp", cw.t[:], convw, W=[cw], sem=ld1)
    kb.dma("sp", nwt.t[:], nw, W=[nwt], sem=ld1)
    kb.seal(ld1, [cw, nwt])
    ph1stk = stk[0]
    stk[0] = ExitStack()
    wst = [sb(f"wst{i}", [128, 3872]) for i in range(2)]
    wsem = [kb.dsem(f"wl{i}") for i in range(2)]
    w_v = w_in.rearrange("(kc p) c -> p kc c", p=128)
    for kc in range(8):
        st = wst[kc % 2]
        kb.dma("sp" if kc % 2 == 0 else "act", st.t[:], w_v[:, kc, :], W=[st], sem=wsem[kc % 2])
        half = 1936
        kb.op(V, lambda e, st=st, kc=kc: e.tensor_scalar(out=Wb.t[:, kc, 0:half], in0=st.t[:, 0:half], scalar1=nwt.t[:, kc:kc + 1], scalar2=None, op0=ALU.mult), R=[st, nwt], W=[Wb])
        kb.op(A, lambda e, st=st, kc=kc: e.activation(out=Wb.t[:, kc, half:3872], in_=st.t[:, half:3872], func=AF.Copy, scale=nwt.t[:, kc:kc + 1]), R=[st, nwt], W=[Wb])

    kb.barrier()
    stk[0].close()
    stk[0] = ph1stk
    xf = [sb(f"xf{i}", [128, 8, 512]) for i in range(2)]
    xsem = [kb.dsem(f"xl{i}") for i in range(2)]
    xb = sb("xb", [128, 8, 512], BF16)
    sq = sb("sq", [128, 8, 512], BF16)
    rbc = sb("rbc", [128, 512])
    cbuf = [sb(f"cbuf{i}", [128, 515]) for i in range(12)]
    for c in cbuf:
        kb.op(P, lambda e, c=c: e.memset(c.t[:, 0:3], 0.0), W=[c])
    acc = [sb(f"acc{i}", [128, 512]) for i in range(4)]
    slb = [sb(f"slb{i}", [128, 512]) for i in range(4)]
    sq2 = [sb(f"sq2{i}", [128, 512]) for i in range(4)]
    rn = [sb(f"rn{i}", [128, 512]) for i in range(4)]
    ptmp = sb("ptmp", [128, 512])
    stg = [sb(f"stg{i}", [128, 512], BF16) for i in range(4)]
    stsem = [kb.dsem(f"st{i}") for i in range(4)]
    tmb = [sb(f"tmb{i}", [128, 1312]) for i in range(2)]
    tmsem = [kb.dsem(f"tm{i}") for i in range(2)]
    rcol = [sb(f"rcol{i}", [128, 2]) for i in range(2)]
    xT_v = xT.rearrange("(kc p) t -> p kc t", p=128)
    nstg = [0]

    def store_stage(src_tile_fn, dram_aps):
        i = nstg[0] % 4
        nstg[0] += 1
        st = stg[i]
        src_tile_fn(st)
        for (dap, p0, p1, bkey) in dram_aps:
            kb.dma("pool", dap, st.t[p0:p1, :], R=[st], sem=stsem[i])

    NB = (16 if PH >= 1 else 0) if NBLK is None else NBLK
    for tb in range(NB):
        t0 = tb * 512
        xt = xf[tb % 2]
        kb.dma("sp", xt.t[:], xT_v[:, :, t0:t0 + 512], W=[xt], sem=xsem[tb % 2])
        kb.op(V, lambda e: e.tensor_copy(out=xb.t[:, 0:4, :], in_=xt.t[:, 0:4, :]), R=[xt], W=[xb])
        kb.op(V, lambda e: e.tensor_copy(out=xb.t[:, 4:8, :], in_=xt.t[:, 4:8, :]), R=[xt], W=[xb])
        kb.op(A, lambda e: e.activation(out=sq.t[:], in_=xt.t[:], func=AF.Square), R=[xt], W=[sq])
        pr = psum[0]
        for kc in range(8):
            kb.op(PE, lambda e, kc=kc: e.matmul(pr.t[:], lhsT=onesb.t[:], rhs=sq.t[:, kc, :], start=(kc == 0), stop=(kc == 7)), R=[onesb, sq], W=[pr])
        kb.op(A, lambda e: e.activation(out=rbc.t[:], in_=pr.t[:], func=AF.Ln, bias=epsc, scale=1.0 / D), R=[pr, cst], W=[rbc])
        kb.op(A, lambda e: e.activation(out=rbc.t[:], in_=rbc.t[:], func=AF.Exp, scale=-0.5), R=[rbc], W=[rbc])
        pending = []
        for m in range(20):
            pm = psum[(1, 2, 3, 0)[m % 4]]
            for kc in range(8):
                kb.op(PE, lambda e, kc=kc, m=m, pm=pm: e.matmul(pm.t[:], lhsT=Wb.t[:, kc, m * 128:(m + 1) * 128], rhs=xb.t[:, kc, :], start=(kc == 0), stop=(kc == 7)), R=[Wb, xb], W=[pm])
            while len(pending) > 2:
                pending.pop(0)()
            if m < 12:
                cb = cbuf[m]
                ce = V
                a_ = acc[m % 4]; s_ = slb[m % 4]
                kb.op(V, lambda e, pm=pm, cb=cb: e.tensor_tensor(out=cb.t[:, 3:515], in0=pm.t[:], in1=rbc.t[:], op=ALU.mult), R=[pm, rbc], W=[cb])
                kb.op(ce, lambda e, cb=cb, a_=a_, m=m: e.tensor_scalar(out=a_.t[:], in0=cb.t[:, 3:515], scalar1=cw.t[:, m, 3:4], scalar2=None, op0=ALU.mult), R=[cb, cw], W=[a_])
                for j in (2, 1, 0):
                    if ce == V:
                        kb.op(ce, lambda e, cb=cb, a_=a_, m=m, j=j: e.scalar_tensor_tensor(out=a_.t[:], in0=cb.t[:, j:j + 512], scalar=cw.t[:, m, j:j + 1], in1=a_.t[:], op0=ALU.mult, op1=ALU.add), R=[cb, cw, a_], W=[a_])
                    else:
                        kb.op(ce, lambda e, cb=cb, m=m, j=j: e.tensor_scalar(out=ptmp.t[:], in0=cb.t[:, j:j + 512], scalar1=cw.t[:, m, j:j + 1], scalar2=None, op0=ALU.mult), R=[cb, cw], W=[ptmp])
                        kb.op(ce, lambda e, a_=a_: e.tensor_tensor(out=a_.t[:], in0=a_.t[:], in1=ptmp.t[:], op=ALU.add), R=[ptmp, a_], W=[a_])
                kb.op(P, lambda e, cb=cb: e.tensor_copy(out=cb.t[:, 0:3], in_=cb.t[:, 512:515]), R=[cb], W=[cb])
                if m < 8:
                    kb.op(A, lambda e, a_=a_, s_=s_: e.activation(out=s_.t[:], in_=a_.t[:], func=AF.Silu), R=[a_], W=[s_])
                    q2 = sq2[m % 4]; r2 = rn[m % 4]
                    kb.op(A, lambda e, s_=s_, q2=q2: e.activation(out=q2.t[:], in_=s_.t[:], func=AF.Square), R=[s_], W=[q2])
                    pn = psum[4 + (m % 2)]

                    def fin(m=m, s_=s_, q2=q2, r2=r2, pn=pn, t0=t0):
                        kb.op(PE, lambda e: e.matmul(pn.t[:], lhsT=ones_f, rhs=q2.t[:], start=True, stop=True), R=[cst, q2], W=[pn])
                        kb.op(A, lambda e: e.activation(out=r2.t[:], in_=pn.t[:], func=AF.Ln, bias=epsc, scale=1.0), R=[pn, cst], W=[r2])
                        kb.op(A, lambda e: e.activation(out=r2.t[:], in_=r2.t[:], func=AF.Exp, scale=-0.5), R=[r2], W=[r2])
                        scl = (128.0 ** -0.5) if m < 4 else 1.0
                        store_stage(lambda st: kb.op(V, lambda e: e.scalar_tensor_tensor(out=st.t[:], in0=s_.t[:], scalar=scl, in1=r2.t[:], op0=ALU.mult, op1=ALU.mult), R=[s_, r2], W=[st]),
                                    [(GQKV[m, :, t0:t0 + 512], 0, 128, "GQKV")])
                    pending.append(fin)
                else:
                    store_stage(lambda st, a_=a_: kb.op(A, lambda e: e.activation(out=st.t[:], in_=a_.t[:], func=AF.Silu), R=[a_], W=[st]),
                                [(GQKV[m, :, t0:t0 + 512], 0, 128, "GQKV")])
            elif m < 16:
                hp = m - 12
                store_stage(lambda st, pm=pm: kb.op(V, lambda e: e.scalar_tensor_tensor(out=st.t[:], in0=pm.t[:], scalar=0.125, in1=rbc.t[:], op0=ALU.mult, op1=ALU.mult), R=[pm, rbc], W=[st]),
                            [(NQ[2 * hp, :, t0:t0 + 512], 0, 64, "NQ"), (NQ[2 * hp + 1, :, t0:t0 + 512], 64, 128, "NQ")])
            else:
                kv = m - 16
                store_stage(lambda st, pm=pm: kb.op(V, lambda e: e.tensor_tensor(out=st.t[:], in0=pm.t[:], in1=rbc.t[:], op=ALU.mult), R=[pm, rbc], W=[st]),
                            [(KVT[kv, :, t0:t0 + 512], 0, 128, "KVT")])
        while pending:
            pending.pop(0)()
        for sub in range(4):
            tt = tb * 4 + sub
            tm_ = tmb[tt % 2]
            rc = rcol[tt % 2]
            pc = psum[6]
            kb.op(PE, lambda e, sub=sub: e.matmul(pc.t[:, 0:1], lhsT=rbc.t[0:1, sub * 128:(sub + 1) * 128], rhs=ones_f[0:1, 0:1], start=True, stop=True), R=[rbc, cst], W=[pc])
            kb.op(A, lambda e, rc=rc: e.copy(out=rc.t[:, 1:2], in_=pc.t[:, 0:1]), R=[pc], W=[rc])
            for part, (c0, c1) in enumerate(((0, 512), (512, 1024), (1024, 1312))):
                pm = psum[1 + part]
                for kc in range(8):
                    kb.op(PE, lambda e, kc=kc, sub=sub, pm=pm, c0=c0, c1=c1: e.matmul(pm.t[:, 0:c1 - c0], lhsT=xb.t[:, kc, sub * 128:(sub + 1) * 128], rhs=Wb.t[:, kc, 2560 + c0:2560 + c1], start=(kc == 0), stop=(kc == 7)), R=[Wb, xb], W=[pm])
                if part < 2:
                    kb.op(A, lambda e, pm=pm, c0=c0, c1=c1, tm_=tm_, rc=rc: e.activation(out=tm_.t[:, c0:c1], in_=pm.t[:, 0:c1 - c0], func=AF.Copy, scale=rc.t[:, 1:2]), R=[pm, rc], W=[tm_])
                else:
                    kb.op(V, lambda e, pm=pm, c0=c0, c1=c1, tm_=tm_, rc=rc: e.tensor_scalar(out=tm_.t[:, c0:c1], in0=pm.t[:, 0:c1 - c0], scalar1=rc.t[:, 1:2], scalar2=None, op0=ALU.mult), R=[pm, rc], W=[tm_])
            kb.op(P, lambda e, tm_=tm_, tt=tt: e.tensor_copy(out=gba.t[:, tt, :], in_=tm_.t[:, 1304:1312]), R=[tm_], W=[gba])
            kb.dma("pool", TM[tt * 128:(tt + 1) * 128, :], tm_.t[:], R=[tm_], sem=tmsem[tt % 2])

    if "gba" in dbg_out:
        dbg_dump("gba", gba.t[:].rearrange("p a b -> p (a b)"), gba)
    kb.barrier()
    stk[0].close()
    stk[0] = None
    for name in ("GQKV", "NQ", "KVT", "TM"):
        if name in dbg_out:
            src = {"GQKV": GQKV, "NQ": NQ, "KVT": KVT, "TM": TM}[name]
            d = kb.dsem(); out_sems.append(d)
            if name == "TM":
                for c in range(64):
                    kb.dma("sp", dbg_out[name][c * 128:(c + 1) * 128, :], src[c * 128:(c + 1) * 128, :], sem=d)
            else:
                for c in range(src.shape[0]):
                    kb.dma("sp", dbg_out[name][c], src[c], sem=d)


    def pq(bank, q0, q1):
        return psum[bank].t[:, q0 * 128:q1 * 128]

    pqb = [[Buf(bank=psum[i].b.bank) for _ in range(4)] for i in range(7)]
    if PH >= 2:
        stk[0] = ExitStack()
        sc = {}
        for nm in ("e1", "d1", "beta", "lnb", "z", "sp", "Aex", "g", "gc", "gl", "yy", "egc", "bg", "kd", "EGL0", "EGL1", "dtb", "alg"):
            sc[nm] = sb("sc_" + nm, [128, 256])
        gbv = gba.t[:, :, 0:4]
        gav = gba.t[:, :, 4:8]

        def v3(tl):
            return tl.t[:].rearrange("p (a b) -> p a b", b=4)

        ld2 = kb.dsem("ld2")
        kb.dma("sp", sc["dtb"].t[:], dtb_b, W=[sc["dtb"]], sem=ld2)
        kb.dma("sp", sc["alg"].t[:], alog_b, W=[sc["alg"]], sem=ld2)
        kb.seal(ld2, [sc["dtb"], sc["alg"]])
        kb.op(A, lambda e: e.activation(out=v3(sc["e1"]), in_=gbv, func=AF.Exp, scale=-1.0), R=[gba], W=[sc["e1"]])
        kb.op(V, lambda e: e.tensor_scalar(out=sc["d1"].t[:], in0=sc["e1"].t[:], scalar1=1.0, scalar2=None, op0=ALU.add), R=[sc["e1"]], W=[sc["d1"]])
        kb.op(V, lambda e: e.reciprocal(out=sc["beta"].t[:], in_=sc["d1"].t[:]), R=[sc["d1"]], W=[sc["beta"]])
        kb.op(A, lambda e: e.activation(out=sc["lnb"].t[:], in_=sc["d1"].t[:], func=AF.Ln), R=[sc["d1"]], W=[sc["lnb"]])
        kb.op(V, lambda e: e.tensor_tensor(out=v3(sc["z"]), in0=gav, in1=v3(sc["dtb"]), op=ALU.add), R=[gba, sc["dtb"]], W=[sc["z"]])
        kb.op(A, lambda e: e.activation(out=sc["z"].t[:], in_=sc["z"].t[:], func=AF.Exp), R=[sc["z"]], W=[sc["z"]])
        kb.op(V, lambda e: e.tensor_scalar(out=sc["z"].t[:], in0=sc["z"].t[:], scalar1=1.0, scalar2=None, op0=ALU.add), R=[sc["z"]], W=[sc["z"]])
        kb.op(A, lambda e: e.activation(out=sc["sp"].t[:], in_=sc["z"].t[:], func=AF.Ln), R=[sc["z"]], W=[sc["sp"]])
        kb.op(A, lambda e: e.activation(out=sc["Aex"].t[:], in_=sc["alg"].t[:], func=AF.Exp), R=[sc["alg"]], W=[sc["Aex"]])
        kb.op(V, lambda e: e.scalar_tensor_tensor(out=sc["g"].t[:], in0=sc["sp"].t[:], scalar=-1.0, in1=sc["Aex"].t[:], op0=ALU.mult, op1=ALU.mult), R=[sc["sp"], sc["Aex"]], W=[sc["g"]])
        p0 = psum[0]
        kb.op(PE, lambda e: e.matmul(p0.t[:, 0:256], lhsT=tri2, rhs=sc["g"].t[:], start=True, stop=True), R=[cst, sc["g"]], W=[p0])
        kb.op(PE, lambda e: e.matmul(p0.t[:, 256:512], lhsT=blk2, rhs=sc["g"].t[:], start=True, stop=True), R=[cst, sc["g"]], W=[p0])
        kb.op(V, lambda e: e.tensor_copy(out=sc["gc"].t[:], in_=p0.t[:, 0:256]), R=[p0], W=[sc["gc"]])
        kb.op(V, lambda e: e.tensor_copy(out=sc["gl"].t[:], in_=p0.t[:, 256:512]), R=[p0], W=[sc["gl"]])
        kb.op(V, lambda e: e.tensor_tensor(out=sc["yy"].t[:], in0=sc["gc"].t[:], in1=sc["lnb"].t[:], op=ALU.subtract), R=[sc["gc"], sc["lnb"]], W=[sc["yy"]])
        kb.op(A, lambda e: e.activation(out=sc["egc"].t[:], in_=sc["gc"].t[:], func=AF.Exp), R=[sc["gc"]], W=[sc["egc"]])
        kb.op(V, lambda e: e.tensor_tensor(out=sc["bg"].t[:], in0=sc["beta"].t[:], in1=sc["egc"].t[:], op=ALU.mult), R=[sc["beta"], sc["egc"]], W=[sc["bg"]])
        kb.op(V, lambda e: e.tensor_tensor(out=sc["kd"].t[:], in0=sc["gl"].t[:], in1=sc["gc"].t[:], op=ALU.subtract), R=[sc["gl"], sc["gc"]], W=[sc["kd"]])
        kb.op(A, lambda e: e.activation(out=sc["kd"].t[:], in_=sc["kd"].t[:], func=AF.Exp), R=[sc["kd"]], W=[sc["kd"]])
        p1 = psum[1]
        kb.op(PE, lambda e: e.matmul(p1.t[:, 0:256], lhsT=sel0, rhs=sc["gl"].t[:], start=True, stop=True), R=[cst, sc["gl"]], W=[p1])
        kb.op(PE, lambda e: e.matmul(p1.t[:, 256:512], lhsT=sel1, rhs=sc["gl"].t[:], start=True, stop=True), R=[cst, sc["gl"]], W=[p1])
        kb.op(A, lambda e: e.activation(out=sc["EGL0"].t[:], in_=p1.t[:, 0:256], func=AF.Exp), R=[p1], W=[sc["EGL0"]])
        kb.op(A, lambda e: e.activation(out=sc["EGL1"].t[:], in_=p1.t[:, 256:512], func=AF.Exp), R=[p1], W=[sc["EGL1"]])
        for nm in ("g", "beta", "gc"):
            dbg_dump("sc_" + nm, sc[nm].t[:], sc[nm])
        kb.barrier()

        kqv = [sb(f"kqv{i}", [128, 12, 128], BF16) for i in range(2)]
        kqsem = [kb.dsem(f"kq{i}") for i in range(2)]
        gzt = [sb(f"gzt{i}", [128, 512]) for i in range(2)]
        gzsem = [kb.dsem(f"gz{i}") for i in range(2)]
        gnw = sb("gnw", [128, 512])
        ld3 = kb.dsem("ld3")
        kb.dma("sp", gnw.t[:], gnw_b, W=[gnw], sem=ld3)
        Sst = [[sb(f"S{h}_{i}", [128, 128]) for i in range(2)] for h in range(4)]
        Sbf = [[sb(f"Sb{h}_{i}", [128, 128], BF16) for i in range(2)] for h in range(4)]
        for h in range(4):
            kb.op(P, lambda e, h=h: e.memset(Sst[h][0].t[:], 0.0), W=[Sst[h][0]])
            kb.op(P, lambda e, h=h: e.memset(Sbf[h][0].t[:], 0.0), W=[Sbf[h][0]])
        H4 = range(4)
        dg = [sb(f"dg{h}", [128, 256]) for h in H4]
        t1 = [sb(f"t1{h}", [128, 256]) for h in H4]
        Dd = [sb(f"Dd{h}", [128, 256]) for h in H4]
        MA = [sb(f"MA{h}", [128, 128]) for h in H4]
        ATb = [sb(f"ATb{h}", [128, 128], BF16) for h in H4]
        TTb = [sb(f"TTb{h}", [128, 128], BF16) for h in H4]
        XYb = [[sb(f"XYb{h}_{i}", [128, 256], BF16) for i in range(2)] for h in H4]
        Zb = [sb(f"Zb{h}", [128, 128], BF16) for h in H4]
        Zt = [sb(f"Z{h}", [128, 128]) for h in H4]
        egb = [sb(f"egb{h}", [128, 128]) for h in H4]
        rk = [sb(f"rk{h}", [128, 256], BF16) for h in H4]
        kdec = [sb(f"kdec{h}", [128, 128], BF16) for h in H4]
        WU = [sb(f"WU{h}", [128, 256], BF16) for h in H4]
        qd = [sb(f"qd{h}", [128, 128]) for h in H4]
        QtT = [sb(f"QtT{h}", [128, 128], BF16) for h in H4]
        NPh = [[sb(f"NPh{h}_{c}", [128, 128], BF16) for c in range(2)] for h in H4]
        ot = [sb(f"ot{i}", [128, 512]) for i in range(2)]
        osq = sb("osq", [128, 512])
        oss = sb("oss", [128, 8])
        gsg = sb("gsg", [128, 512])
        mixst = [sb(f"mixst{i}", [128, 512], BF16) for i in range(2)]
        mxsem = [kb.dsem(f"mx{i}") for i in range(2)]
        GQ_v = GQKV.rearrange("c p t -> p c t")
        NTG = NT if GT is None else GT
        cslot = [0]

        dslot = [0]

        def cps():
            q = cslot[0] % 4
            cslot[0] += 1
            return psum[4].t[:, q * 128:(q + 1) * 128], pqb[4][q]

        def cpd():
            i = dslot[0] % 8
            dslot[0] += 1
            bank = 5 + i // 4
            q = i % 4
            return psum[bank].t[:, q * 128:(q + 1) * 128], pqb[bank][q]

        import os as _os
        GST = int(_os.environ.get('GSTAGE', '99'))
        out_defer = []
        for t in range(NTG):
            kq = kqv[t % 2]
            kb.dma("sp", kq.t[:], GQ_v[:, :, t * 128:(t + 1) * 128], R=[scr_b["GQKV"]], W=[kq], sem=kqsem[t % 2])
            gz_ = gzt[t % 2]
            kb.dma("act", gz_.t[:], TM[t * 128:(t + 1) * 128, 0:512], R=[scr_b["TM"]], W=[gz_], sem=gzsem[t % 2])
            o_ = ot[t % 2]
            col = lambda nm, h: sc[nm].t[:, t * 4 + h:t * 4 + h + 1]
            B = lambda h, q: pqb[h][q]
            for h in H4:
                qT = kq.t[:, h, :]; kT = kq.t[:, 4 + h, :]
                kb.op(PE, lambda e, h=h, kT=kT: e.matmul(pq(h, 0, 1), lhsT=kT, rhs=kT, start=True, stop=True), R=[kq], W=[B(h, 0)])
                kb.op(PE, lambda e, h=h, kT=kT, qT=qT: e.matmul(pq(h, 1, 2), lhsT=kT, rhs=qT, start=True, stop=True), R=[kq], W=[B(h, 1)])
                kb.op(PE, lambda e, h=h: e.matmul(pq(h, 2, 3), lhsT=col("yy", h).to_broadcast([128, 128]), rhs=ident, start=True, stop=True), R=[cst, sc["yy"]], W=[B(h, 2)])
                kb.op(PE, lambda e, h=h: e.matmul(pq(h, 3, 4), lhsT=col("gc", h).to_broadcast([128, 128]), rhs=ident, start=True, stop=True), R=[cst, sc["gc"]], W=[B(h, 3)])
            if GST < 2:
                continue
            for h in H4:
                kb.op(V, lambda e, h=h: e.tensor_scalar(out=t1[h].t[:], in0=pq(h, 2, 4), scalar1=col("gc", h), scalar2=0.0, op0=ALU.subtract, op1=ALU.min), R=[B(h, 2), B(h, 3), sc["gc"]], W=[t1[h]])
            for h in H4:
                kb.op(A, lambda e, h=h: e.activation(out=egb[h].t[:], in_=pq(h, 3, 4), func=AF.Exp), R=[B(h, 3), t1[h]], W=[egb[h]])
                kb.op(A, lambda e, h=h: e.activation(out=t1[h].t[:], in_=t1[h].t[:], func=AF.Exp), R=[t1[h]], W=[t1[h]])
            for h in H4:
                kb.op(P, lambda e, h=h: e.tensor_tensor(out=Dd[h].t[:], in0=t1[h].t[:], in1=mask2, op=ALU.mult), R=[t1[h], cst], W=[Dd[h]])
            for h in H4:
                kb.op(V, lambda e, h=h: e.tensor_tensor(out=MA[h].t[:], in0=pq(h, 0, 1), in1=Dd[h].t[:, 0:128], op=ALU.mult), R=[B(h, 0), Dd[h]], W=[MA[h]])
                kb.op(V, lambda e, h=h: e.tensor_tensor(out=ATb[h].t[:], in0=pq(h, 1, 2), in1=Dd[h].t[:, 128:256], op=ALU.mult), R=[B(h, 1), Dd[h]], W=[ATb[h]])
            if GST < 3:
                continue
            for h in H4:
                kb.op(PE, lambda e, h=h: e.transpose(out=pq(h, 2, 3), in_=MA[h].t[:, 0:128], identity=ident), R=[MA[h], cst], W=[B(h, 2)])
                kb.op(A, lambda e, h=h: e.copy(out=XYb[h][0].t[:, 0:128], in_=pq(h, 2, 3)), R=[B(h, 2)], W=[XYb[h][0]])
                kb.op(P, lambda e, h=h: e.tensor_copy(out=XYb[h][0].t[:, 128:256], in_=MA[h].t[:, 0:128]), R=[MA[h]], W=[XYb[h][0]])
                kb.op(P, lambda e, h=h: e.tensor_tensor(out=Zt[h].t[:], in0=ident, in1=MA[h].t[:, 0:128], op=ALU.subtract), R=[cst, MA[h]], W=[Zt[h]])
            if GST < 4:
                continue
            for lv in range(5):
                for h in H4:
                    cur = XYb[h][lv % 2]
                    nxtb = XYb[h][(lv + 1) % 2]
                    kb.op(PE, lambda e, h=h, cur=cur: e.matmul(pq(h, 0, 1), lhsT=cur.t[:, 128:256], rhs=cur.t[:, 0:128], start=True, stop=True), R=[cur], W=[B(h, 0)])
                    if lv < 4:
                        kb.op(PE, lambda e, h=h, cur=cur: e.matmul(pq(h, 1, 2), lhsT=cur.t[:, 0:128], rhs=cur.t[:, 128:256], start=True, stop=True), R=[cur], W=[B(h, 1)])
                        kb.op(A, lambda e, h=h, nxtb=nxtb: e.copy(out=nxtb.t[:], in_=pq(h, 0, 2)), R=[B(h, 0), B(h, 1)], W=[nxtb])
                    else:
                        kb.op(A, lambda e, h=h, nxtb=nxtb: e.copy(out=nxtb.t[:, 0:128], in_=pq(h, 0, 1)), R=[B(h, 0)], W=[nxtb])
                for h in H4:
                    kb.op(A, lambda e, h=h: e.copy(out=Zb[h].t[:], in_=Zt[h].t[:]), R=[Zt[h]], W=[Zb[h]])
                for h in H4:
                    zq = 2 + (lv % 2)
                    nxtb = XYb[h][(lv + 1) % 2]
                    kb.op(PE, lambda e, h=h, nxtb=nxtb, zq=zq: e.matmul(pq(h, zq, zq + 1), lhsT=nxtb.t[:, 0:128], rhs=Zb[h].t[:], start=True, stop=True), R=[nxtb, Zb[h]], W=[B(h, zq)])
                    kb.op(V, lambda e, h=h, zq=zq: e.tensor_tensor(out=Zt[h].t[:], in0=Zt[h].t[:], in1=pq(h, zq, zq + 1), op=ALU.add), R=[B(h, zq), Zt[h]], W=[Zt[h]])
            while out_defer:
                out_defer.pop(0)()
            for h in H4:
                kT = kq.t[:, 4 + h, :]; vT = kq.t[:, 8 + h, :]
                kb.op(PE, lambda e, h=h, kT=kT: e.transpose(out=psb.t[:, h * 256:h * 256 + 128], in_=kT, identity=identb.t[:]), R=[kq, identb], W=[psb])
                kb.op(PE, lambda e, h=h, vT=vT: e.transpose(out=psb.t[:, h * 256 + 128:h * 256 + 256], in_=vT, identity=identb.t[:]), R=[kq, identb], W=[psb])
            for h in H4:
                kb.op(A, lambda e, h=h: e.activation(out=rk[h].t[:, 0:128], in_=psb.t[:, h * 256:h * 256 + 128], func=AF.Copy, scale=col("bg", h)), R=[psb, sc["bg"]], W=[rk[h]])
                kb.op(A, lambda e, h=h: e.activation(out=rk[h].t[:, 128:256], in_=psb.t[:, h * 256 + 128:h * 256 + 256], func=AF.Copy, scale=col("beta", h)), R=[psb, sc["beta"]], W=[rk[h]])
                kb.op(A, lambda e, h=h: e.activation(out=kdec[h].t[:], in_=psb.t[:, h * 256:h * 256 + 128], func=AF.Copy, scale=col("kd", h)), R=[psb, sc["kd"]], W=[kdec[h]])
                kb.op(V, lambda e, h=h: e.tensor_tensor(out=qd[h].t[:], in0=kq.t[:, h, :], in1=egb[h].t[:], op=ALU.mult), R=[kq, egb[h]], W=[qd[h]])
            if GST < 6:
                continue
            for h in H4:
                kb.op(A, lambda e, h=h: e.copy(out=TTb[h].t[:], in_=Zt[h].t[:]), R=[Zt[h]], W=[TTb[h]])
                kb.op(PE, lambda e, h=h: e.matmul(pq(h, 0, 2), lhsT=TTb[h].t[:], rhs=rk[h].t[:], start=True, stop=True), R=[TTb[h], rk[h]], W=[B(h, 0), B(h, 1)])
                kb.op(A, lambda e, h=h: e.copy(out=WU[h].t[:], in_=pq(h, 0, 2)), R=[B(h, 0), B(h, 1)], W=[WU[h]])
            if GST < 7:
                continue
            for h in H4:
                aap, abf = cpd()
                kb.op(PE, lambda e, h=h, aap=aap: e.matmul(aap, lhsT=WU[h].t[:, 0:128], rhs=ATb[h].t[:], start=True, stop=True), R=[WU[h], ATb[h]], W=[abf])
                kb.op(V, lambda e, h=h, aap=aap: e.tensor_tensor(out=QtT[h].t[:], in0=qd[h].t[:], in1=aap, op=ALU.subtract), R=[qd[h], abf], W=[QtT[h]])
                for c in range(int(_os.environ.get('GS7', '2'))):
                    r = slice(64 * c, 64 * c + 64)
                    kb.op(PE, lambda e, h=h, c=c, r=r: e.matmul(pq(h, c, c + 1), lhsT=WU[h].t[r, 0:128], rhs=kdec[h].t[r, :], start=True, stop=True), R=[WU[h], kdec[h]], W=[B(h, c)])
                    kb.op(A, lambda e, h=h, c=c: e.activation(out=NPh[h][c].t[:], in_=pq(h, c, c + 1), func=AF.Copy, scale=-1.0), R=[B(h, c)], W=[NPh[h][c]])
            if GST < 8:
                continue
            for c in range(2):
                r = slice(64 * c, 64 * c + 64)
                for h in H4:
                    Sc = Sst[h][c]; Sn = Sst[h][1 - c]; Scb = Sbf[h][c]; Snb = Sbf[h][1 - c]
                    oap, ob = cps()
                    kb.op(PE, lambda e, h=h, oap=oap, Scb=Scb: e.matmul(oap, lhsT=QtT[h].t[:], rhs=Scb.t[:], start=True, stop=False), R=[QtT[h], Scb], W=[ob])
                    kb.op(PE, lambda e, h=h, oap=oap, r=r: e.matmul(oap, lhsT=ATb[h].t[r, :], rhs=WU[h].t[r, 128:256], start=False, stop=True), R=[ATb[h], WU[h]], W=[ob])
                    kb.op(A, lambda e, h=h, oap=oap, r=r: e.copy(out=o_.t[r, h * 128:(h + 1) * 128], in_=oap[r, :]), R=[ob], W=[o_])
                    sap, sbf = cpd()
                    kb.op(PE, lambda e, h=h, sap=sap, r=r: e.matmul(sap, lhsT=kdec[h].t[r, :], rhs=WU[h].t[r, 128:256], start=True, stop=False), R=[kdec[h], WU[h]], W=[sbf])
                    kb.op(PE, lambda e, h=h, c=c, sap=sap, Scb=Scb: e.matmul(sap, lhsT=NPh[h][c].t[:], rhs=Scb.t[:], start=False, stop=True), R=[NPh[h][c], Scb], W=[sbf])
                    egl = sc["EGL%d" % c].t[:, t * 4 + h:t * 4 + h + 1]
                    kb.op(V, lambda e, sap=sap, Sc=Sc, Snb=Snb, egl=egl: e.scalar_tensor_tensor(out=Snb.t[:], in0=Sc.t[:], scalar=egl, in1=sap, op0=ALU.mult, op1=ALU.add), R=[Sc, sbf, sc["EGL%d" % c]], W=[Snb])
                    kb.op(V, lambda e, sap=sap, Sc=Sc, Sn=Sn, egl=egl: e.scalar_tensor_tensor(out=Sn.t[:], in0=Sc.t[:], scalar=egl, in1=sap, op0=ALU.mult, op1=ALU.add), R=[Sc, sbf, sc["EGL%d" % c]], W=[Sn])
            def out_stage(t=t, o_=o_, gz_=gz_):
                ms = mixst[t % 2]
                kb.op(A, lambda e: e.activation(out=osq.t[:], in_=o_.t[:], func=AF.Square), R=[o_], W=[osq])
                kb.op(V, lambda e: e.tensor_reduce(out=oss.t[:, 0:4], in_=osq.t[:].rearrange("p (h d) -> p h d", d=128), axis=mybir.AxisListType.X, op=ALU.add), R=[osq], W=[oss])
                kb.op(A, lambda e: e.activation(out=oss.t[:, 0:4], in_=oss.t[:, 0:4], func=AF.Ln, bias=epsc, scale=1.0 / 128), R=[oss, cst], W=[oss])
                kb.op(A, lambda e: e.activation(out=oss.t[:, 4:8], in_=oss.t[:, 0:4], func=AF.Exp, scale=-0.5), R=[oss], W=[oss])
                kb.op(A, lambda e: e.activation(out=gsg.t[:], in_=gz_.t[:], func=AF.Exp, scale=-1.0), R=[gz_], W=[gsg])
                kb.op(A, lambda e: e.activation(out=gsg.t[:], in_=gsg.t[:], func=AF.Ln, bias=ones_f[:, 0:1], scale=1.0), R=[gsg, cst], W=[gsg])
                kb.op(A, lambda e: e.activation(out=gsg.t[:], in_=gsg.t[:], func=AF.Exp, scale=-1.0), R=[gsg], W=[gsg])
                kb.op(P, lambda e: e.tensor_tensor(out=gsg.t[:], in0=gsg.t[:], in1=gz_.t[:], op=ALU.mult), R=[gsg, gz_], W=[gsg])
                kb.op(P, lambda e: e.tensor_tensor(out=gsg.t[:], in0=gsg.t[:], in1=gnw.t[:], op=ALU.mult), R=[gsg, gnw], W=[gsg])
                for h in H4:
                    kb.op(V, lambda e, h=h: e.scalar_tensor_tensor(out=ms.t[:, h * 128:(h + 1) * 128], in0=o_.t[:, h * 128:(h + 1) * 128], scalar=oss.t[:, 4 + h:5 + h], in1=gsg.t[:, h * 128:(h + 1) * 128], op0=ALU.mult, op1=ALU.mult), R=[o_, oss, gsg], W=[ms])
                kb.dma("pool", MIXG[t * 128:(t + 1) * 128, :], ms.t[:], R=[ms], sem=mxsem[t % 2])

            out_defer.append(out_stage)
        while out_defer:
            out_defer.pop(0)()
        kb.barrier()
        stk[0].close()
        stk[0] = None
        if "MIXG" in dbg_out:
            d = kb.dsem(); out_sems.append(d)
            for c in range(16):
                kb.dma("sp", dbg_out["MIXG"][c * 512:(c + 1) * 512, :], MIXG[c * 512:(c + 1) * 512, :], sem=d)


    if PH >= 3:
        stk[0] = ExitStack()
        EXP = AF.Exp
        ksT = [sb(f"ksT{g}", [128, S], BF16) for g in range(2)]
        Vs = [sb(f"Vs{g}", [128, 64, 65], BF16) for g in range(2)]
        tS = sb("tS", [128, 2, 9, 512])
        tWx = sb("tWx", [128, 2, 2, 512])
        tCf = sb("tCf", [128, 2, 512])
        Wo = sb("Wo", [128, 8, D], BF16)
        ovb = sb("ovb", [128, 4, 128], BF16)
        adjc = sb("adjc", [128, 256])
        j0c = sb("j0c", [128, 32])
        pwc = sb("pwc", [128, 2])
        ecx = sb("ecx", [128, 8])
        fnw = sb("fnw", [128, D])
        kcT = [sb(f"kcT{g}", [64, 512], BF16) for g in range(2)]
        vca = [sb(f"vca{g}", [128, 4, 65], BF16) for g in range(2)]
        l3 = kb.dsem("l3")
        mainstk = stk[0]
        stk[0] = ExitStack()
        stgA = sb("stgA", [128, 2048])
        for g in range(2):
            kb.dma("sp", ksT[g].t[0:64, :], KVT[2, 64 * g:64 * g + 64, :], R=[scr_b["KVT"]], W=[ksT[g]], sem=l3)
            for m in range(9):
                kb.dma("act", tS.t[:, g, m, :], tabS[g, m], W=[tS], sem=l3)
            for m in range(2):
                kb.dma("act", tWx.t[:, m, g, :], tabWx[m, g], W=[tWx], sem=l3)
            kb.dma("act", tCf.t[:, g, :], tabCfar[g], W=[tCf], sem=l3)
        for nm_, dst, src in (("adj", adjc, adjT), ("j0", j0c, j0b), ("pw", pwc, pw), ("ec", ecx, cexp), ("fnw", fnw, fnw_b)):
            kb.dma("sp", dst.t[:], src, W=[dst], sem=l3)
        kb.seal(l3, ksT + [tS, tWx, tCf, adjc, j0c, pwc, ecx, fnw])
        kb.op(A, lambda e: e.activation(out=ecx.t[:], in_=ecx.t[:], func=EXP), R=[ecx], W=[ecx])
        sgs = kb.dsem("sgs")
        for c in range(4):
            kb.dma("sp", stgA.t[64:128, :], estk[:, c * 2048:(c + 1) * 2048], W=[stgA], sem=sgs)
            for g in range(2):
                kb.op(V, lambda e, c=c, g=g: e.tensor_copy(out=ksT[g].t[64:128, c * 2048:(c + 1) * 2048], in_=stgA.t[64:128, :]), R=[stgA], W=[ksT[g]])
        stgB = sb("stgB", [128, 2048])
        sgs2 = kb.dsem("sgs2")
        TMv = TM[:, 1024:1152].rearrange("(k p) c -> p k c", p=128)
        for c in range(4):
            stg_, sem_, q_ = (stgA, sgs, "sp") if c % 2 == 0 else (stgB, sgs2, "act")
            kb.dma(q_, stg_.t[:], TMv[:, c * 16:(c + 1) * 16, :], R=[scr_b["TM"]], W=[stg_], sem=sem_)
            sv = stg_.t[:].rearrange("p (k c) -> p k c", c=128)
            for g in range(2):
                kb.op(V if g == 0 else A, lambda e, c=c, g=g, sv=sv: (e.tensor_copy(out=Vs[g].t[:, c * 16:(c + 1) * 16, 0:64], in_=sv[:, :, 64 * g:64 * g + 64]) if g == 0 else e.copy(out=Vs[g].t[:, c * 16:(c + 1) * 16, 0:64], in_=sv[:, :, 64 * g:64 * g + 64])), R=[stg_], W=[Vs[g]])
        for g in range(2):
            kb.op(P, lambda e, g=g: e.memset(Vs[g].t[:, :, 64:65], 1.0), W=[Vs[g]])
        wo_v = w_out.rearrange("(c p) n -> p c n", p=128)
        for c in range(8):
            kb.dma("sp", stgA.t[:, 0:1024], wo_v[:, c, :], W=[stgA], sem=sgs)
            kb.op(V, lambda e, c=c: e.tensor_copy(out=Wo.t[:, c, :], in_=stgA.t[:, 0:1024]), R=[stgA], W=[Wo])
        kb.dma("sp", stgA.t[:, 0:512], ovl.rearrange("p a b -> p (a b)"), W=[stgA], sem=sgs)
        kb.op(V, lambda e: e.tensor_copy(out=ovb.t[:].rearrange("p a b -> p (a b)"), in_=stgA.t[:, 0:512]), R=[stgA], W=[ovb])
        rawT = sb("rawT", [64, S], BF16)
        w1s = sb("w1s", [64, 4096])
        w1b = sb("w1b", [64, 32, 128], BF16)
        peb_ = sb("peb_", [64, 32])
        pebb = sb("pebb", [64, 32], BF16)
        w2s = sb("w2s", [128, 64])
        w2b = sb("w2b", [128, 64], BF16)
        pcol = sb("pcol", [128, 1])
        hb = sb("hb", [128, 512], BF16)
        kb.op(P, lambda e: e.memset(hb.t[:], 0.0), W=[hb])
        for kind, (w1d, ped, w2d) in enumerate(((w1k, pek, w2k), (w1v, pev, w2v))):
            kb.dma("sp", w1s.t[:], w1d.rearrange("p a b -> p (a b)"), W=[w1s], sem=sgs)
            kb.dma("sp", peb_.t[:], ped, W=[peb_], sem=sgs)
            kb.dma("sp", w2s.t[:], w2d, W=[w2s], sem=sgs)
            kb.seal(sgs, [w1s, peb_, w2s])
            kb.op(V, lambda e: e.tensor_copy(out=w1b.t[:].rearrange("p a b -> p (a b)"), in_=w1s.t[:]), R=[w1s], W=[w1b])
            kb.op(V, lambda e: e.tensor_copy(out=pebb.t[:], in_=peb_.t[:]), R=[peb_], W=[pebb])
            kb.op(V, lambda e: e.tensor_copy(out=w2b.t[:], in_=w2s.t[:]), R=[w2s], W=[w2b])
            for g in range(2):
                kb.dma("sp", rawT.t[:], KVT[kind, 64 * g:64 * g + 64, :], R=[scr_b["KVT"]], W=[rawT], sem=sgs)
                rv = rawT.t[:].rearrange("p (n r) -> p n r", r=16)
                ph_ = psum[0]; pp_ = psum[1]
                for l in range(32):
                    a_, r_ = l // 16, l % 16
                    kb.op(PE, lambda e, l=l, a_=a_, r_=r_: e.matmul(ph_.t[:, 0:511], lhsT=w1b.t[:, l, :], rhs=rv[:, a_:a_ + 511, r_], start=(l == 0), stop=(l == 31)), R=[w1b, rawT], W=[ph_])
                for l in range(32):
                    kb.op(PE, lambda e, l=l: e.matmul(pp_.t[:, 0:1], lhsT=w1b.t[:, l, :], rhs=pebb.t[:, l:l + 1], start=(l == 0), stop=(l == 31)), R=[w1b, pebb], W=[pp_])
                kb.op(V, lambda e: e.tensor_copy(out=pcol.t[:], in_=pp_.t[:, 0:1]), R=[pp_], W=[pcol])
                kb.op(A, lambda e: e.activation(out=hb.t[:, 0:511], in_=ph_.t[:, 0:511], func=AF.Silu, bias=pcol.t[:, 0:1], scale=1.0), R=[ph_, pcol], W=[hb])
                if kind == 0:
                    pk_ = psum[2]
                    kb.op(PE, lambda e: e.matmul(pk_.t[0:64, :], lhsT=w2b.t[:], rhs=hb.t[:], start=True, stop=True), R=[w2b, hb], W=[pk_])
                    kb.op(V, lambda e, g=g: e.tensor_copy(out=kcT[g].t[:], in_=pk_.t[0:64, :]), R=[pk_], W=[kcT[g]])
                else:
                    pk_ = psum[2]
                    for ct in range(4):
                        kb.op(PE, lambda e, ct=ct: e.matmul(pk_.t[:, ct * 64:(ct + 1) * 64], lhsT=hb.t[:, ct * 128:(ct + 1) * 128], rhs=w2b.t[:], start=True, stop=True), R=[w2b, hb], W=[pk_])
                    kb.op(V, lambda e, g=g: e.tensor_copy(out=vca[g].t[:, :, 0:64], in_=pk_.t[:, 0:256].rearrange("p (a b) -> p a b", b=64)), R=[pk_], W=[vca[g]])
                    kb.op(V, lambda e, g=g: e.tensor_copy(out=vca[g].t[:, :, 64:65], in_=nvalid.rearrange("p (a b) -> p a b", b=1)), R=[cst], W=[vca[g]])
        if "kcT" in dbg_out:
            for g in range(2):
                dbg_dump("kcT", None, None)
        kb.barrier()
        stk[0].close()
        stk[0] = mainstk
        NB2 = 1
        qp = [sb(f"qp{i}", [64, 8, 256], BF16) for i in range(NB2)]
        qT_ = [sb(f"qT{i}", [64, 8, 128], BF16) for i in range(NB2)]
        qtmp = sb("qtmp", [64, 8, 128], BF16)
        tm2 = [sb(f"tm2{i}", [128, 2, 536]) for i in range(NB2)]
        tmq = sb("tmq", [128, 536])
        mg2 = [sb(f"mg2{i}", [128, 2, 512], BF16) for i in range(NB2)]
        xrt = [sb(f"xrt{i}", [128, D]) for i in range(NB2)]
        kwt = [sb(f"kwt{i}", [64, 2, 768], BF16) for i in range(NB2)]
        vwst = [sb(f"vwst{i}", [128, 6, 128]) for i in range(NB2)]
        Vw = sb("Vw", [128, 6, 2, 65], BF16)
        kb.op(P, lambda e: e.memset(Vw.t[:, :, :, 64:65], 1.0), W=[Vw])
        slsem = [[kb.dsem(f"sl{k}_{i}") for i in range(NB2)] for k in range(6)]
        tcb = [sb(f"tcb{i}", [128, 512]) for i in range(2)]
        tcsem = [kb.dsem(f"tc{i}") for i in range(2)]
        sbt = [sb(f"sbt{i}", [128, 512]) for i in range(3)]
        Pt = [sb(f"Pt{i}", [128, 512], BF16) for i in range(5)]
        QS = [[sb(f"QS{g}_{i}", [128, 512], BF16) for i in range(2)] for g in range(2)]
        selm2 = [sb(f"selm2{g}", [128, 256]) for g in range(2)]
        gate = sb("gate", [128, 24])
        ocs = [sb(f"ocs{g}", [128, 260]) for g in range(2)]; cmb = sb("cmb", [128, 260]); ows = [sb(f"ows{g}", [128, 260]) for g in range(2)]
        den3 = [sb(f"den3{g}", [128, 12]) for g in range(2)]; cf3 = [sb(f"cf3{g}", [128, 12]) for g in range(2)]
        score = [sb(f"score{g}", [128, 128]) for g in range(2)]; work = sb("work", [128, 128])
        m8a = sb("m8a", [128, 8]); m8b = sb("m8b", [128, 8])
        o_n = sb("o_n", [128, 512]); sgn = sb("sgn", [128, 512])
        mix = sb("mix", [128, D], BF16); mixT = sb("mixT", [128, 8, 128], BF16)
        hbuf = sb("hbuf", [128, D]); hss = sb("hss", [128, 2])
        obuf = [sb("obuf0", [128, D])] * 2
        osem = [kb.dsem(f"os{i}") for i in range(2)]
        out_sems.extend(osem)
        NQv = NQ.rearrange("h d t -> d h t")
        print("sbuf remaining", nc.sbuf_bytes_remaining)
        nS = [0]; nP = [0]; nC = [0]; nTC = [0]
        SB_ = [psum[0], psum[1], psum[2], psum[6]]
        ACCA, ACCB, ACCC, MISC = psum[3], psum[4], psum[5], psum[6]
        SL = range(32) if NSL is None else NSL
        for i in SL:
            bi = i % NB2
            t0 = 2 * i * 128
            kb.dma("sp", qp[bi].t[:], NQv[:, :, t0:t0 + 256], R=[scr_b["NQ"]], W=[qp[bi]], sem=slsem[0][bi])
            kb.dma("sp", tm2[bi].t[:, :, 0:512], TM[t0:t0 + 256, 512:1024].rearrange("(a p) c -> p a c", p=128), R=[scr_b["TM"]], W=[tm2[bi]], sem=slsem[1][bi])
            kb.dma("sp", tm2[bi].t[:, :, 512:536], TM[t0:t0 + 256, 1280:1304].rearrange("(a p) c -> p a c", p=128), R=[scr_b["TM"]], W=[tm2[bi]], sem=slsem[1][bi])
            kb.dma("act", mg2[bi].t[:], MIXG[t0:t0 + 256, :].rearrange("(a p) c -> p a c", p=128), R=[scr_b["MIXG"]], W=[mg2[bi]], sem=slsem[2][bi])
            kb.dma("act", xrt[bi].t[:], xr[i * 128:(i + 1) * 128, :], W=[xrt[bi]], sem=slsem[3][bi])
            kt_lo = max(0, 2 * i - 4); nkw = 2 * i + 2 - kt_lo
            for g in range(2):
                kb.dma("sp", kwt[bi].t[:, g, 0:nkw * 128], KVT[3, 64 * g:64 * g + 64, kt_lo * 128:(2 * i + 2) * 128], R=[scr_b["KVT"]], W=[kwt[bi]], sem=slsem[4][bi])
            kb.dma("act", vwst[bi].t[:, 0:nkw, :], TM[kt_lo * 128:(2 * i + 2) * 128, 1152:1280].rearrange("(k p) c -> p k c", p=128), R=[scr_b["TM"]], W=[vwst[bi]], sem=slsem[5][bi])
            w0 = pwc.t[:, 0:1]; w1_ = pwc.t[:, 1:2]
            kb.op(V, lambda e: e.tensor_scalar(out=qtmp.t[:], in0=qp[bi].t[:, :, 0:128], scalar1=w0[0:64, :], scalar2=None, op0=ALU.mult), R=[qp[bi], pwc], W=[qtmp])
            kb.op(V, lambda e: e.scalar_tensor_tensor(out=qT_[bi].t[:], in0=qp[bi].t[:, :, 128:256], scalar=w1_[0:64, :], in1=qtmp.t[:], op0=ALU.mult, op1=ALU.add), R=[qp[bi], pwc, qtmp], W=[qT_[bi]])
            kb.op(V, lambda e: e.tensor_scalar(out=tmq.t[:], in0=tm2[bi].t[:, 0, :], scalar1=w0, scalar2=None, op0=ALU.mult), R=[tm2[bi], pwc], W=[tmq])
            kb.op(V, lambda e: e.scalar_tensor_tensor(out=tmq.t[:], in0=tm2[bi].t[:, 1, :], scalar=w1_, in1=tmq.t[:], op0=ALU.mult, op1=ALU.add), R=[tm2[bi], pwc, tmq], W=[tmq])
            kb.op(V, lambda e: e.tensor_scalar(out=mix.t[:, 0:512], in0=mg2[bi].t[:, 0, :], scalar1=w0, scalar2=None, op0=ALU.mult), R=[mg2[bi], pwc], W=[mix])
            kb.op(V, lambda e: e.scalar_tensor_tensor(out=mix.t[:, 0:512], in0=mg2[bi].t[:, 1, :], scalar=w1_, in1=mix.t[:, 0:512], op0=ALU.mult, op1=ALU.add), R=[mg2[bi], pwc, mix], W=[mix])
            for g in range(2):
                kb.op(P, lambda e, g=g: e.tensor_copy(out=Vw.t[:, 0:nkw, g, 0:64], in_=vwst[bi].t[:, 0:nkw, 64 * g:64 * g + 64]), R=[vwst[bi]], W=[Vw])
            kb.op(A, lambda e: e.activation(out=gate.t[:], in_=tmq.t[:, 512:536], func=EXP, scale=-1.0), R=[tmq], W=[gate])
            kb.op(V, lambda e: e.tensor_scalar(out=gate.t[:], in0=gate.t[:], scalar1=1.0, scalar2=None, op0=ALU.add), R=[gate], W=[gate])
            kb.op(V, lambda e: e.reciprocal(out=gate.t[:], in_=gate.t[:]), R=[gate], W=[gate])
            qT = qT_[bi]

            def stageA(j, g):
                ps_ = SB_[nS[0] % 4]; nS[0] += 1
                if j.get("qs") is not None:
                    kb.op(PE, lambda e: e.matmul(ps_.t[:], lhsT=j["lhsT"], rhs=j["qs"].t[:], start=True, stop=True), R=[j["qs"]] + j["lt"], W=[ps_])
                else:
                    rhs = qT.t[:, 4 * g:4 * g + 4, :].rearrange("p a b -> p (a b)")
                    kb.op(PE, lambda e: e.matmul(ps_.t[:], lhsT=j["lhsT"], rhs=rhs, start=True, stop=True), R=[qT] + j["lt"], W=[ps_])
                pt = Pt[nP[0] % 5]; nP[0] += 1
                if j.get("tabdma") is not None:
                    tb_ = tcb[nTC[0] % 2]; ts_ = tcsem[nTC[0] % 2]; nTC[0] += 1
                    kb.dma("sp", tb_.t[:], j["tabdma"], W=[tb_], sem=ts_)
                    j["tab"], j["tt"] = tb_.t[:], [tb_]
                if j.get("tab") is not None:
                    st_ = sbt[nC[0] % 3]; nC[0] += 1
                    kb.op(V, lambda e: e.tensor_tensor(out=st_.t[:], in0=ps_.t[:], in1=j["tab"], op=ALU.add), R=[ps_] + j["tt"], W=[st_])
                    kb.op(A, lambda e: e.activation(out=pt.t[:], in_=st_.t[:], func=EXP), R=[st_], W=[pt])
                else:
                    kb.op(A, lambda e: e.activation(out=pt.t[:], in_=ps_.t[:], func=EXP), R=[ps_], W=[pt])
                j["pt"] = pt

            def stageB(j):
                pt = j["pt"]
                for (acc, v_ap, v_tiles, first, last, width, stride) in j["pv"]:
                    for h in range(4):
                        kb.op(PE, lambda e, h=h: e.matmul(acc.t[:, h * stride:h * stride + width], lhsT=pt.t[:, h * 128:(h + 1) * 128], rhs=v_ap, start=(first and h == 0), stop=(last and h == 3)), R=[pt] + v_tiles, W=[acc])

            def run_jobs(jobs, g, LA=3):
                for idx in range(len(jobs) + LA):
                    if idx < len(jobs):
                        stageA(jobs[idx], g)
                    if idx - LA >= 0:
                        stageB(jobs[idx - LA])

            def front(g):
                    nkt = 2 * i + 2
                    cts = [ct for ct in range(4) if 2 * i + 1 - 16 * ct >= 0]
                    jobs = []
                    for ci, ct in enumerate(cts):
                        j = dict(lhsT=kcT[g].t[:, ct * 128:(ct + 1) * 128], lt=[kcT[g]])
                        if 2 * i - 16 * ct >= 23:
                            j["tab"], j["tt"] = tCf.t[:, g, :], [tCf]
                        else:
                            j["tabdma"] = tabC[g, CIDX[(i, ct)]]
                        f_, l_ = ci == 0, ci == len(cts) - 1
                        j["pv"] = [(ACCA, vca[g].t[:, ct, :], [vca[g]], f_, l_, 65, 65), (ACCB, ovb.t[:, ct, :], [ovb], f_, l_, 128, 128)]
                        jobs.append(j)
                    run_jobs(jobs, g)
                    wk = [kt for kt in range(kt_lo, nkt)]
                    jobs = []
                    for ki, kt in enumerate(wk):
                        m = 2 * i + 1 - kt
                        sl_ = kt - kt_lo
                        tab = tS.t[:, g, m, :] if m <= 3 else tWx.t[:, m - 4, g, :]
                        jobs.append(dict(lhsT=kwt[bi].t[:, g, sl_ * 128:(sl_ + 1) * 128], lt=[kwt[bi]], tab=tab, tt=[tS, tWx],
                                         pv=[(ACCC, Vw.t[:, sl_, g, :], [Vw], ki == 0, ki == len(wk) - 1, 65, 65)]))
                    run_jobs(jobs, g)
                    kb.op(V, lambda e: e.tensor_copy(out=ows[g].t[:], in_=ACCC.t[:, 0:260]), R=[ACCC], W=[ows[g]])
                    kb.op(V, lambda e: e.tensor_copy(out=ocs[g].t[:], in_=ACCA.t[:, 0:260]), R=[ACCA], W=[ocs[g]])
                    o3 = ocs[g].t[:].rearrange("p (h c) -> p h c", c=65)
                    kb.op(V, lambda e: e.tensor_scalar(out=den3[g].t[:, 0:4], in0=o3[:, :, 64], scalar1=1e-30, scalar2=None, op0=ALU.max), R=[ocs[g]], W=[den3[g]])
                    kb.op(V, lambda e: e.reciprocal(out=cf3[g].t[:, 0:4], in_=den3[g].t[:, 0:4]), R=[den3[g]], W=[cf3[g]])
                    kb.op(V, lambda e: e.tensor_scalar(out=score[g].t[:], in0=ACCB.t[:, 0:128], scalar1=cf3[g].t[:, 0:1], scalar2=None, op0=ALU.mult), R=[ACCB, cf3[g]], W=[score[g]])
                    for h in range(1, 4):
                        kb.op(V, lambda e, h=h: e.scalar_tensor_tensor(out=score[g].t[:], in0=ACCB.t[:, h * 128:(h + 1) * 128], scalar=cf3[g].t[:, h:h + 1], in1=score[g].t[:], op0=ALU.mult, op1=ALU.add), R=[ACCB, cf3[g], score[g]], W=[score[g]])
                    a0 = 126 - 4 * i
                    kb.op(V, lambda e: e.tensor_tensor(out=score[g].t[:], in0=score[g].t[:], in1=adjc.t[:, a0:a0 + 128], op=ALU.add), R=[score[g], adjc], W=[score[g]])
                    kb.op(V, lambda e: e.tensor_tensor(out=score[g].t[:, 0:1], in0=score[g].t[:, 0:1], in1=j0c.t[:, i:i + 1], op=ALU.add), R=[score[g], j0c], W=[score[g]])
                    kb.op(V, lambda e: e.max(out=m8a.t[:], in_=score[g].t[:]), R=[score[g]], W=[m8a])
                    kb.op(V, lambda e: e.match_replace(out=work.t[:], in_to_replace=m8a.t[:], in_values=score[g].t[:], imm_value=-3.0e38), R=[score[g], m8a], W=[work])
                    kb.op(V, lambda e: e.max(out=m8b.t[:], in_=work.t[:]), R=[work], W=[m8b])
                    for hf in range(2):
                        kb.op(V, lambda e, hf=hf: e.tensor_scalar(out=selm2[g].t[:, hf * 128:(hf + 1) * 128], in0=score[g].t[:], scalar1=m8b.t[:, 7:8], scalar2=None, op0=ALU.is_ge), R=[score[g], m8b], W=[selm2[g]])
                    ngb = 2 if nkt > 32 else 1
                    kb.op(PE, lambda e: e.transpose(out=MISC.t[:, 0:128], in_=selm2[g].t[:, 64:192], identity=ident), R=[selm2[g], cst], W=[MISC])
                    if ngb == 2:
                        kb.op(PE, lambda e: e.transpose(out=MISC.t[:, 128:256], in_=selm2[g].t[:, 0:128], identity=ident), R=[selm2[g], cst], W=[MISC])
                    for gb in range(ngb):
                        kb.op(P, lambda e, gb=gb: e.tensor_copy(out=QS[g][gb].t[0:64, :], in_=qT.t[:, 4 * g:4 * g + 4, :].rearrange("p a b -> p (a b)")), R=[qT], W=[QS[g][gb]])
                        for h in range(4):
                            kb.op(V, lambda e, h=h, gb=gb: e.tensor_scalar(out=QS[g][gb].t[64:128, h * 128:(h + 1) * 128], in0=MISC.t[64:128, gb * 128:(gb + 1) * 128], scalar1=-1.0, scalar2=-MNEG, op0=ALU.add, op1=ALU.mult), R=[MISC], W=[QS[g][gb]])
                    if g == 0 and ("selm" in dbg_out) and i == DSLOT:
                        dbg_dump("selm", selm2[g].t[:, 0:128], selm2[g])
                        dbg_dump("score", score[g].t[:], score[g])
                        dbg_dump("ocs", ocs[g].t[:], ocs[g])
                        dbg_dump("cf3", cf3[g].t[:], cf3[g])

            def back(g):
                    nkt = 2 * i + 2
                    o3 = ocs[g].t[:].rearrange("p (h c) -> p h c", c=65)
                    far = [kt for kt in range(nkt) if 2 * i + 1 - kt >= 9]
                    near = [kt for kt in range(nkt) if 2 * i + 1 - kt < 9]
                    jobs = []
                    for ki, kt in enumerate(far):
                        jobs.append(dict(lhsT=ksT[g].t[:, kt * 128:(kt + 1) * 128], lt=[ksT[g]], qs=QS[g][kt // 32],
                                         pv=[(ACCB, Vs[g].t[:, kt, :], [Vs[g]], ki == 0, ki == len(far) - 1, 65, 65)]))
                    for ki, kt in enumerate(near):
                        m = 2 * i + 1 - kt
                        jobs.append(dict(lhsT=ksT[g].t[:, kt * 128:(kt + 1) * 128], lt=[ksT[g]], qs=QS[g][kt // 32], tab=tS.t[:, g, m, :], tt=[tS],
                                         pv=[(ACCA, Vs[g].t[:, kt, :], [Vs[g]], ki == 0, ki == len(near) - 1, 65, 65)]))
                    run_jobs(jobs, g)
                    kb.op(V, lambda e: e.tensor_copy(out=cmb.t[:], in_=ACCA.t[:, 0:260]), R=[ACCA], W=[cmb])
                    if far:
                        for h in range(4):
                            kb.op(V, lambda e, h=h, g=g: e.scalar_tensor_tensor(out=cmb.t[:, h * 65:(h + 1) * 65], in0=ACCB.t[:, h * 65:(h + 1) * 65], scalar=ecx.t[:, 4 * g + h:4 * g + h + 1], in1=cmb.t[:, h * 65:(h + 1) * 65], op0=ALU.mult, op1=ALU.add), R=[ACCB, ecx, cmb], W=[cmb])
                    c3 = cmb.t[:].rearrange("p (h c) -> p h c", c=65)
                    w3 = ows[g].t[:].rearrange("p (h c) -> p h c", c=65)
                    kb.op(V, lambda e: e.tensor_scalar(out=den3[g].t[:, 4:8], in0=c3[:, :, 64], scalar1=1e-30, scalar2=None, op0=ALU.max), R=[cmb], W=[den3[g]])
                    kb.op(V, lambda e: e.tensor_scalar(out=den3[g].t[:, 8:12], in0=w3[:, :, 64], scalar1=1e-30, scalar2=None, op0=ALU.max), R=[ows[g]], W=[den3[g]])
                    kb.op(V, lambda e: e.reciprocal(out=cf3[g].t[:], in_=den3[g].t[:]), R=[den3[g]], W=[cf3[g]])
                    gv = gate.t[:].rearrange("p (x h) -> p x h", h=8)[:, :, 4 * g:4 * g + 4]
                    kb.op(V, lambda e, gv=gv: e.tensor_tensor(out=cf3[g].t[:].rearrange("p (x h) -> p x h", h=4), in0=cf3[g].t[:].rearrange("p (x h) -> p x h", h=4), in1=gv, op=ALU.mult), R=[cf3[g], gate], W=[cf3[g]])
                    for h in range(4):
                        oc_ = o_n.t[:, (4 * g + h) * 64:(4 * g + h + 1) * 64]
                        kb.op(V, lambda e, h=h, oc_=oc_: e.tensor_scalar(out=oc_, in0=o3[:, h, 0:64], scalar1=cf3[g].t[:, h:h + 1], scalar2=None, op0=ALU.mult), R=[ocs[g], cf3[g]], W=[o_n])
                        kb.op(V, lambda e, h=h, oc_=oc_: e.scalar_tensor_tensor(out=oc_, in0=c3[:, h, 0:64], scalar=cf3[g].t[:, 4 + h:5 + h], in1=oc_, op0=ALU.mult, op1=ALU.add), R=[cmb, cf3[g], o_n], W=[o_n])
                        kb.op(V, lambda e, h=h, oc_=oc_: e.scalar_tensor_tensor(out=oc_, in0=w3[:, h, 0:64], scalar=cf3[g].t[:, 8 + h:9 + h], in1=oc_, op0=ALU.mult, op1=ALU.add), R=[ows[g], cf3[g], o_n], W=[o_n])

            for g in range(2):
                front(g)
            for g in range(2):
                back(g)
            if ("o_n" in dbg_out) and i == DSLOT:
                dbg_dump("o_n", o_n.t[:], o_n)
            kb.op(A, lambda e: e.activation(out=sgn.t[:], in_=tmq.t[:, 0:512], func=EXP, scale=-1.0), R=[tmq], W=[sgn])
            kb.op(A, lambda e: e.activation(out=sgn.t[:], in_=sgn.t[:], func=AF.Ln, bias=ones_f[:, 0:1], scale=1.0), R=[sgn, cst], W=[sgn])
            kb.op(A, lambda e: e.activation(out=sgn.t[:], in_=sgn.t[:], func=EXP, scale=-1.0), R=[sgn], W=[sgn])
            kb.op(P, lambda e: e.tensor_tensor(out=sgn.t[:], in0=sgn.t[:], in1=tmq.t[:, 0:512], op=ALU.mult), R=[sgn, tmq], W=[sgn])
            kb.op(V, lambda e: e.tensor_tensor(out=mix.t[:, 512:1024], in0=o_n.t[:], in1=sgn.t[:], op=ALU.mult), R=[o_n, sgn], W=[mix])
            for c in range(8):
                kb.op(PE, lambda e, c=c: e.transpose(out=psb.t[:, c * 128:(c + 1) * 128], in_=mix.t[:, c * 128:(c + 1) * 128], identity=identb.t[:]), R=[mix, identb], W=[psb])
            kb.op(A, lambda e: e.copy(out=mixT.t[:].rearrange("p a b -> p (a b)"), in_=psb.t[:]), R=[psb], W=[mixT])
            for half, pY in ((0, MISC), (1, ACCC)):
                for c in range(8):
                    kb.op(PE, lambda e, c=c, half=half, pY=pY: e.matmul(pY.t[:], lhsT=mixT.t[:, c, :], rhs=Wo.t[:, c, half * 512:(half + 1) * 512], start=(c == 0), stop=(c == 7)), R=[mixT, Wo], W=[pY])
                kb.op(V, lambda e, half=half, pY=pY: e.tensor_tensor(out=hbuf.t[:, half * 512:(half + 1) * 512], in0=pY.t[:], in1=xrt[bi].t[:, half * 512:(half + 1) * 512], op=ALU.add), R=[pY, xrt[bi]], W=[hbuf])
            hsq = obuf[i % 2]
            kb.op(A, lambda e: e.activation(out=hsq.t[:], in_=hbuf.t[:], func=AF.Square), R=[hbuf], W=[hsq])
            kb.op(V, lambda e: e.tensor_reduce(out=hss.t[:, 0:1], in_=hsq.t[:], axis=mybir.AxisListType.X, op=ALU.add), R=[hsq], W=[hss])
            kb.op(A, lambda e: e.activation(out=hss.t[:, 0:1], in_=hss.t[:, 0:1], func=AF.Ln, bias=epsc, scale=1.0 / D), R=[hss, cst], W=[hss])
            kb.op(A, lambda e: e.activation(out=hss.t[:, 1:2], in_=hss.t[:, 0:1], func=EXP, scale=-0.5), R=[hss], W=[hss])
            ob = obuf[i % 2]
            kb.op(V, lambda e, ob=ob: e.scalar_tensor_tensor(out=ob.t[:], in0=hbuf.t[:], scalar=hss.t[:, 1:2], in1=fnw.t[:], op0=ALU.mult, op1=ALU.mult), R=[hbuf, hss, fnw], W=[ob])
            kb.dma("pool", y[i * 128:(i + 1) * 128, :], ob.t[:], R=[ob], sem=osem[i % 2])
        kb.barrier()
        stk[0].close()
        stk[0] = None

    kb.final_wait(out_sems)
    print('KB nops', kb.nops, kb.cnt)
    return nc, list(dbg_out.keys())


PERM = None


def _perm():
    gq, gk, gv, gz, gb, ga, nq, kc, vc, ks, vs, kw, vw, ng, nz = [np.arange(a, b) for a, b in zip(
        np.cumsum([0, 512, 512, 512, 512, 4, 4, 512, 128, 128, 128, 128, 128, 128, 24]),
        np.cumsum([512, 512, 512, 512, 4, 4, 512, 128, 128, 128, 128, 128, 128, 24, 512]))]
    return np.concatenate([gq, gk, gv, nq, kc, vc, ks, kw, gz, nz, vs, vw, ng, gb, ga])


def _consts():
    c = np.zeros((128, 1024), np.float32)
    j = np.arange(128)[:, None]; i = np.arange(128)[None, :]
    same = (j // 64) == (i // 64)
    c[:, 0:128] = np.eye(128)
    c[:, 128:256] = (same & (j < i))
    c[:, 256:384] = (same & (j <= i))
    c[:, 384:512] = same
    c[0, 512:640] = 1.0
    c[64, 640:768] = 1.0
    c[:, 768:772] = 1.0
    c[127, 771] = 0.0
    c[:, 772:900] = 1.0
    c[:, 900] = EPS
    return c


def _bucket(dist):
    n = np.maximum(dist, 0)
    large = 16 + (np.log(np.maximum(n, 16).astype(np.float32) / np.float32(16)) / np.float32(np.log(1024.0 / 16.0)) * np.float32(16)).astype(np.int32)
    return np.where(n < 16, n, np.minimum(large, 31)).astype(np.int64)


def _gather_bias(rel_ext, dist, valid, g):
    idx = np.where(valid, _bucket(dist), 32)
    t = rel_ext[idx][:, :, 4 * g:4 * g + 4]
    return np.ascontiguousarray(t.transpose(0, 2, 1).reshape(128, 512))


def prep_shared(inputs):
    sh = {}
    sh["w_in"] = np.ascontiguousarray(np.asarray(inputs["w_in"], np.float32)[0][:, _perm()])
    sh["nw"] = np.ascontiguousarray(np.asarray(inputs["norm_w"], np.float32)[0].reshape(8, 128).T)
    cwv = np.asarray(inputs["conv_w"], np.float32)[0]
    sh["convw"] = np.ascontiguousarray(cwv.reshape(4, 12, 128).transpose(2, 1, 0))
    sh["alog_b"] = np.ascontiguousarray(np.broadcast_to(np.tile(np.asarray(inputs["a_log"], np.float32)[0], 64)[None, :], (128, 256)))
    sh["dtb_b"] = np.ascontiguousarray(np.broadcast_to(np.tile(np.asarray(inputs["dt_bias"], np.float32)[0], 64)[None, :], (128, 256)))
    sh["gnw_b"] = np.ascontiguousarray(np.broadcast_to(np.tile(np.asarray(inputs["gdn_norm_w"], np.float32)[0], 4)[None, :], (128, 512)))
    sh["fnw_b"] = np.ascontiguousarray(np.broadcast_to(np.asarray(inputs["final_norm_w"], np.float32)[None, :], (128, D)))
    sh["w_out"] = np.ascontiguousarray(np.asarray(inputs["w_out"], np.float32)[0])
    for nm, k1, kp, k2 in (("k", "cmp_k_w1", "cmp_pe_k", "cmp_k_w2"), ("v", "cmp_v_w1", "cmp_pe_v", "cmp_v_w2")):
        sh["w1" + nm] = np.ascontiguousarray(np.asarray(inputs[k1], np.float32)[0].transpose(1, 0, 2))
        sh["pe" + nm] = np.ascontiguousarray(np.asarray(inputs[kp], np.float32)[0].T)
        sh["w2" + nm] = np.ascontiguousarray(np.asarray(inputs[k2], np.float32)[0])
    sh["consts"] = _consts()
    rel = np.asarray(inputs["rel_bias"], np.float32)
    rel_ext = np.concatenate([rel, np.full((1, 8), NEG, np.float32)], axis=0)
    sh["cexp"] = np.ascontiguousarray(np.broadcast_to(rel[31][None, :], (128, 8)))
    sh["tabCfar"] = np.ascontiguousarray(np.broadcast_to(np.repeat(rel[31].reshape(2, 4), 128, axis=1)[:, None, :], (2, 128, 512)))
    n_in = np.arange(128)[:, None, None]; ct = np.arange(4)[None, :, None]; j = np.arange(128)[None, None, :]
    n = 128 * ct + n_in
    sh["ovl"] = ((n <= 510) & (16 * n < 64 * j + 64) & (16 * n + 32 > 64 * j)).astype(np.float32)
    kk = np.arange(S)[None, :]
    sh["estk"] = (((kk // 64) % 64) == np.arange(64)[:, None]).astype(np.float32)
    ik = np.arange(128)[:, None]; iq = np.arange(128)[None, :]
    par = []
    for a in range(2):
        p = {}
        tS_ = np.empty((2, 9, 128, 512), np.float32)
        tW_ = np.empty((2, 2, 128, 512), np.float32)
        for g in range(2):
            for m in range(9):
                d = m + a - 1
                dist = 128 * d + iq - ik
                tS_[g, m] = _gather_bias(rel_ext, dist, dist >= 0, g)
            for mi in range(2):
                d = 4 + mi + a - 1
                dist = 128 * d + iq - ik
                tW_[mi, g] = _gather_bias(rel_ext, dist, (dist >= 0) & (dist < 512), g)
        p["tabS"] = tS_; p["tabWx"] = tW_
        tC_ = np.empty((2, len(CIDX), 128, 512), np.float32)
        for (i, ct_), idx in CIDX.items():
            e = 2 * i + a - 16 * ct_
            dist = 128 * e + iq - 16 * ik - 31
            for g in range(2):
                tC_[g, idx] = _gather_bias(rel_ext, dist, dist >= 0, g)
        p["tabC"] = tC_
        x_ = np.arange(256)[None, :]; hh = (np.arange(128) // 64)[:, None]
        relj = (x_ - 126) - 2 * a
        adj = np.where(relj > hh, np.float32(NEG), np.where((relj == hh) | (relj == hh - 1), np.float32(1e4), np.float32(0.0)))
        p["adjT"] = np.ascontiguousarray(adj.astype(np.float32))
        j0 = np.zeros((128, 32), np.float32)
        for i in range(32):
            if 2 * i + a >= 1:
                j0[:, i] = 1e4
        p["j0b"] = j0
        p["pw"] = np.ascontiguousarray(np.broadcast_to(np.array([1.0 - a, float(a)], np.float32)[None, :], (128, 2)))
        par.append(p)
    return sh, par


def prep_inputs(inputs, core, sh=None, par=None):
    if sh is None:
        sh, par = prep_shared(inputs)
    b = core // 2
    a = core % 2
    x = np.asarray(inputs["x"], np.float32)[b]
    m = dict(sh)
    m.update(par[a])
    m["xT"] = np.ascontiguousarray(x.T)
    m["xr"] = np.ascontiguousarray(x.reshape(64, 128, D)[a::2].reshape(S // 2, D))
    return m


_NC = [None]


def kernel(**inputs):
    if _NC[0] is None:
        _NC[0] = build()[0]
    nc = _NC[0]
    sh, par = prep_shared(inputs)
    in_maps = [prep_inputs(inputs, c, sh, par) for c in range(8)]
    res = run_bass_kernel_spmd(nc, in_maps, core_ids=list(range(8)))
    out = np.empty((4, S, D), np.float32)
    for c in range(8):
        b, a = c // 2, c % 2
        yc = np.asarray(res.results[c]["y"], np.float32).reshape(32, 128, D)
        out[b].reshape(64, 128, D)[a::2] = yc
    return out
```

```python
import numpy as np
import concourse.bass as bass
import concourse.mybir as mybir
from concourse.bass_utils import run_bass_kernel_spmd

F32 = mybir.dt.float32
BF16 = mybir.dt.bfloat16
ALU = mybir.AluOpType
AF = mybir.ActivationFunctionType

S = 8192
NT = 64
D = 1024
EPS = 1e-6
NEG = -1e30
MNEG = -30000.0


class Buf:
    __slots__ = ("w", "r", "bank")

    def __init__(self, bank=None):
        self.w = None
        self.r = {}
        self.bank = bank


class Tile:
    def __init__(self, t):
        self.t = t
        self.b = Buf()


class DSem:
    def __init__(self, nc, name):
        self.h = nc.alloc_semaphore(name)
        self.count = 0
        self.key = ("dma", name)


class KB:
    def __init__(self, nc):
        self.nc = nc
        self.eng = {"pe": nc.tensor, "dve": nc.vector, "act": nc.scalar, "pool": nc.gpsimd, "sp": nc.sync}
        self.semh = {}
        self.cnt = {}
        self.seen = {}
        for e in self.eng:
            self.semh[("eng", e)] = nc.alloc_semaphore("c_" + e)
            self.cnt[e] = 0
            self.seen[e] = {}
        self.dsems = []
        self.nd = 0
        self.nops = 0
        self.pe_bank_excl = True
        import os
        self.limit = int(os.environ.get("KBLIMIT", "1000000000"))

    def dsem(self, name=None):
        self.nd += 1
        d = DSem(self.nc, name or f"d{self.nd}")
        self.semh[d.key] = d.h
        self.dsems.append(d)
        return d

    def _wait(self, e, dep):
        key, val = dep
        if key == ("eng", "pe") and e == "pe":
            return
        if self.seen[e].get(key, 0) >= val:
            return
        self.eng[e].wait_ge(self.semh[key], val)
        self.seen[e][key] = val

    def _deps(self, e, R, W):
        for b in R:
            if b.w is not None:
                self._wait(e, b.w)
        for b in W:
            if b.w is not None:
                self._wait(e, b.w)
            for d in b.r.values():
                self._wait(e, d)

    def op(self, e, fn, R=(), W=()):
        self.nops += 1
        if self.nops > self.limit:
            return
        R = [x.b if isinstance(x, Tile) else x for x in R]
        W = [x.b if isinstance(x, Tile) else x for x in W]
        self._deps(e, R, W)
        if e != "pe":
            for b in R:
                if b.bank is not None:
                    for e2, tk in b.bank.items():
                        if e2 != e:
                            self._wait(e, tk)
        elif self.pe_bank_excl:
            for b in W:
                if b.bank is not None:
                    for e2, tk in b.bank.items():
                        self._wait(e, tk)
        ins = fn(self.eng[e])
        self.cnt[e] += 1
        ins.then_inc(self.semh[("eng", e)], 1)
        tok = (("eng", e), self.cnt[e])
        for b in R:
            b.r[tok[0]] = tok
            if b.bank is not None and e != "pe":
                b.bank[e] = tok
        for b in W:
            b.w = tok
            b.r = {}

    def dma(self, q, out, in_, R=(), W=(), sem=None):
        self.nops += 1
        if self.nops > self.limit:
            return
        R = [x.b if isinstance(x, Tile) else x for x in R]
        W = [x.b if isinstance(x, Tile) else x for x in W]
        self._deps(q, R, W)
        ins = self.eng[q].dma_start(out=out, in_=in_)
        sem.count += 16
        ins.then_inc(sem.h, 16)
        tok = (sem.key, sem.count)
        for b in R:
            b.r[tok[0]] = tok
        for b in W:
            b.w = tok
            b.r = {}

    def seal(self, sem, tiles):
        for t in tiles:
            b = t.b if isinstance(t, Tile) else t
            b.w = (sem.key, sem.count)

    def barrier(self):
        for e in self.eng:
            for e2 in self.eng:
                if e2 != e and self.cnt[e2] > 0:
                    self._wait(e, (("eng", e2), self.cnt[e2]))
            for d in self.dsems:
                if d.count > 0:
                    self._wait(e, (d.key, d.count))

    def final_wait(self, sems):
        for d in sems:
            if d.count > 0:
                self._wait("sp", (d.key, d.count))


CIDX = {}
for _ct in range(4):
    for _i in range(32):
        if 2 * _i + 1 - 16 * _ct >= 0 and 2 * _i - 16 * _ct < 23:
            CIDX[(_i, _ct)] = len(CIDX)


def build(PH=99, DBG=(), NBLK=None, GT=None, NSL=None, DSLOT=-1):
    nc = bass.Bass("TRN2", target_bir_lowering=False)
    try:
        nc.allow_low_precision("bf16 matmul operands, fp32 accumulation")
        nc.allow_non_contiguous_dma("strided scratch layouts")
    except Exception:
        pass
    kb = KB(nc)
    V, A, P, PE = "dve", "act", "pool", "pe"

    def din(name, shape, dt=F32):
        return nc.dram_tensor(name, list(shape), dt, kind="ExternalInput").ap()

    def dscr(name, shape, dt):
        return nc.dram_tensor(name, list(shape), dt).ap()

    from contextlib import ExitStack
    stk = [None]

    def sb(name, shape, dt=F32):
        if stk[0] is None:
            return Tile(nc.alloc_sbuf_tensor(name, list(shape), dt))
        return Tile(stk[0].enter_context(nc.sbuf_tensor(name, list(shape), dt)))

    xT = din("xT", [D, S])
    xr = din("xr", [S // 2, D])
    w_in = din("w_in", [D, 3872])
    nw = din("nw", [128, 8])
    convw = din("convw", [128, 12, 4])
    alog_b = din("alog_b", [128, 256])
    dtb_b = din("dtb_b", [128, 256])
    gnw_b = din("gnw_b", [128, 512])
    fnw_b = din("fnw_b", [128, D])
    w_out = din("w_out", [D, D])
    w1k = din("w1k", [64, 32, 128]); w1v = din("w1v", [64, 32, 128])
    pek = din("pek", [64, 32]); pev = din("pev", [64, 32])
    w2k = din("w2k", [128, 64]); w2v = din("w2v", [128, 64])
    tabS = din("tabS", [2, 9, 128, 512])
    tabWx = din("tabWx", [2, 2, 128, 512])
    tabCfar = din("tabCfar", [2, 128, 512])
    tabC = din("tabC", [2, 44, 128, 512])
    cexp = din("cexp", [128, 8])
    ovl = din("ovl", [128, 4, 128])
    adjT = din("adjT", [128, 256])
    j0b = din("j0b", [128, 32])
    pw = din("pw", [128, 2])
    consts = din("consts", [128, 1024])
    estk = din("estk", [64, S])
    y = nc.dram_tensor("y", [S // 2, D], F32, kind="ExternalOutput").ap()
    dbg_out = {}
    for name, shape, dt in DBG:
        dbg_out[name] = nc.dram_tensor("dbg_" + name, list(shape), dt, kind="ExternalOutput").ap()

    GQKV = dscr("GQKV", [12, 128, S], BF16)
    NQ = dscr("NQ", [8, 64, S], BF16)
    KVT = dscr("KVT", [4, 128, S], BF16)
    TM = dscr("TM", [S, 1312], F32)
    MIXG = dscr("MIXG", [S, 512], BF16)
    scr_b = {n: Buf() for n in ("GQKV", "NQ", "KVT", "TM", "MIXG")}

    cst = sb("cst", [128, 1024])
    ident = cst.t[:, 0:128]
    mask2 = cst.t[:, 128:384]
    tri2 = cst.t[:, 256:384]
    blk2 = cst.t[:, 384:512]
    sel0 = cst.t[:, 512:640]
    sel1 = cst.t[:, 640:768]
    nvalid = cst.t[:, 768:772]
    ones_f = cst.t[:, 772:900]
    epsc = cst.t[:, 900:901]
    identb = sb("identb", [128, 128], BF16)
    onesb = sb("onesb", [128, 128], BF16)
    gba = sb("gba", [128, 64, 8])
    ld = kb.dsem("ld0")
    kb.dma("sp", cst.t[:], consts, W=[cst], sem=ld)
    kb.op(V, lambda e: e.tensor_copy(out=identb.t[:], in_=ident), R=[cst], W=[identb])
    kb.op(V, lambda e: e.tensor_copy(out=onesb.t[:], in_=ones_f), R=[cst], W=[onesb])

    psum = [Tile(nc.alloc_psum_tensor(f"ps{i}", [128, 512], F32)) for i in range(7)]
    psb = Tile(nc.alloc_psum_tensor("psb", [128, 1024], BF16))
    for p_ in psum + [psb]:
        p_.b.bank = {}
    out_sems = []

    def dbg_dump(name, ap_src, tile, rows=128):
        if name in dbg_out:
            d = kb.dsem()
            out_sems.append(d)
            kb.dma("sp", dbg_out[name], ap_src, R=[tile], sem=d)

    stk[0] = ExitStack()
    Wb = sb("Wb", [128, 8, 3872], BF16)
    cw = sb("cw", [128, 12, 4])
    nwt = sb("nwt", [128, 8])
    ld1 = kb.dsem("ld1")
    kb.dma("sp", cw.t[:], convw, W=[cw], sem=ld1)
    kb.dma("sp", nwt.t[:], nw, W=[nwt], sem=ld1)
    kb.seal(ld1, [cw, nwt])
    ph1stk = stk[0]
    stk[0] = ExitStack()
    wst = [sb(f"wst{i}", [128, 3872]) for i in range(2)]
    wsem = [kb.dsem(f"wl{i}") for i in range(2)]
    w_v = w_in.rearrange("(kc p) c -> p kc c", p=128)
    for kc in range(8):
        st = wst[kc % 2]
        kb.dma("sp" if kc % 2 == 0 else "act", st.t[:], w_v[:, kc, :], W=[st], sem=wsem[kc % 2])
        half = 1936
        kb.op(V, lambda e, st=st, kc=kc: e.tensor_scalar(out=Wb.t[:, kc, 0:half], in0=st.t[:, 0:half], scalar1=nwt.t[:, kc:kc + 1], scalar2=None, op0=ALU.mult), R=[st, nwt], W=[Wb])
        kb.op(A, lambda e, st=st, kc=kc: e.activation(out=Wb.t[:, kc, half:3872], in_=st.t[:, half:3872], func=AF.Copy, scale=nwt.t[:, kc:kc + 1]), R=[st, nwt], W=[Wb])

    kb.barrier()
    stk[0].close()
    stk[0] = ph1stk
    xf = [sb(f"xf{i}", [128, 8, 512]) for i in range(2)]
    xsem = [kb.dsem(f"xl{i}") for i in range(2)]
    xb = sb("xb", [128, 8, 512], BF16)
    sq = sb("sq", [128, 8, 512], BF16)
    rbc = sb("rbc", [128, 512])
    cbuf = [sb(f"cbuf{i}", [128, 515]) for i in range(12)]
    for c in cbuf:
        kb.op(P, lambda e, c=c: e.memset(c.t[:, 0:3], 0.0), W=[c])
    acc = [sb(f"acc{i}", [128, 512]) for i in range(4)]
    slb = [sb(f"slb{i}", [128, 512]) for i in range(4)]
    sq2 = [sb(f"sq2{i}", [128, 512]) for i in range(4)]
    rn = [sb(f"rn{i}", [128, 512]) for i in range(4)]
    ptmp = sb("ptmp", [128, 512])
    stg = [sb(f"stg{i}", [128, 512], BF16) for i in range(4)]
    stsem = [kb.dsem(f"st{i}") for i in range(4)]
    tmb = [sb(f"tmb{i}", [128, 1312]) for i in range(2)]
    tmsem = [kb.dsem(f"tm{i}") for i in range(2)]
    rcol = [sb(f"rcol{i}", [128, 2]) for i in range(2)]
    xT_v = xT.rearrange("(kc p) t -> p kc t", p=128)
    nstg = [0]

    def store_stage(src_tile_fn, dram_aps):
        i = nstg[0] % 4
        nstg[0] += 1
        st = stg[i]
        src_tile_fn(st)
        for (dap, p0, p1, bkey) in dram_aps:
            kb.dma("pool", dap, st.t[p0:p1, :], R=[st], sem=stsem[i])

    NB = (16 if PH >= 1 else 0) if NBLK is None else NBLK
    for tb in range(NB):
        t0 = tb * 512
        xt = xf[tb % 2]
        kb.dma("sp", xt.t[:], xT_v[:, :, t0:t0 + 512], W=[xt], sem=xsem[tb % 2])
        kb.op(V, lambda e: e.tensor_copy(out=xb.t[:, 0:4, :], in_=xt.t[:, 0:4, :]), R=[xt], W=[xb])
        kb.op(V, lambda e: e.tensor_copy(out=xb.t[:, 4:8, :], in_=xt.t[:, 4:8, :]), R=[xt], W=[xb])
        kb.op(A, lambda e: e.activation(out=sq.t[:], in_=xt.t[:], func=AF.Square), R=[xt], W=[sq])
        pr = psum[0]
        for kc in range(8):
            kb.op(PE, lambda e, kc=kc: e.matmul(pr.t[:], lhsT=onesb.t[:], rhs=sq.t[:, kc, :], start=(kc == 0), stop=(kc == 7)), R=[onesb, sq], W=[pr])
        kb.op(A, lambda e: e.activation(out=rbc.t[:], in_=pr.t[:], func=AF.Ln, bias=epsc, scale=1.0 / D), R=[pr, cst], W=[rbc])
        kb.op(A, lambda e: e.activation(out=rbc.t[:], in_=rbc.t[:], func=AF.Exp, scale=-0.5), R=[rbc], W=[rbc])
        pending = []
        for m in range(20):
            pm = psum[(1, 2, 3, 0)[m % 4]]
            for kc in range(8):
                kb.op(PE, lambda e, kc=kc, m=m, pm=pm: e.matmul(pm.t[:], lhsT=Wb.t[:, kc, m * 128:(m + 1) * 128], rhs=xb.t[:, kc, :], start=(kc == 0), stop=(kc == 7)), R=[Wb, xb], W=[pm])
            while len(pending) > 2:
                pending.pop(0)()
            if m < 12:
                cb = cbuf[m]
                ce = V
                a_ = acc[m % 4]; s_ = slb[m % 4]
                kb.op(V, lambda e, pm=pm, cb=cb: e.tensor_tensor(out=cb.t[:, 3:515], in0=pm.t[:], in1=rbc.t[:], op=ALU.mult), R=[pm, rbc], W=[cb])
                kb.op(ce, lambda e, cb=cb, a_=a_, m=m: e.tensor_scalar(out=a_.t[:], in0=cb.t[:, 3:515], scalar1=cw.t[:, m, 3:4], scalar2=None, op0=ALU.mult), R=[cb, cw], W=[a_])
                for j in (2, 1, 0):
                    if ce == V:
                        kb.op(ce, lambda e, cb=cb, a_=a_, m=m, j=j: e.scalar_tensor_tensor(out=a_.t[:], in0=cb.t[:, j:j + 512], scalar=cw.t[:, m, j:j + 1], in1=a_.t[:], op0=ALU.mult, op1=ALU.add), R=[cb, cw, a_], W=[a_])
                    else:
                        kb.op(ce, lambda e, cb=cb, m=m, j=j: e.tensor_scalar(out=ptmp.t[:], in0=cb.t[:, j:j + 512], scalar1=cw.t[:, m, j:j + 1], scalar2=None, op0=ALU.mult), R=[cb, cw], W=[ptmp])
                        kb.op(ce, lambda e, a_=a_: e.tensor_tensor(out=a_.t[:], in0=a_.t[:], in1=ptmp.t[:], op=ALU.add), R=[ptmp, a_], W=[a_])
                kb.op(P, lambda e, cb=cb: e.tensor_copy(out=cb.t[:, 0:3], in_=cb.t[:, 512:515]), R=[cb], W=[cb])
                if m < 8:
                    kb.op(A, lambda e, a_=a_, s_=s_: e.activation(out=s_.t[:], in_=a_.t[:], func=AF.Silu), R=[a_], W=[s_])
                    q2 = sq2[m % 4]; r2 = rn[m % 4]
                    kb.op(A, lambda e, s_=s_, q2=q2: e.activation(out=q2.t[:], in_=s_.t[:], func=AF.Square), R=[s_], W=[q2])
                    pn = psum[4 + (m % 2)]

                    def fin(m=m, s_=s_, q2=q2, r2=r2, pn=pn, t0=t0):
                        kb.op(PE, lambda e: e.matmul(pn.t[:], lhsT=ones_f, rhs=q2.t[:], start=True, stop=True), R=[cst, q2], W=[pn])
                        kb.op(A, lambda e: e.activation(out=r2.t[:], in_=pn.t[:], func=AF.Ln, bias=epsc, scale=1.0), R=[pn, cst], W=[r2])
                        kb.op(A, lambda e: e.activation(out=r2.t[:], in_=r2.t[:], func=AF.Exp, scale=-0.5), R=[r2], W=[r2])
                        scl = (128.0 ** -0.5) if m < 4 else 1.0
                        store_stage(lambda st: kb.op(V, lambda e: e.scalar_tensor_tensor(out=st.t[:], in0=s_.t[:], scalar=scl, in1=r2.t[:], op0=ALU.mult, op1=ALU.mult), R=[s_, r2], W=[st]),
                                    [(GQKV[m, :, t0:t0 + 512], 0, 128, "GQKV")])
                    pending.append(fin)
                else:
                    store_stage(lambda st, a_=a_: kb.op(A, lambda e: e.activation(out=st.t[:], in_=a_.t[:], func=AF.Silu), R=[a_], W=[st]),
                                [(GQKV[m, :, t0:t0 + 512], 0, 128, "GQKV")])
            elif m < 16:
                hp = m - 12
                store_stage(lambda st, pm=pm: kb.op(V, lambda e: e.scalar_tensor_tensor(out=st.t[:], in0=pm.t[:], scalar=0.125, in1=rbc.t[:], op0=ALU.mult, op1=ALU.mult), R=[pm, rbc], W=[st]),
                            [(NQ[2 * hp, :, t0:t0 + 512], 0, 64, "NQ"), (NQ[2 * hp + 1, :, t0:t0 + 512], 64, 128, "NQ")])
            else:
                kv = m - 16
                store_stage(lambda st, pm=pm: kb.op(V, lambda e: e.tensor_tensor(out=st.t[:], in0=pm.t[:], in1=rbc.t[:], op=ALU.mult), R=[pm, rbc], W=[st]),
                            [(KVT[kv, :, t0:t0 + 512], 0, 128, "KVT")])
        while pending:
            pending.pop(0)()
        for sub in range(4):
            tt = tb * 4 + sub
            tm_ = tmb[tt % 2]
            rc = rcol[tt % 2]
            pc = psum[6]
            kb.op(PE, lambda e, sub=sub: e.matmul(pc.t[:, 0:1], lhsT=rbc.t[0:1, sub * 128:(sub + 1) * 128], rhs=ones_f[0:1, 0:1], start=True, stop=True), R=[rbc, cst], W=[pc])
            kb.op(A, lambda e, rc=rc: e.copy(out=rc.t[:, 1:2], in_=pc.t[:, 0:1]), R=[pc], W=[rc])
            for part, (c0, c1) in enumerate(((0, 512), (512, 1024), (1024, 1312))):
                pm = psum[1 + part]
                for kc in range(8):
                    kb.op(PE, lambda e, kc=kc, sub=sub, pm=pm, c0=c0, c1=c1: e.matmul(pm.t[:, 0:c1 - c0], lhsT=xb.t[:, kc, sub * 128:(sub + 1) * 128], rhs=Wb.t[:, kc, 2560 + c0:2560 + c1], start=(kc == 0), stop=(kc == 7)), R=[Wb, xb], W=[pm])
                if part < 2:
                    kb.op(A, lambda e, pm=pm, c0=c0, c1=c1, tm_=tm_, rc=rc: e.activation(out=tm_.t[:, c0:c1], in_=pm.t[:, 0:c1 - c0], func=AF.Copy, scale=rc.t[:, 1:2]), R=[pm, rc], W=[tm_])
                else:
                    kb.op(V, lambda e, pm=pm, c0=c0, c1=c1, tm_=tm_, rc=rc: e.tensor_scalar(out=tm_.t[:, c0:c1], in0=pm.t[:, 0:c1 - c0], scalar1=rc.t[:, 1:2], scalar2=None, op0=ALU.mult), R=[pm, rc], W=[tm_])
            kb.op(P, lambda e, tm_=tm_, tt=tt: e.tensor_copy(out=gba.t[:, tt, :], in_=tm_.t[:, 1304:1312]), R=[tm_], W=[gba])
            kb.dma("pool", TM[tt * 128:(tt + 1) * 128, :], tm_.t[:], R=[tm_], sem=tmsem[tt % 2])

    if "gba" in dbg_out:
        dbg_dump("gba", gba.t[:].rearrange("p a b -> p (a b)"), gba)
    kb.barrier()
    stk[0].close()
    stk[0] = None
    for name in ("GQKV", "NQ", "KVT", "TM"):
        if name in dbg_out:
            src = {"GQKV": GQKV, "NQ": NQ, "KVT": KVT, "TM": TM}[name]
            d = kb.dsem(); out_sems.append(d)
            if name == "TM":
                for c in range(64):
                    kb.dma("sp", dbg_out[name][c * 128:(c + 1) * 128, :], src[c * 128:(c + 1) * 128, :], sem=d)
            else:
                for c in range(src.shape[0]):
                    kb.dma("sp", dbg_out[name][c], src[c], sem=d)


    def pq(bank, q0, q1):
        return psum[bank].t[:, q0 * 128:q1 * 128]

    pqb = [[Buf(bank=psum[i].b.bank) for _ in range(4)] for i in range(7)]
    if PH >= 2:
        stk[0] = ExitStack()
        sc = {}
        for nm in ("e1", "d1", "beta", "lnb", "z", "sp", "Aex", "g", "gc", "gl", "yy", "egc", "bg", "kd", "EGL0", "EGL1", "dtb", "alg"):
            sc[nm] = sb("sc_" + nm, [128, 256])
        gbv = gba.t[:, :, 0:4]
        gav = gba.t[:, :, 4:8]

        def v3(tl):
            return tl.t[:].rearrange("p (a b) -> p a b", b=4)

        ld2 = kb.dsem("ld2")
        kb.dma("sp", sc["dtb"].t[:], dtb_b, W=[sc["dtb"]], sem=ld2)
        kb.dma("sp", sc["alg"].t[:], alog_b, W=[sc["alg"]], sem=ld2)
        kb.seal(ld2, [sc["dtb"], sc["alg"]])
        kb.op(A, lambda e: e.activation(out=v3(sc["e1"]), in_=gbv, func=AF.Exp, scale=-1.0), R=[gba], W=[sc["e1"]])
        kb.op(V, lambda e: e.tensor_scalar(out=sc["d1"].t[:], in0=sc["e1"].t[:], scalar1=1.0, scalar2=None, op0=ALU.add), R=[sc["e1"]], W=[sc["d1"]])
        kb.op(V, lambda e: e.reciprocal(out=sc["beta"].t[:], in_=sc["d1"].t[:]), R=[sc["d1"]], W=[sc["beta"]])
        kb.op(A, lambda e: e.activation(out=sc["lnb"].t[:], in_=sc["d1"].t[:], func=AF.Ln), R=[sc["d1"]], W=[sc["lnb"]])
        kb.op(V, lambda e: e.tensor_tensor(out=v3(sc["z"]), in0=gav, in1=v3(sc["dtb"]), op=ALU.add), R=[gba, sc["dtb"]], W=[sc["z"]])
        kb.op(A, lambda e: e.activation(out=sc["z"].t[:], in_=sc["z"].t[:], func=AF.Exp), R=[sc["z"]], W=[sc["z"]])
        kb.op(V, lambda e: e.tensor_scalar(out=sc["z"].t[:], in0=sc["z"].t[:], scalar1=1.0, scalar2=None, op0=ALU.add), R=[sc["z"]], W=[sc["z"]])
        kb.op(A, lambda e: e.activation(out=sc["sp"].t[:], in_=sc["z"].t[:], func=AF.Ln), R=[sc["z"]], W=[sc["sp"]])
        kb.op(A, lambda e: e.activation(out=sc["Aex"].t[:], in_=sc["alg"].t[:], func=AF.Exp), R=[sc["alg"]], W=[sc["Aex"]])
        kb.op(V, lambda e: e.scalar_tensor_tensor(out=sc["g"].t[:], in0=sc["sp"].t[:], scalar=-1.0, in1=sc["Aex"].t[:], op0=ALU.mult, op1=ALU.mult), R=[sc["sp"], sc["Aex"]], W=[sc["g"]])
        p0 = psum[0]
        kb.op(PE, lambda e: e.matmul(p0.t[:, 0:256], lhsT=tri2, rhs=sc["g"].t[:], start=True, stop=True), R=[cst, sc["g"]], W=[p0])
        kb.op(PE, lambda e: e.matmul(p0.t[:, 256:512], lhsT=blk2, rhs=sc["g"].t[:], start=True, stop=True), R=[cst, sc["g"]], W=[p0])
        kb.op(V, lambda e: e.tensor_copy(out=sc["gc"].t[:], in_=p0.t[:, 0:256]), R=[p0], W=[sc["gc"]])
        kb.op(V, lambda e: e.tensor_copy(out=sc["gl"].t[:], in_=p0.t[:, 256:512]), R=[p0], W=[sc["gl"]])
        kb.op(V, lambda e: e.tensor_tensor(out=sc["yy"].t[:], in0=sc["gc"].t[:], in1=sc["lnb"].t[:], op=ALU.subtract), R=[sc["gc"], sc["lnb"]], W=[sc["yy"]])
        kb.op(A, lambda e: e.activation(out=sc["egc"].t[:], in_=sc["gc"].t[:], func=AF.Exp), R=[sc["gc"]], W=[sc["egc"]])
        kb.op(V, lambda e: e.tensor_tensor(out=sc["bg"].t[:], in0=sc["beta"].t[:], in1=sc["egc"].t[:], op=ALU.mult), R=[sc["beta"], sc["egc"]], W=[sc["bg"]])
        kb.op(V, lambda e: e.tensor_tensor(out=sc["kd"].t[:], in0=sc["gl"].t[:], in1=sc["gc"].t[:], op=ALU.subtract), R=[sc["gl"], sc["gc"]], W=[sc["kd"]])
        kb.op(A, lambda e: e.activation(out=sc["kd"].t[:], in_=sc["kd"].t[:], func=AF.Exp), R=[sc["kd"]], W=[sc["kd"]])
        p1 = psum[1]
        kb.op(PE, lambda e: e.matmul(p1.t[:, 0:256], lhsT=sel0, rhs=sc["gl"].t[:], start=True, stop=True), R=[cst, sc["gl"]], W=[p1])
        kb.op(PE, lambda e: e.matmul(p1.t[:, 256:512], lhsT=sel1, rhs=sc["gl"].t[:], start=True, stop=True), R=[cst, sc["gl"]], W=[p1])
        kb.op(A, lambda e: e.activation(out=sc["EGL0"].t[:], in_=p1.t[:, 0:256], func=AF.Exp), R=[p1], W=[sc["EGL0"]])
        kb.op(A, lambda e: e.activation(out=sc["EGL1"].t[:], in_=p1.t[:, 256:512], func=AF.Exp), R=[p1], W=[sc["EGL1"]])
        for nm in ("g", "beta", "gc"):
            dbg_dump("sc_" + nm, sc[nm].t[:], sc[nm])
        kb.barrier()

        kqv = [sb(f"kqv{i}", [128, 12, 128], BF16) for i in range(2)]
        kqsem = [kb.dsem(f"kq{i}") for i in range(2)]
        gzt = [sb(f"gzt{i}", [128, 512]) for i in range(2)]
        gzsem = [kb.dsem(f"gz{i}") for i in range(2)]
        gnw = sb("gnw", [128, 512])
        ld3 = kb.dsem("ld3")
        kb.dma("sp", gnw.t[:], gnw_b, W=[gnw], sem=ld3)
        Sst = [[sb(f"S{h}_{i}", [128, 128]) for i in range(2)] for h in range(4)]
        Sbf = [[sb(f"Sb{h}_{i}", [128, 128], BF16) for i in range(2)] for h in range(4)]
        for h in range(4):
            kb.op(P, lambda e, h=h: e.memset(Sst[h][0].t[:], 0.0), W=[Sst[h][0]])
            kb.op(P, lambda e, h=h: e.memset(Sbf[h][0].t[:], 0.0), W=[Sbf[h][0]])
        H4 = range(4)
        dg = [sb(f"dg{h}", [128, 256]) for h in H4]
        t1 = [sb(f"t1{h}", [128, 256]) for h in H4]
        Dd = [sb(f"Dd{h}", [128, 256]) for h in H4]
        MA = [sb(f"MA{h}", [128, 128]) for h in H4]
        ATb = [sb(f"ATb{h}", [128, 128], BF16) for h in H4]
        TTb = [sb(f"TTb{h}", [128, 128], BF16) for h in H4]
        XYb = [[sb(f"XYb{h}_{i}", [128, 256], BF16) for i in range(2)] for h in H4]
        Zb = [sb(f"Zb{h}", [128, 128], BF16) for h in H4]
        Zt = [sb(f"Z{h}", [128, 128]) for h in H4]
        egb = [sb(f"egb{h}", [128, 128]) for h in H4]
        rk = [sb(f"rk{h}", [128, 256], BF16) for h in H4]
        kdec = [sb(f"kdec{h}", [128, 128], BF16) for h in H4]
        WU = [sb(f"WU{h}", [128, 256], BF16) for h in H4]
        qd = [sb(f"qd{h}", [128, 128]) for h in H4]
        QtT = [sb(f"QtT{h}", [128, 128], BF16) for h in H4]
        NPh = [[sb(f"NPh{h}_{c}", [128, 128], BF16) for c in range(2)] for h in H4]
        ot = [sb(f"ot{i}", [128, 512]) for i in range(2)]
        osq = sb("osq", [128, 512])
        oss = sb("oss", [128, 8])
        gsg = sb("gsg", [128, 512])
        mixst = [sb(f"mixst{i}", [128, 512], BF16) for i in range(2)]
        mxsem = [kb.dsem(f"mx{i}") for i in range(2)]
        GQ_v = GQKV.rearrange("c p t -> p c t")
        NTG = NT if GT is None else GT
        cslot = [0]

        dslot = [0]

        def cps():
            q = cslot[0] % 4
            cslot[0] += 1
            return psum[4].t[:, q * 128:(q + 1) * 128], pqb[4][q]

        def cpd():
            i = dslot[0] % 8
            dslot[0] += 1
            bank = 5 + i // 4
            q = i % 4
            return psum[bank].t[:, q * 128:(q + 1) * 128], pqb[bank][q]

        import os as _os
        GST = int(_os.environ.get('GSTAGE', '99'))
        out_defer = []
        for t in range(NTG):
            kq = kqv[t % 2]
            kb.dma("sp", kq.t[:], GQ_v[:, :, t * 128:(t + 1) * 128], R=[scr_b["GQKV"]], W=[kq], sem=kqsem[t % 2])
            gz_ = gzt[t % 2]
            kb.dma("act", gz_.t[:], TM[t * 128:(t + 1) * 128, 0:512], R=[scr_b["TM"]], W=[gz_], sem=gzsem[t % 2])
            o_ = ot[t % 2]
            col = lambda nm, h: sc[nm].t[:, t * 4 + h:t * 4 + h + 1]
            B = lambda h, q: pqb[h][q]
            for h in H4:
                qT = kq.t[:, h, :]; kT = kq.t[:, 4 + h, :]
                kb.op(PE, lambda e, h=h, kT=kT: e.matmul(pq(h, 0, 1), lhsT=kT, rhs=kT, start=True, stop=True), R=[kq], W=[B(h, 0)])
                kb.op(PE, lambda e, h=h, kT=kT, qT=qT: e.matmul(pq(h, 1, 2), lhsT=kT, rhs=qT, start=True, stop=True), R=[kq], W=[B(h, 1)])
                kb.op(PE, lambda e, h=h: e.matmul(pq(h, 2, 3), lhsT=col("yy", h).to_broadcast([128, 128]), rhs=ident, start=True, stop=True), R=[cst, sc["yy"]], W=[B(h, 2)])
                kb.op(PE, lambda e, h=h: e.matmul(pq(h, 3, 4), lhsT=col("gc", h).to_broadcast([128, 128]), rhs=ident, start=True, stop=True), R=[cst, sc["gc"]], W=[B(h, 3)])
            if GST < 2:
                continue
            for h in H4:
                kb.op(V, lambda e, h=h: e.tensor_scalar(out=t1[h].t[:], in0=pq(h, 2, 4), scalar1=col("gc", h), scalar2=0.0, op0=ALU.subtract, op1=ALU.min), R=[B(h, 2), B(h, 3), sc["gc"]], W=[t1[h]])
            for h in H4:
                kb.op(A, lambda e, h=h: e.activation(out=egb[h].t[:], in_=pq(h, 3, 4), func=AF.Exp), R=[B(h, 3), t1[h]], W=[egb[h]])
                kb.op(A, lambda e, h=h: e.activation(out=t1[h].t[:], in_=t1[h].t[:], func=AF.Exp), R=[t1[h]], W=[t1[h]])
            for h in H4:
                kb.op(V, lambda e, h=h: e.tensor_tensor(out=Dd[h].t[:], in0=t1[h].t[:], in1=mask2, op=ALU.mult), R=[t1[h], cst], W=[Dd[h]])
            for h in H4:
                kb.op(V, lambda e, h=h: e.tensor_tensor(out=MA[h].t[:], in0=pq(h, 0, 1), in1=Dd[h].t[:, 0:128], op=ALU.mult), R=[B(h, 0), Dd[h]], W=[MA[h]])
                kb.op(V, lambda e, h=h: e.tensor_tensor(out=ATb[h].t[:], in0=pq(h, 1, 2), in1=Dd[h].t[:, 128:256], op=ALU.mult), R=[B(h, 1), Dd[h]], W=[ATb[h]])
            if GST < 3:
                continue
            for h in H4:
                kb.op(PE, lambda e, h=h: e.transpose(out=pq(h, 2, 3), in_=MA[h].t[:, 0:128], identity=ident), R=[MA[h], cst], W=[B(h, 2)])
                kb.op(A, lambda e, h=h: e.copy(out=XYb[h][0].t[:, 0:128], in_=pq(h, 2, 3)), R=[B(h, 2)], W=[XYb[h][0]])
                kb.op(P, lambda e, h=h: e.tensor_copy(out=XYb[h][0].t[:, 128:256], in_=MA[h].t[:, 0:128]), R=[MA[h]], W=[XYb[h][0]])
                kb.op(P, lambda e, h=h: e.tensor_tensor(out=Zt[h].t[:], in0=ident, in1=MA[h].t[:, 0:128], op=ALU.subtract), R=[cst, MA[h]], W=[Zt[h]])
            if GST < 4:
                continue
            for lv in range(5):
                for h in H4:
                    cur = XYb[h][lv % 2]
                    nxtb = XYb[h][(lv + 1) % 2]
                    kb.op(PE, lambda e, h=h, cur=cur: e.matmul(pq(h, 0, 1), lhsT=cur.t[:, 128:256], rhs=cur.t[:, 0:128], start=True, stop=True), R=[cur], W=[B(h, 0)])
                    if lv < 4:
                        kb.op(PE, lambda e, h=h, cur=cur: e.matmul(pq(h, 1, 2), lhsT=cur.t[:, 0:128], rhs=cur.t[:, 128:256], start=True, stop=True), R=[cur], W=[B(h, 1)])
                        kb.op(A, lambda e, h=h, nxtb=nxtb: e.copy(out=nxtb.t[:], in_=pq(h, 0, 2)), R=[B(h, 0), B(h, 1)], W=[nxtb])
                    else:
                        kb.op(A, lambda e, h=h, nxtb=nxtb: e.copy(out=nxtb.t[:, 0:128], in_=pq(h, 0, 1)), R=[B(h, 0)], W=[nxtb])
                    kb.op(A, lambda e, h=h: e.copy(out=Zb[h].t[:], in_=Zt[h].t[:]), R=[Zt[h]], W=[Zb[h]])
                for h in H4:
                    zq = 2 + (lv % 2)
                    nxtb = XYb[h][(lv + 1) % 2]
                    kb.op(PE, lambda e, h=h, nxtb=nxtb, zq=zq: e.matmul(pq(h, zq, zq + 1), lhsT=nxtb.t[:, 0:128], rhs=Zb[h].t[:], start=True, stop=True), R=[nxtb, Zb[h]], W=[B(h, zq)])
                    kb.op(V, lambda e, h=h, zq=zq: e.tensor_tensor(out=Zt[h].t[:], in0=Zt[h].t[:], in1=pq(h, zq, zq + 1), op=ALU.add), R=[B(h, zq), Zt[h]], W=[Zt[h]])
            while out_defer:
                out_defer.pop(0)()
            for h in H4:
                kT = kq.t[:, 4 + h, :]; vT = kq.t[:, 8 + h, :]
                kb.op(PE, lambda e, h=h, kT=kT: e.transpose(out=psb.t[:, h * 256:h * 256 + 128], in_=kT, identity=identb.t[:]), R=[kq, identb], W=[psb])
                kb.op(PE, lambda e, h=h, vT=vT: e.transpose(out=psb.t[:, h * 256 + 128:h * 256 + 256], in_=vT, identity=identb.t[:]), R=[kq, identb], W=[psb])
            for h in H4:
                kb.op(A, lambda e, h=h: e.activation(out=rk[h].t[:, 0:128], in_=psb.t[:, h * 256:h * 256 + 128], func=AF.Copy, scale=col("bg", h)), R=[psb, sc["bg"]], W=[rk[h]])
                kb.op(A, lambda e, h=h: e.activation(out=rk[h].t[:, 128:256], in_=psb.t[:, h * 256 + 128:h * 256 + 256], func=AF.Copy, scale=col("beta", h)), R=[psb, sc["beta"]], W=[rk[h]])
                kb.op(A, lambda e, h=h: e.activation(out=kdec[h].t[:], in_=psb.t[:, h * 256:h * 256 + 128], func=AF.Copy, scale=col("kd", h)), R=[psb, sc["kd"]], W=[kdec[h]])
                kb.op(V, lambda e, h=h: e.tensor_tensor(out=qd[h].t[:], in0=kq.t[:, h, :], in1=egb[h].t[:], op=ALU.mult), R=[kq, egb[h]], W=[qd[h]])
            if GST < 6:
                continue
            for h in H4:
                kb.op(A, lambda e, h=h: e.copy(out=TTb[h].t[:], in_=Zt[h].t[:]), R=[Zt[h]], W=[TTb[h]])
                kb.op(PE, lambda e, h=h: e.matmul(pq(h, 0, 2), lhsT=TTb[h].t[:], rhs=rk[h].t[:], start=True, stop=True), R=[TTb[h], rk[h]], W=[B(h, 0), B(h, 1)])
                kb.op(A, lambda e, h=h: e.copy(out=WU[h].t[:], in_=pq(h, 0, 2)), R=[B(h, 0), B(h, 1)], W=[WU[h]])
            if GST < 7:
                continue
            for h in H4:
                aap, abf = cpd()
                kb.op(PE, lambda e, h=h, aap=aap: e.matmul(aap, lhsT=WU[h].t[:, 0:128], rhs=ATb[h].t[:], start=True, stop=True), R=[WU[h], ATb[h]], W=[abf])
                kb.op(V, lambda e, h=h, aap=aap: e.tensor_tensor(out=QtT[h].t[:], in0=qd[h].t[:], in1=aap, op=ALU.subtract), R=[qd[h], abf], W=[QtT[h]])
                for c in range(int(_os.environ.get('GS7', '2'))):
                    r = slice(64 * c, 64 * c + 64)
                    kb.op(PE, lambda e, h=h, c=c, r=r: e.matmul(pq(h, c, c + 1), lhsT=WU[h].t[r, 0:128], rhs=kdec[h].t[r, :], start=True, stop=True), R=[WU[h], kdec[h]], W=[B(h, c)])
                    kb.op(A, lambda e, h=h, c=c: e.activation(out=NPh[h][c].t[:], in_=pq(h, c, c + 1), func=AF.Copy, scale=-1.0), R=[B(h, c)], W=[NPh[h][c]])
            if GST < 8:
                continue
            for c in range(2):
                r = slice(64 * c, 64 * c + 64)
                for h in H4:
                    Sc = Sst[h][c]; Sn = Sst[h][1 - c]; Scb = Sbf[h][c]; Snb = Sbf[h][1 - c]
                    oap, ob = cps()
                    kb.op(PE, lambda e, h=h, oap=oap, Scb=Scb: e.matmul(oap, lhsT=QtT[h].t[:], rhs=Scb.t[:], start=True, stop=False), R=[QtT[h], Scb], W=[ob])
                    kb.op(PE, lambda e, h=h, oap=oap, r=r: e.matmul(oap, lhsT=ATb[h].t[r, :], rhs=WU[h].t[r, 128:256], start=False, stop=True), R=[ATb[h], WU[h]], W=[ob])
                    kb.op(A, lambda e, h=h, oap=oap, r=r: e.copy(out=o_.t[r, h * 128:(h + 1) * 128], in_=oap[r, :]), R=[ob], W=[o_])
                    sap, sbf = cpd()
                    kb.op(PE, lambda e, h=h, sap=sap, r=r: e.matmul(sap, lhsT=kdec[h].t[r, :], rhs=WU[h].t[r, 128:256], start=True, stop=False), R=[kdec[h], WU[h]], W=[sbf])
                    kb.op(PE, lambda e, h=h, c=c, sap=sap, Scb=Scb: e.matmul(sap, lhsT=NPh[h][c].t[:], rhs=Scb.t[:], start=False, stop=True), R=[NPh[h][c], Scb], W=[sbf])
                    egl = sc["EGL%d" % c].t[:, t * 4 + h:t * 4 + h + 1]
                    kb.op(V, lambda e, sap=sap, Sc=Sc, Snb=Snb, egl=egl: e.scalar_tensor_tensor(out=Snb.t[:], in0=Sc.t[:], scalar=egl, in1=sap, op0=ALU.mult, op1=ALU.add), R=[Sc, sbf, sc["EGL%d" % c]], W=[Snb])
                    kb.op(V, lambda e, sap=sap, Sc=Sc, Sn=Sn, egl=egl: e.scalar_tensor_tensor(out=Sn.t[:], in0=Sc.t[:], scalar=egl, in1=sap, op0=ALU.mult, op1=ALU.add), R=[Sc, sbf, sc["EGL%d" % c]], W=[Sn])
            def out_stage(t=t, o_=o_, gz_=gz_):
                ms = mixst[t % 2]
                kb.op(A, lambda e: e.activation(out=osq.t[:], in_=o_.t[:], func=AF.Square), R=[o_], W=[osq])
                kb.op(V, lambda e: e.tensor_reduce(out=oss.t[:, 0:4], in_=osq.t[:].rearrange("p (h d) -> p h d", d=128), axis=mybir.AxisListType.X, op=ALU.add), R=[osq], W=[oss])
                kb.op(A, lambda e: e.activation(out=oss.t[:, 0:4], in_=oss.t[:, 0:4], func=AF.Ln, bias=epsc, scale=1.0 / 128), R=[oss, cst], W=[oss])
                kb.op(A, lambda e: e.activation(out=oss.t[:, 4:8], in_=oss.t[:, 0:4], func=AF.Exp, scale=-0.5), R=[oss], W=[oss])
                kb.op(A, lambda e: e.activation(out=gsg.t[:], in_=gz_.t[:], func=AF.Exp, scale=-1.0), R=[gz_], W=[gsg])
                kb.op(A, lambda e: e.activation(out=gsg.t[:], in_=gsg.t[:], func=AF.Ln, bias=ones_f[:, 0:1], scale=1.0), R=[gsg, cst], W=[gsg])
                kb.op(A, lambda e: e.activation(out=gsg.t[:], in_=gsg.t[:], func=AF.Exp, scale=-1.0), R=[gsg], W=[gsg])
                kb.op(P, lambda e: e.tensor_tensor(out=gsg.t[:], in0=gsg.t[:], in1=gz_.t[:], op=ALU.mult), R=[gsg, gz_], W=[gsg])
                kb.op(P, lambda e: e.tensor_tensor(out=gsg.t[:], in0=gsg.t[:], in1=gnw.t[:], op=ALU.mult), R=[gsg, gnw], W=[gsg])
                for h in H4:
                    kb.op(V, lambda e, h=h: e.scalar_tensor_tensor(out=ms.t[:, h * 128:(h + 1) * 128], in0=o_.t[:, h * 128:(h + 1) * 128], scalar=oss.t[:, 4 + h:5 + h], in1=gsg.t[:, h * 128:(h + 1) * 128], op0=ALU.mult, op1=ALU.mult), R=[o_, oss, gsg], W=[ms])
                kb.dma("pool", MIXG[t * 128:(t + 1) * 128, :], ms.t[:], R=[ms], sem=mxsem[t % 2])

            out_defer.append(out_stage)
        while out_defer:
            out_defer.pop(0)()
        kb.barrier()
        stk[0].close()
        stk[0] = None
        if "MIXG" in dbg_out:
            d = kb.dsem(); out_sems.append(d)
            for c in range(16):
                kb.dma("sp", dbg_out["MIXG"][c * 512:(c + 1) * 512, :], MIXG[c * 512:(c + 1) * 512, :], sem=d)


    if PH >= 3:
        stk[0] = ExitStack()
        EXP = AF.Exp
        ksT = [sb(f"ksT{g}", [128, S], BF16) for g in range(2)]
        Vs = [sb(f"Vs{g}", [128, 64, 65], BF16) for g in range(2)]
        tS = sb("tS", [128, 2, 9, 512])
        tWx = sb("tWx", [128, 2, 2, 512])
        tCf = sb("tCf", [128, 2, 512])
        Wo = sb("Wo", [128, 8, D], BF16)
        ovb = sb("ovb", [128, 4, 128], BF16)
        adjc = sb("adjc", [128, 256])
        j0c = sb("j0c", [128, 32])
        pwc = sb("pwc", [128, 2])
        ecx = sb("ecx", [128, 8])
        fnw = sb("fnw", [128, D])
        kcT = [sb(f"kcT{g}", [64, 512], BF16) for g in range(2)]
        vca = [sb(f"vca{g}", [128, 4, 65], BF16) for g in range(2)]
        l3 = kb.dsem("l3")
        mainstk = stk[0]
        stk[0] = ExitStack()
        stgA = sb("stgA", [128, 2048])
        for g in range(2):
            kb.dma("sp", ksT[g].t[0:64, :], KVT[2, 64 * g:64 * g + 64, :], R=[scr_b["KVT"]], W=[ksT[g]], sem=l3)
            for m in range(9):
                kb.dma("act", tS.t[:, g, m, :], tabS[g, m], W=[tS], sem=l3)
            for m in range(2):
                kb.dma("act", tWx.t[:, m, g, :], tabWx[m, g], W=[tWx], sem=l3)
            kb.dma("act", tCf.t[:, g, :], tabCfar[g], W=[tCf], sem=l3)
        for nm_, dst, src in (("adj", adjc, adjT), ("j0", j0c, j0b), ("pw", pwc, pw), ("ec", ecx, cexp), ("fnw", fnw, fnw_b)):
            kb.dma("sp", dst.t[:], src, W=[dst], sem=l3)
        kb.seal(l3, ksT + [tS, tWx, tCf, adjc, j0c, pwc, ecx, fnw])
        kb.op(A, lambda e: e.activation(out=ecx.t[:], in_=ecx.t[:], func=EXP), R=[ecx], W=[ecx])
        sgs = kb.dsem("sgs")
        for c in range(4):
            kb.dma("sp", stgA.t[64:128, :], estk[:, c * 2048:(c + 1) * 2048], W=[stgA], sem=sgs)
            for g in range(2):
                kb.op(V, lambda e, c=c, g=g: e.tensor_copy(out=ksT[g].t[64:128, c * 2048:(c + 1) * 2048], in_=stgA.t[64:128, :]), R=[stgA], W=[ksT[g]])
        stgB = sb("stgB", [128, 2048])
        sgs2 = kb.dsem("sgs2")
        TMv = TM[:, 1024:1152].rearrange("(k p) c -> p k c", p=128)
        for c in range(4):
            stg_, sem_, q_ = (stgA, sgs, "sp") if c % 2 == 0 else (stgB, sgs2, "act")
            kb.dma(q_, stg_.t[:], TMv[:, c * 16:(c + 1) * 16, :], R=[scr_b["TM"]], W=[stg_], sem=sem_)
            sv = stg_.t[:].rearrange("p (k c) -> p k c", c=128)
            for g in range(2):
                kb.op(V if g == 0 else A, lambda e, c=c, g=g, sv=sv: (e.tensor_copy(out=Vs[g].t[:, c * 16:(c + 1) * 16, 0:64], in_=sv[:, :, 64 * g:64 * g + 64]) if g == 0 else e.copy(out=Vs[g].t[:, c * 16:(c + 1) * 16, 0:64], in_=sv[:, :, 64 * g:64 * g + 64])), R=[stg_], W=[Vs[g]])
        for g in range(2):
            kb.op(P, lambda e, g=g: e.memset(Vs[g].t[:, :, 64:65], 1.0), W=[Vs[g]])
        wo_v = w_out.rearrange("(c p) n -> p c n", p=128)
        for c in range(8):
            kb.dma("sp", stgA.t[:, 0:1024], wo_v[:, c, :], W=[stgA], sem=sgs)
            kb.op(V, lambda e, c=c: e.tensor_copy(out=Wo.t[:, c, :], in_=stgA.t[:, 0:1024]), R=[stgA], W=[Wo])
        kb.dma("sp", stgA.t[:, 0:512], ovl.rearrange("p a b -> p (a b)"), W=[stgA], sem=sgs)
        kb.op(V, lambda e: e.tensor_copy(out=ovb.t[:].rearrange("p a b -> p (a b)"), in_=stgA.t[:, 0:512]), R=[stgA], W=[ovb])
        rawT = sb("rawT", [64, S], BF16)
        w1s = sb("w1s", [64, 4096])
        w1b = sb("w1b", [64, 32, 128], BF16)
        peb_ = sb("peb_", [64, 32])
        pebb = sb("pebb", [64, 32], BF16)
        w2s = sb("w2s", [128, 64])
        w2b = sb("w2b", [128, 64], BF16)
        pcol = sb("pcol", [128, 1])
        hb = sb("hb", [128, 512], BF16)
        kb.op(P, lambda e: e.memset(hb.t[:], 0.0), W=[hb])
        for kind, (w1d, ped, w2d) in enumerate(((w1k, pek, w2k), (w1v, pev, w2v))):
            kb.dma("sp", w1s.t[:], w1d.rearrange("p a b -> p (a b)"), W=[w1s], sem=sgs)
            kb.dma("sp", peb_.t[:], ped, W=[peb_], sem=sgs)
            kb.dma("sp", w2s.t[:], w2d, W=[w2s], sem=sgs)
            kb.seal(sgs, [w1s, peb_, w2s])
            kb.op(V, lambda e: e.tensor_copy(out=w1b.t[:].rearrange("p a b -> p (a b)"), in_=w1s.t[:]), R=[w1s], W=[w1b])
            kb.op(V, lambda e: e.tensor_copy(out=pebb.t[:], in_=peb_.t[:]), R=[peb_], W=[pebb])
            kb.op(V, lambda e: e.tensor_copy(out=w2b.t[:], in_=w2s.t[:]), R=[w2s], W=[w2b])
            for g in range(2):
                kb.dma("sp", rawT.t[:], KVT[kind, 64 * g:64 * g + 64, :], R=[scr_b["KVT"]], W=[rawT], sem=sgs)
                rv = rawT.t[:].rearrange("p (n r) -> p n r", r=16)
                ph_ = psum[0]; pp_ = psum[1]
                for l in range(32):
                    a_, r_ = l // 16, l % 16
                    kb.op(PE, lambda e, l=l, a_=a_, r_=r_: e.matmul(ph_.t[:, 0:511], lhsT=w1b.t[:, l, :], rhs=rv[:, a_:a_ + 511, r_], start=(l == 0), stop=(l == 31)), R=[w1b, rawT], W=[ph_])
                for l in range(32):
                    kb.op(PE, lambda e, l=l: e.matmul(pp_.t[:, 0:1], lhsT=w1b.t[:, l, :], rhs=pebb.t[:, l:l + 1], start=(l == 0), stop=(l == 31)), R=[w1b, pebb], W=[pp_])
                kb.op(V, lambda e: e.tensor_copy(out=pcol.t[:], in_=pp_.t[:, 0:1]), R=[pp_], W=[pcol])
                kb.op(A, lambda e: e.activation(out=hb.t[:, 0:511], in_=ph_.t[:, 0:511], func=AF.Silu, bias=pcol.t[:, 0:1], scale=1.0), R=[ph_, pcol], W=[hb])
                if kind == 0:
                    pk_ = psum[2]
                    kb.op(PE, lambda e: e.matmul(pk_.t[0:64, :], lhsT=w2b.t[:], rhs=hb.t[:], start=True, stop=True), R=[w2b, hb], W=[pk_])
                    kb.op(V, lambda e, g=g: e.tensor_copy(out=kcT[g].t[:], in_=pk_.t[0:64, :]), R=[pk_], W=[kcT[g]])
                else:
                    pk_ = psum[2]
                    for ct in range(4):
                        kb.op(PE, lambda e, ct=ct: e.matmul(pk_.t[:, ct * 64:(ct + 1) * 64], lhsT=hb.t[:, ct * 128:(ct + 1) * 128], rhs=w2b.t[:], start=True, stop=True), R=[w2b, hb], W=[pk_])
                    kb.op(V, lambda e, g=g: e.tensor_copy(out=vca[g].t[:, :, 0:64], in_=pk_.t[:, 0:256].rearrange("p (a b) -> p a b", b=64)), R=[pk_], W=[vca[g]])
                    kb.op(V, lambda e, g=g: e.tensor_copy(out=vca[g].t[:, :, 64:65], in_=nvalid.rearrange("p (a b) -> p a b", b=1)), R=[cst], W=[vca[g]])
        if "kcT" in dbg_out:
            for g in range(2):
                dbg_dump("kcT", None, None)
        kb.barrier()
        stk[0].close()
        stk[0] = mainstk
        NB2 = 1
        qp = [sb(f"qp{i}", [64, 8, 256], BF16) for i in range(NB2)]
        qT_ = [sb(f"qT{i}", [64, 8, 128], BF16) for i in range(NB2)]
        qtmp = sb("qtmp", [64, 8, 128], BF16)
        tm2 = [sb(f"tm2{i}", [128, 2, 536]) for i in range(NB2)]
        tmq = sb("tmq", [128, 536])
        mg2 = [sb(f"mg2{i}", [128, 2, 512], BF16) for i in range(NB2)]
        xrt = [sb(f"xrt{i}", [128, D]) for i in range(NB2)]
        kwt = [sb(f"kwt{i}", [64, 2, 768], BF16) for i in range(NB2)]
        vwst = [sb(f"vwst{i}", [128, 6, 128]) for i in range(NB2)]
        Vw = sb("Vw", [128, 6, 2, 65], BF16)
        kb.op(P, lambda e: e.memset(Vw.t[:, :, :, 64:65], 1.0), W=[Vw])
        slsem = [[kb.dsem(f"sl{k}_{i}") for i in range(NB2)] for k in range(6)]
        tcb = [sb(f"tcb{i}", [128, 512]) for i in range(2)]
        tcsem = [kb.dsem(f"tc{i}") for i in range(2)]
        sbt = [sb(f"sbt{i}", [128, 512]) for i in range(3)]
        Pt = [sb(f"Pt{i}", [128, 512], BF16) for i in range(5)]
        QS = [[sb(f"QS{g}_{i}", [128, 512], BF16) for i in range(2)] for g in range(2)]
        selm2 = [sb(f"selm2{g}", [128, 256]) for g in range(2)]
        gate = sb("gate", [128, 24])
        ocs = [sb(f"ocs{g}", [128, 260]) for g in range(2)]; cmb = sb("cmb", [128, 260]); ows = [sb(f"ows{g}", [128, 260]) for g in range(2)]
        den3 = [sb(f"den3{g}", [128, 12]) for g in range(2)]; cf3 = [sb(f"cf3{g}", [128, 12]) for g in range(2)]
        score = [sb(f"score{g}", [128, 128]) for g in range(2)]; work = sb("work", [128, 128])
        m8a = sb("m8a", [128, 8]); m8b = sb("m8b", [128, 8])
        o_n = sb("o_n", [128, 512]); sgn = sb("sgn", [128, 512])
        mix = sb("mix", [128, D], BF16); mixT = sb("mixT", [128, 8, 128], BF16)
        hbuf = sb("hbuf", [128, D]); hss = sb("hss", [128, 2])
        obuf = [sb("obuf0", [128, D])] * 2
        osem = [kb.dsem(f"os{i}") for i in range(2)]
        out_sems.extend(osem)
        NQv = NQ.rearrange("h d t -> d h t")
        print("sbuf remaining", nc.sbuf_bytes_remaining)
        nS = [0]; nP = [0]; nC = [0]; nTC = [0]
        SB_ = [psum[0], psum[1], psum[2], psum[6]]
        ACCA, ACCB, ACCC, MISC = psum[3], psum[4], psum[5], psum[6]
        SL = range(32) if NSL is None else NSL
        for i in SL:
            bi = i % NB2
            t0 = 2 * i * 128
            kb.dma("sp", qp[bi].t[:], NQv[:, :, t0:t0 + 256], R=[scr_b["NQ"]], W=[qp[bi]], sem=slsem[0][bi])
            kb.dma("sp", tm2[bi].t[:, :, 0:512], TM[t0:t0 + 256, 512:1024].rearrange("(a p) c -> p a c", p=128), R=[scr_b["TM"]], W=[tm2[bi]], sem=slsem[1][bi])
            kb.dma("sp", tm2[bi].t[:, :, 512:536], TM[t0:t0 + 256, 1280:1304].rearrange("(a p) c -> p a c", p=128), R=[scr_b["TM"]], W=[tm2[bi]], sem=slsem[1][bi])
            kb.dma("act", mg2[bi].t[:], MIXG[t0:t0 + 256, :].rearrange("(a p) c -> p a c", p=128), R=[scr_b["MIXG"]], W=[mg2[bi]], sem=slsem[2][bi])
            kb.dma("act", xrt[bi].t[:], xr[i * 128:(i + 1) * 128, :], W=[xrt[bi]], sem=slsem[3][bi])
            kt_lo = max(0, 2 * i - 4); nkw = 2 * i + 2 - kt_lo
            for g in range(2):
                kb.dma("sp", kwt[bi].t[:, g, 0:nkw * 128], KVT[3, 64 * g:64 * g + 64, kt_lo * 128:(2 * i + 2) * 128], R=[scr_b["KVT"]], W=[kwt[bi]], sem=slsem[4][bi])
            kb.dma("act", vwst[bi].t[:, 0:nkw, :], TM[kt_lo * 128:(2 * i + 2) * 128, 1152:1280].rearrange("(k p) c -> p k c", p=128), R=[scr_b["TM"]], W=[vwst[bi]], sem=slsem[5][bi])
            w0 = pwc.t[:, 0:1]; w1_ = pwc.t[:, 1:2]
            kb.op(V, lambda e: e.tensor_scalar(out=qtmp.t[:], in0=qp[bi].t[:, :, 0:128], scalar1=w0[0:64, :], scalar2=None, op0=ALU.mult), R=[qp[bi], pwc], W=[qtmp])
            kb.op(V, lambda e: e.scalar_tensor_tensor(out=qT_[bi].t[:], in0=qp[bi].t[:, :, 128:256], scalar=w1_[0:64, :], in1=qtmp.t[:], op0=ALU.mult, op1=ALU.add), R=[qp[bi], pwc, qtmp], W=[qT_[bi]])
            kb.op(V, lambda e: e.tensor_scalar(out=tmq.t[:], in0=tm2[bi].t[:, 0, :], scalar1=w0, scalar2=None, op0=ALU.mult), R=[tm2[bi], pwc], W=[tmq])
            kb.op(V, lambda e: e.scalar_tensor_tensor(out=tmq.t[:], in0=tm2[bi].t[:, 1, :], scalar=w1_, in1=tmq.t[:], op0=ALU.mult, op1=ALU.add), R=[tm2[bi], pwc, tmq], W=[tmq])
            kb.op(V, lambda e: e.tensor_scalar(out=mix.t[:, 0:512], in0=mg2[bi].t[:, 0, :], scalar1=w0, scalar2=None, op0=ALU.mult), R=[mg2[bi], pwc], W=[mix])
            kb.op(V, lambda e: e.scalar_tensor_tensor(out=mix.t[:, 0:512], in0=mg2[bi].t[:, 1, :], scalar=w1_, in1=mix.t[:, 0:512], op0=ALU.mult, op1=ALU.add), R=[mg2[bi], pwc, mix], W=[mix])
            for g in range(2):
                kb.op(P, lambda e, g=g: e.tensor_copy(out=Vw.t[:, 0:nkw, g, 0:64], in_=vwst[bi].t[:, 0:nkw, 64 * g:64 * g + 64]), R=[vwst[bi]], W=[Vw])
            kb.op(A, lambda e: e.activation(out=gate.t[:], in_=tmq.t[:, 512:536], func=EXP, scale=-1.0), R=[tmq], W=[gate])
            kb.op(V, lambda e: e.tensor_scalar(out=gate.t[:], in0=gate.t[:], scalar1=1.0, scalar2=None, op0=ALU.add), R=[gate], W=[gate])
            kb.op(V, lambda e: e.reciprocal(out=gate.t[:], in_=gate.t[:]), R=[gate], W=[gate])
            qT = qT_[bi]

            def stageA(j, g):
                ps_ = SB_[nS[0] % 4]; nS[0] += 1
                if j.get("qs") is not None:
                    kb.op(PE, lambda e: e.matmul(ps_.t[:], lhsT=j["lhsT"], rhs=j["qs"].t[:], start=True, stop=True), R=[j["qs"]] + j["lt"], W=[ps_])
                else:
                    rhs = qT.t[:, 4 * g:4 * g + 4, :].rearrange("p a b -> p (a b)")
                    kb.op(PE, lambda e: e.matmul(ps_.t[:], lhsT=j["lhsT"], rhs=rhs, start=True, stop=True), R=[qT] + j["lt"], W=[ps_])
                pt = Pt[nP[0] % 5]; nP[0] += 1
                if j.get("tabdma") is not None:
                    tb_ = tcb[nTC[0] % 2]; ts_ = tcsem[nTC[0] % 2]; nTC[0] += 1
                    kb.dma("sp", tb_.t[:], j["tabdma"], W=[tb_], sem=ts_)
                    j["tab"], j["tt"] = tb_.t[:], [tb_]
                if j.get("tab") is not None:
                    st_ = sbt[nC[0] % 3]; nC[0] += 1
                    kb.op(V, lambda e: e.tensor_tensor(out=st_.t[:], in0=ps_.t[:], in1=j["tab"], op=ALU.add), R=[ps_] + j["tt"], W=[st_])
                    kb.op(A, lambda e: e.activation(out=pt.t[:], in_=st_.t[:], func=EXP), R=[st_], W=[pt])
                else:
                    kb.op(A, lambda e: e.activation(out=pt.t[:], in_=ps_.t[:], func=EXP), R=[ps_], W=[pt])
                j["pt"] = pt

            def stageB(j):
                pt = j["pt"]
                for (acc, v_ap, v_tiles, first, last, width, stride) in j["pv"]:
                    for h in range(4):
                        kb.op(PE, lambda e, h=h: e.matmul(acc.t[:, h * stride:h * stride + width], lhsT=pt.t[:, h * 128:(h + 1) * 128], rhs=v_ap, start=(first and h == 0), stop=(last and h == 3)), R=[pt] + v_tiles, W=[acc])

            def run_jobs(jobs, g, LA=3):
                for idx in range(len(jobs) + LA):
                    if idx < len(jobs):
                        stageA(jobs[idx], g)
                    if idx - LA >= 0:
                        stageB(jobs[idx - LA])

            def front(g):
                    nkt = 2 * i + 2
                    cts = [ct for ct in range(4) if 2 * i + 1 - 16 * ct >= 0]
                    jobs = []
                    for ci, ct in enumerate(cts):
                        j = dict(lhsT=kcT[g].t[:, ct * 128:(ct + 1) * 128], lt=[kcT[g]])
                        if 2 * i - 16 * ct >= 23:
                            j["tab"], j["tt"] = tCf.t[:, g, :], [tCf]
                        else:
                            j["tabdma"] = tabC[g, CIDX[(i, ct)]]
                        f_, l_ = ci == 0, ci == len(cts) - 1
                        j["pv"] = [(ACCA, vca[g].t[:, ct, :], [vca[g]], f_, l_, 65, 65), (ACCB, ovb.t[:, ct, :], [ovb], f_, l_, 128, 128)]
                        jobs.append(j)
                    run_jobs(jobs, g)
                    wk = [kt for kt in range(kt_lo, nkt)]
                    jobs = []
                    for ki, kt in enumerate(wk):
                        m = 2 * i + 1 - kt
                        sl_ = kt - kt_lo
                        tab = tS.t[:, g, m, :] if m <= 3 else tWx.t[:, m - 4, g, :]
                        jobs.append(dict(lhsT=kwt[bi].t[:, g, sl_ * 128:(sl_ + 1) * 128], lt=[kwt[bi]], tab=tab, tt=[tS, tWx],
                                         pv=[(ACCC, Vw.t[:, sl_, g, :], [Vw], ki == 0, ki == len(wk) - 1, 65, 65)]))
                    run_jobs(jobs, g)
                    kb.op(V, lambda e: e.tensor_copy(out=ows[g].t[:], in_=ACCC.t[:, 0:260]), R=[ACCC], W=[ows[g]])
                    kb.op(V, lambda e: e.tensor_copy(out=ocs[g].t[:], in_=ACCA.t[:, 0:260]), R=[ACCA], W=[ocs[g]])
                    o3 = ocs[g].t[:].rearrange("p (h c) -> p h c", c=65)
                    kb.op(V, lambda e: e.tensor_scalar(out=den3[g].t[:, 0:4], in0=o3[:, :, 64], scalar1=1e-30, scalar2=None, op0=ALU.max), R=[ocs[g]], W=[den3[g]])
                    kb.op(V, lambda e: e.reciprocal(out=cf3[g].t[:, 0:4], in_=den3[g].t[:, 0:4]), R=[den3[g]], W=[cf3[g]])
                    kb.op(V, lambda e: e.tensor_scalar(out=score[g].t[:], in0=ACCB.t[:, 0:128], scalar1=cf3[g].t[:, 0:1], scalar2=None, op0=ALU.mult), R=[ACCB, cf3[g]], W=[score[g]])
                    for h in range(1, 4):
                        kb.op(V, lambda e, h=h: e.scalar_tensor_tensor(out=score[g].t[:], in0=ACCB.t[:, h * 128:(h + 1) * 128], scalar=cf3[g].t[:, h:h + 1], in1=score[g].t[:], op0=ALU.mult, op1=ALU.add), R=[ACCB, cf3[g], score[g]], W=[score[g]])
                    a0 = 126 - 4 * i
                    kb.op(V, lambda e: e.tensor_tensor(out=score[g].t[:], in0=score[g].t[:], in1=adjc.t[:, a0:a0 + 128], op=ALU.add), R=[score[g], adjc], W=[score[g]])
                    kb.op(V, lambda e: e.tensor_tensor(out=score[g].t[:, 0:1], in0=score[g].t[:, 0:1], in1=j0c.t[:, i:i + 1], op=ALU.add), R=[score[g], j0c], W=[score[g]])
                    kb.op(V, lambda e: e.max(out=m8a.t[:], in_=score[g].t[:]), R=[score[g]], W=[m8a])
                    kb.op(V, lambda e: e.match_replace(out=work.t[:], in_to_replace=m8a.t[:], in_values=score[g].t[:], imm_value=-3.0e38), R=[score[g], m8a], W=[work])
                    kb.op(V, lambda e: e.max(out=m8b.t[:], in_=work.t[:]), R=[work], W=[m8b])
                    for hf in range(2):
                        kb.op(V, lambda e, hf=hf: e.tensor_scalar(out=selm2[g].t[:, hf * 128:(hf + 1) * 128], in0=score[g].t[:], scalar1=m8b.t[:, 7:8], scalar2=None, op0=ALU.is_ge), R=[score[g], m8b], W=[selm2[g]])
                    ngb = 2 if nkt > 32 else 1
                    kb.op(PE, lambda e: e.transpose(out=MISC.t[:, 0:128], in_=selm2[g].t[:, 64:192], identity=ident), R=[selm2[g], cst], W=[MISC])
                    if ngb == 2:
                        kb.op(PE, lambda e: e.transpose(out=MISC.t[:, 128:256], in_=selm2[g].t[:, 0:128], identity=ident), R=[selm2[g], cst], W=[MISC])
                    for gb in range(ngb):
                        kb.op(P, lambda e, gb=gb: e.tensor_copy(out=QS[g][gb].t[0:64, :], in_=qT.t[:, 4 * g:4 * g + 4, :].rearrange("p a b -> p (a b)")), R=[qT], W=[QS[g][gb]])
                        for h in range(4):
                            kb.op(V, lambda e, h=h, gb=gb: e.tensor_scalar(out=QS[g][gb].t[64:128, h * 128:(h + 1) * 128], in0=MISC.t[64:128, gb * 128:(gb + 1) * 128], scalar1=-1.0, scalar2=-MNEG, op0=ALU.add, op1=ALU.mult), R=[MISC], W=[QS[g][gb]])
                    if g == 0 and ("selm" in dbg_out) and i == DSLOT:
                        dbg_dump("selm", selm2[g].t[:, 0:128], selm2[g])
                        dbg_dump("score", score[g].t[:], score[g])
                        dbg_dump("ocs", ocs[g].t[:], ocs[g])
                        dbg_dump("cf3", cf3[g].t[:], cf3[g])

            def back(g):
                    nkt = 2 * i + 2
                    o3 = ocs[g].t[:].rearrange("p (h c) -> p h c", c=65)
                    far = [kt for kt in range(nkt) if 2 * i + 1 - kt >= 9]
                    near = [kt for kt in range(nkt) if 2 * i + 1 - kt < 9]
                    jobs = []
                    for ki, kt in enumerate(far):
                        jobs.append(dict(lhsT=ksT[g].t[:, kt * 128:(kt + 1) * 128], lt=[ksT[g]], qs=QS[g][kt // 32],
                                         pv=[(ACCB, Vs[g].t[:, kt, :], [Vs[g]], ki == 0, ki == len(far) - 1, 65, 65)]))
                    for ki, kt in enumerate(near):
                        m = 2 * i + 1 - kt
                        jobs.append(dict(lhsT=ksT[g].t[:, kt * 128:(kt + 1) * 128], lt=[ksT[g]], qs=QS[g][kt // 32], tab=tS.t[:, g, m, :], tt=[tS],
                                         pv=[(ACCA, Vs[g].t[:, kt, :], [Vs[g]], ki == 0, ki == len(near) - 1, 65, 65)]))
                    run_jobs(jobs, g)
                    kb.op(V, lambda e: e.tensor_copy(out=cmb.t[:], in_=ACCA.t[:, 0:260]), R=[ACCA], W=[cmb])
                    if far:
                        for h in range(4):
                            kb.op(V, lambda e, h=h, g=g: e.scalar_tensor_tensor(out=cmb.t[:, h * 65:(h + 1) * 65], in0=ACCB.t[:, h * 65:(h + 1) * 65], scalar=ecx.t[:, 4 * g + h:4 * g + h + 1], in1=cmb.t[:, h * 65:(h + 1) * 65], op0=ALU.mult, op1=ALU.add), R=[ACCB, ecx, cmb], W=[cmb])
                    c3 = cmb.t[:].rearrange("p (h c) -> p h c", c=65)
                    w3 = ows[g].t[:].rearrange("p (h c) -> p h c", c=65)
                    kb.op(V, lambda e: e.tensor_scalar(out=den3[g].t[:, 4:8], in0=c3[:, :, 64], scalar1=1e-30, scalar2=None, op0=ALU.max), R=[cmb], W=[den3[g]])
                    kb.op(V, lambda e: e.tensor_scalar(out=den3[g].t[:, 8:12], in0=w3[:, :, 64], scalar1=1e-30, scalar2=None, op0=ALU.max), R=[ows[g]], W=[den3[g]])
                    kb.op(V, lambda e: e.reciprocal(out=cf3[g].t[:], in_=den3[g].t[:]), R=[den3[g]], W=[cf3[g]])
                    gv = gate.t[:].rearrange("p (x h) -> p x h", h=8)[:, :, 4 * g:4 * g + 4]
                    kb.op(V, lambda e, gv=gv: e.tensor_tensor(out=cf3[g].t[:].rearrange("p (x h) -> p x h", h=4), in0=cf3[g].t[:].rearrange("p (x h) -> p x h", h=4), in1=gv, op=ALU.mult), R=[cf3[g], gate], W=[cf3[g]])
                    for h in range(4):
                        oc_ = o_n.t[:, (4 * g + h) * 64:(4 * g + h + 1) * 64]
                        kb.op(V, lambda e, h=h, oc_=oc_: e.tensor_scalar(out=oc_, in0=o3[:, h, 0:64], scalar1=cf3[g].t[:, h:h + 1], scalar2=None, op0=ALU.mult), R=[ocs[g], cf3[g]], W=[o_n])
                        kb.op(V, lambda e, h=h, oc_=oc_: e.scalar_tensor_tensor(out=oc_, in0=c3[:, h, 0:64], scalar=cf3[g].t[:, 4 + h:5 + h], in1=oc_, op0=ALU.mult, op1=ALU.add), R=[cmb, cf3[g], o_n], W=[o_n])
                        kb.op(V, lambda e, h=h, oc_=oc_: e.scalar_tensor_tensor(out=oc_, in0=w3[:, h, 0:64], scalar=cf3[g].t[:, 8 + h:9 + h], in1=oc_, op0=ALU.mult, op1=ALU.add), R=[ows[g], cf3[g], o_n], W=[o_n])

            for g in range(2):
                front(g)
            for g in range(2):
                back(g)
            if ("o_n" in dbg_out) and i == DSLOT:
                dbg_dump("o_n", o_n.t[:], o_n)
            kb.op(A, lambda e: e.activation(out=sgn.t[:], in_=tmq.t[:, 0:512], func=EXP, scale=-1.0), R=[tmq], W=[sgn])
            kb.op(A, lambda e: e.activation(out=sgn.t[:], in_=sgn.t[:], func=AF.Ln, bias=ones_f[:, 0:1], scale=1.0), R=[sgn, cst], W=[sgn])
            kb.op(A, lambda e: e.activation(out=sgn.t[:], in_=sgn.t[:], func=EXP, scale=-1.0), R=[sgn], W=[sgn])
            kb.op(P, lambda e: e.tensor_tensor(out=sgn.t[:], in0=sgn.t[:], in1=tmq.t[:, 0:512], op=ALU.mult), R=[sgn, tmq], W=[sgn])
            kb.op(V, lambda e: e.tensor_tensor(out=mix.t[:, 512:1024], in0=o_n.t[:], in1=sgn.t[:], op=ALU.mult), R=[o_n, sgn], W=[mix])
            for c in range(8):
                kb.op(PE, lambda e, c=c: e.transpose(out=psb.t[:, c * 128:(c + 1) * 128], in_=mix.t[:, c * 128:(c + 1) * 128], identity=identb.t[:]), R=[mix, identb], W=[psb])
            kb.op(A, lambda e: e.copy(out=mixT.t[:].rearrange("p a b -> p (a b)"), in_=psb.t[:]), R=[psb], W=[mixT])
            for half, pY in ((0, MISC), (1, ACCC)):
                for c in range(8):
                    kb.op(PE, lambda e, c=c, half=half, pY=pY: e.matmul(pY.t[:], lhsT=mixT.t[:, c, :], rhs=Wo.t[:, c, half * 512:(half + 1) * 512], start=(c == 0), stop=(c == 7)), R=[mixT, Wo], W=[pY])
                kb.op(V, lambda e, half=half, pY=pY: e.tensor_tensor(out=hbuf.t[:, half * 512:(half + 1) * 512], in0=pY.t[:], in1=xrt[bi].t[:, half * 512:(half + 1) * 512], op=ALU.add), R=[pY, xrt[bi]], W=[hbuf])
            hsq = obuf[i % 2]
            kb.op(A, lambda e: e.activation(out=hsq.t[:], in_=hbuf.t[:], func=AF.Square), R=[hbuf], W=[hsq])
            kb.op(V, lambda e: e.tensor_reduce(out=hss.t[:, 0:1], in_=hsq.t[:], axis=mybir.AxisListType.X, op=ALU.add), R=[hsq], W=[hss])
            kb.op(A, lambda e: e.activation(out=hss.t[:, 0:1], in_=hss.t[:, 0:1], func=AF.Ln, bias=epsc, scale=1.0 / D), R=[hss, cst], W=[hss])
            kb.op(A, lambda e: e.activation(out=hss.t[:, 1:2], in_=hss.t[:, 0:1], func=EXP, scale=-0.5), R=[hss], W=[hss])
            ob = obuf[i % 2]
            kb.op(V, lambda e, ob=ob: e.scalar_tensor_tensor(out=ob.t[:], in0=hbuf.t[:], scalar=hss.t[:, 1:2], in1=fnw.t[:], op0=ALU.mult, op1=ALU.mult), R=[hbuf, hss, fnw], W=[ob])
            kb.dma("pool", y[i * 128:(i + 1) * 128, :], ob.t[:], R=[ob], sem=osem[i % 2])
        kb.barrier()
        stk[0].close()
        stk[0] = None

    kb.final_wait(out_sems)
    print('KB nops', kb.nops, kb.cnt)
    return nc, list(dbg_out.keys())


PERM = None


def _perm():
    gq, gk, gv, gz, gb, ga, nq, kc, vc, ks, vs, kw, vw, ng, nz = [np.arange(a, b) for a, b in zip(
        np.cumsum([0, 512, 512, 512, 512, 4, 4, 512, 128, 128, 128, 128, 128, 128, 24]),
        np.cumsum([512, 512, 512, 512, 4, 4, 512, 128, 128, 128, 128, 128, 128, 24, 512]))]
    return np.concatenate([gq, gk, gv, nq, kc, vc, ks, kw, gz, nz, vs, vw, ng, gb, ga])


def _consts():
    c = np.zeros((128, 1024), np.float32)
    j = np.arange(128)[:, None]; i = np.arange(128)[None, :]
    same = (j // 64) == (i // 64)
    c[:, 0:128] = np.eye(128)
    c[:, 128:256] = (same & (j < i))
    c[:, 256:384] = (same & (j <= i))
    c[:, 384:512] = same
    c[0, 512:640] = 1.0
    c[64, 640:768] = 1.0
    c[:, 768:772] = 1.0
    c[127, 771] = 0.0
    c[:, 772:900] = 1.0
    c[:, 900] = EPS
    return c


def _bucket(dist):
    n = np.maximum(dist, 0)
    large = 16 + (np.log(np.maximum(n, 16).astype(np.float32) / np.float32(16)) / np.float32(np.log(1024.0 / 16.0)) * np.float32(16)).astype(np.int32)
    return np.where(n < 16, n, np.minimum(large, 31)).astype(np.int64)


def _gather_bias(rel_ext, dist, valid, g):
    idx = np.where(valid, _bucket(dist), 32)
    t = rel_ext[idx][:, :, 4 * g:4 * g + 4]
    return np.ascontiguousarray(t.transpose(0, 2, 1).reshape(128, 512))


def prep_shared(inputs):
    sh = {}
    sh["w_in"] = np.ascontiguousarray(np.asarray(inputs["w_in"], np.float32)[0][:, _perm()])
    sh["nw"] = np.ascontiguousarray(np.asarray(inputs["norm_w"], np.float32)[0].reshape(8, 128).T)
    cwv = np.asarray(inputs["conv_w"], np.float32)[0]
    sh["convw"] = np.ascontiguousarray(cwv.reshape(4, 12, 128).transpose(2, 1, 0))
    sh["alog_b"] = np.ascontiguousarray(np.broadcast_to(np.tile(np.asarray(inputs["a_log"], np.float32)[0], 64)[None, :], (128, 256)))
    sh["dtb_b"] = np.ascontiguousarray(np.broadcast_to(np.tile(np.asarray(inputs["dt_bias"], np.float32)[0], 64)[None, :], (128, 256)))
    sh["gnw_b"] = np.ascontiguousarray(np.broadcast_to(np.tile(np.asarray(inputs["gdn_norm_w"], np.float32)[0], 4)[None, :], (128, 512)))
    sh["fnw_b"] = np.ascontiguousarray(np.broadcast_to(np.asarray(inputs["final_norm_w"], np.float32)[None, :], (128, D)))
    sh["w_out"] = np.ascontiguousarray(np.asarray(inputs["w_out"], np.float32)[0])
    for nm, k1, kp, k2 in (("k", "cmp_k_w1", "cmp_pe_k", "cmp_k_w2"), ("v", "cmp_v_w1", "cmp_pe_v", "cmp_v_w2")):
        sh["w1" + nm] = np.ascontiguousarray(np.asarray(inputs[k1], np.float32)[0].transpose(1, 0, 2))
        sh["pe" + nm] = np.ascontiguousarray(np.asarray(inputs[kp], np.float32)[0].T)
        sh["w2" + nm] = np.ascontiguousarray(np.asarray(inputs[k2], np.float32)[0])
    sh["consts"] = _consts()
    rel = np.asarray(inputs["rel_bias"], np.float32)
    rel_ext = np.concatenate([rel, np.full((1, 8), NEG, np.float32)], axis=0)
    sh["cexp"] = np.ascontiguousarray(np.broadcast_to(rel[31][None, :], (128, 8)))
    sh["tabCfar"] = np.ascontiguousarray(np.broadcast_to(np.repeat(rel[31].reshape(2, 4), 128, axis=1)[:, None, :], (2, 128, 512)))
    n_in = np.arange(128)[:, None, None]; ct = np.arange(4)[None, :, None]; j = np.arange(128)[None, None, :]
    n = 128 * ct + n_in
    sh["ovl"] = ((n <= 510) & (16 * n < 64 * j + 64) & (16 * n + 32 > 64 * j)).astype(np.float32)
    kk = np.arange(S)[None, :]
    sh["estk"] = (((kk // 64) % 64) == np.arange(64)[:, None]).astype(np.float32)
    ik = np.arange(128)[:, None]; iq = np.arange(128)[None, :]
    par = []
    for a in range(2):
        p = {}
        tS_ = np.empty((2, 9, 128, 512), np.float32)
        tW_ = np.empty((2, 2, 128, 512), np.float32)
        for g in range(2):
            for m in range(9):
                d = m + a - 1
                dist = 128 * d + iq - ik
                tS_[g, m] = _gather_bias(rel_ext, dist, dist >= 0, g)
            for mi in range(2):
                d = 4 + mi + a - 1
                dist = 128 * d + iq - ik
                tW_[mi, g] = _gather_bias(rel_ext, dist, (dist >= 0) & (dist < 512), g)
        p["tabS"] = tS_; p["tabWx"] = tW_
        tC_ = np.empty((2, len(CIDX), 128, 512), np.float32)
        for (i, ct_), idx in CIDX.items():
            e = 2 * i + a - 16 * ct_
            dist = 128 * e + iq - 16 * ik - 31
            for g in range(2):
                tC_[g, idx] = _gather_bias(rel_ext, dist, dist >= 0, g)
        p["tabC"] = tC_
        x_ = np.arange(256)[None, :]; hh = (np.arange(128) // 64)[:, None]
        relj = (x_ - 126) - 2 * a
        adj = np.where(relj > hh, np.float32(NEG), np.where((relj == hh) | (relj == hh - 1), np.float32(1e4), np.float32(0.0)))
        p["adjT"] = np.ascontiguousarray(adj.astype(np.float32))
        j0 = np.zeros((128, 32), np.float32)
        for i in range(32):
            if 2 * i + a >= 1:
                j0[:, i] = 1e4
        p["j0b"] = j0
        p["pw"] = np.ascontiguousarray(np.broadcast_to(np.array([1.0 - a, float(a)], np.float32)[None, :], (128, 2)))
        par.append(p)
    return sh, par


def prep_inputs(inputs, core, sh=None, par=None):
    if sh is None:
        sh, par = prep_shared(inputs)
    b = core // 2
    a = core % 2
    x = np.asarray(inputs["x"], np.float32)[b]
    m = dict(sh)
    m.update(par[a])
    m["xT"] = np.ascontiguousarray(x.T)
    m["xr"] = np.ascontiguousarray(x.reshape(64, 128, D)[a::2].reshape(S // 2, D))
    return m


_NC = [None]


def kernel(**inputs):
    if _NC[0] is None:
        _NC[0] = build()[0]
    nc = _NC[0]
    sh, par = prep_shared(inputs)
    in_maps = [prep_inputs(inputs, c, sh, par) for c in range(8)]
    res = run_bass_kernel_spmd(nc, in_maps, core_ids=list(range(8)))
    out = np.empty((4, S, D), np.float32)
    for c in range(8):
        b, a = c // 2, c % 2
        yc = np.asarray(res.results[c]["y"], np.float32).reshape(32, 128, D)
        out[b].reshape(64, 128, D)[a::2] = yc
    return out
```

```python
import numpy as np
import concourse.bass as bass
import concourse.mybir as mybir
from concourse.bass_utils import run_bass_kernel_spmd

F32 = mybir.dt.float32
BF16 = mybir.dt.bfloat16
ALU = mybir.AluOpType
AF = mybir.ActivationFunctionType

S = 8192
NT = 64
D = 1024
EPS = 1e-6
NEG = -1e30
MNEG = -30000.0


class Buf:
    __slots__ = ("w", "r", "bank")

    def __init__(self, bank=None):
        self.w = None
        self.r = {}
        self.bank = bank


class Tile:
    def __init__(self, t):
        self.t = t
        self.b = Buf()


class DSem:
    def __init__(self, nc, name):
        self.h = nc.alloc_semaphore(name)
        self.count = 0
        self.key = ("dma", name)


class KB:
    def __init__(self, nc):
        self.nc = nc
        self.eng = {"pe": nc.tensor, "dve": nc.vector, "act": nc.scalar, "pool": nc.gpsimd, "sp": nc.sync}
        self.semh = {}
        self.cnt = {}
        self.seen = {}
        for e in self.eng:
            self.semh[("eng", e)] = nc.alloc_semaphore("c_" + e)
            self.cnt[e] = 0
            self.seen[e] = {}
        self.dsems = []
        self.nd = 0
        self.nops = 0
        self.pe_bank_excl = True
        import os
        self.limit = int(os.environ.get("KBLIMIT", "1000000000"))

    def dsem(self, name=None):
        self.nd += 1
        d = DSem(self.nc, name or f"d{self.nd}")
        self.semh[d.key] = d.h
        self.dsems.append(d)
        return d

    def _wait(self, e, dep):
        key, val = dep
        if key == ("eng", "pe") and e == "pe":
            return
        if self.seen[e].get(key, 0) >= val:
            return
        self.eng[e].wait_ge(self.semh[key], val)
        self.seen[e][key] = val

    def _deps(self, e, R, W):
        for b in R:
            if b.w is not None:
                self._wait(e, b.w)
        for b in W:
            if b.w is not None:
                self._wait(e, b.w)
            for d in b.r.values():
                self._wait(e, d)

    def op(self, e, fn, R=(), W=()):
        self.nops += 1
        if self.nops > self.limit:
            return
        R = [x.b if isinstance(x, Tile) else x for x in R]
        W = [x.b if isinstance(x, Tile) else x for x in W]
        self._deps(e, R, W)
        if e != "pe":
            for b in R:
                if b.bank is not None:
                    for e2, tk in b.bank.items():
                        if e2 != e:
                            self._wait(e, tk)
        elif self.pe_bank_excl:
            for b in W:
                if b.bank is not None:
                    for e2, tk in b.bank.items():
                        self._wait(e, tk)
        ins = fn(self.eng[e])
        self.cnt[e] += 1
        ins.then_inc(self.semh[("eng", e)], 1)
        tok = (("eng", e), self.cnt[e])
        for b in R:
            b.r[tok[0]] = tok
            if b.bank is not None and e != "pe":
                b.bank[e] = tok
        for b in W:
            b.w = tok
            b.r = {}

    def dma(self, q, out, in_, R=(), W=(), sem=None):
        self.nops += 1
        if self.nops > self.limit:
            return
        R = [x.b if isinstance(x, Tile) else x for x in R]
        W = [x.b if isinstance(x, Tile) else x for x in W]
        self._deps(q, R, W)
        ins = self.eng[q].dma_start(out=out, in_=in_)
        sem.count += 16
        ins.then_inc(sem.h, 16)
        tok = (sem.key, sem.count)
        for b in R:
            b.r[tok[0]] = tok
        for b in W:
            b.w = tok
            b.r = {}

    def seal(self, sem, tiles):
        for t in tiles:
            b = t.b if isinstance(t, Tile) else t
            b.w = (sem.key, sem.count)

    def barrier(self):
        for e in self.eng:
            for e2 in self.eng:
                if e2 != e and self.cnt[e2] > 0:
                    self._wait(e, (("eng", e2), self.cnt[e2]))
            for d in self.dsems:
                if d.count > 0:
                    self._wait(e, (d.key, d.count))

    def final_wait(self, sems):
        for d in sems:
            if d.count > 0:
                self._wait("sp", (d.key, d.count))


CIDX = {}
for _ct in range(4):
    for _i in range(32):
        if 2 * _i + 1 - 16 * _ct >= 0 and 2 * _i - 16 * _ct < 23:
            CIDX[(_i, _ct)] = len(CIDX)


def build(PH=99, DBG=(), NBLK=None, GT=None, NSL=None, DSLOT=-1):
    nc = bass.Bass("TRN2", target_bir_lowering=False)
    try:
        nc.allow_low_precision("bf16 matmul operands, fp32 accumulation")
        nc.allow_non_contiguous_dma("strided scratch layouts")
    except Exception:
        pass
    kb = KB(nc)
    V, A, P, PE = "dve", "act", "pool", "pe"

    def din(name, shape, dt=F32):
        return nc.dram_tensor(name, list(shape), dt, kind="ExternalInput").ap()

    def dscr(name, shape, dt):
        return nc.dram_tensor(name, list(shape), dt).ap()

    from contextlib import ExitStack
    stk = [None]

    def sb(name, shape, dt=F32):
        if stk[0] is None:
            return Tile(nc.alloc_sbuf_tensor(name, list(shape), dt))
        return Tile(stk[0].enter_context(nc.sbuf_tensor(name, list(shape), dt)))

    xT = din("xT", [D, S])
    xr = din("xr", [S // 2, D])
    w_in = din("w_in", [D, 3872])
    nw = din("nw", [128, 8])
    convw = din("convw", [128, 12, 4])
    alog_b = din("alog_b", [128, 256])
    dtb_b = din("dtb_b", [128, 256])
    gnw_b = din("gnw_b", [128, 512])
    fnw_b = din("fnw_b", [128, D])
    w_out = din("w_out", [D, D])
    w1k = din("w1k", [64, 32, 128]); w1v = din("w1v", [64, 32, 128])
    pek = din("pek", [64, 32]); pev = din("pev", [64, 32])
    w2k = din("w2k", [128, 64]); w2v = din("w2v", [128, 64])
    tabS = din("tabS", [2, 9, 128, 512])
    tabWx = din("tabWx", [2, 2, 128, 512])
    tabCfar = din("tabCfar", [2, 128, 512])
    tabC = din("tabC", [2, 44, 128, 512])
    cexp = din("cexp", [128, 8])
    ovl = din("ovl", [128, 4, 128])
    adjT = din("adjT", [128, 256])
    j0b = din("j0b", [128, 32])
    pw = din("pw", [128, 2])
    consts = din("consts", [128, 1024])
    estk = din("estk", [64, S])
    y = nc.dram_tensor("y", [S // 2, D], F32, kind="ExternalOutput").ap()
    dbg_out = {}
    for name, shape, dt in DBG:
        dbg_out[name] = nc.dram_tensor("dbg_" + name, list(shape), dt, kind="ExternalOutput").ap()

    GQKV = dscr("GQKV", [12, 128, S], BF16)
    NQ = dscr("NQ", [8, 64, S], BF16)
    KVT = dscr("KVT", [4, 128, S], BF16)
    TM = dscr("TM", [S, 1312], F32)
    MIXG = dscr("MIXG", [S, 512], BF16)
    scr_b = {n: Buf() for n in ("GQKV", "NQ", "KVT", "TM", "MIXG")}

    cst = sb("cst", [128, 1024])
    ident = cst.t[:, 0:128]
    mask2 = cst.t[:, 128:384]
    tri2 = cst.t[:, 256:384]
    blk2 = cst.t[:, 384:512]
    sel0 = cst.t[:, 512:640]
    sel1 = cst.t[:, 640:768]
    nvalid = cst.t[:, 768:772]
    ones_f = cst.t[:, 772:900]
    epsc = cst.t[:, 900:901]
    identb = sb("identb", [128, 128], BF16)
    onesb = sb("onesb", [128, 128], BF16)
    gba = sb("gba", [128, 64, 8])
    ld = kb.dsem("ld0")
    kb.dma("sp", cst.t[:], consts, W=[cst], sem=ld)
    kb.op(V, lambda e: e.tensor_copy(out=identb.t[:], in_=ident), R=[cst], W=[identb])
    kb.op(V, lambda e: e.tensor_copy(out=onesb.t[:], in_=ones_f), R=[cst], W=[onesb])

    psum = [Tile(nc.alloc_psum_tensor(f"ps{i}", [128, 512], F32)) for i in range(7)]
    psb = Tile(nc.alloc_psum_tensor("psb", [128, 1024], BF16))
    for p_ in psum + [psb]:
        p_.b.bank = {}
    out_sems = []

    def dbg_dump(name, ap_src, tile, rows=128):
        if name in dbg_out:
            d = kb.dsem()
            out_sems.append(d)
            kb.dma("sp", dbg_out[name], ap_src, R=[tile], sem=d)

    stk[0] = ExitStack()
    Wb = sb("Wb", [128, 8, 3872], BF16)
    cw = sb("cw", [128, 12, 4])
    nwt = sb("nwt", [128, 8])
    ld1 = kb.dsem("ld1")
    kb.dma("sp", cw.t[:], convw, W=[cw], sem=ld1)
    kb.dma("sp", nwt.t[:], nw, W=[nwt], sem=ld1)
    kb.seal(ld1, [cw, nwt])
    ph1stk = stk[0]
    stk[0] = ExitStack()
    wst = [sb(f"wst{i}", [128, 3872]) for i in range(2)]
    wsem = [kb.dsem(f"wl{i}") for i in range(2)]
    w_v = w_in.rearrange("(kc p) c -> p kc c", p=128)
    for kc in range(8):
        st = wst[kc % 2]
        kb.dma("sp" if kc % 2 == 0 else "act", st.t[:], w_v[:, kc, :], W=[st], sem=wsem[kc % 2])
        half = 1936
        kb.op(V, lambda e, st=st, kc=kc: e.tensor_scalar(out=Wb.t[:, kc, 0:half], in0=st.t[:, 0:half], scalar1=nwt.t[:, kc:kc + 1], scalar2=None, op0=ALU.mult), R=[st, nwt], W=[Wb])
        kb.op(A, lambda e, st=st, kc=kc: e.activation(out=Wb.t[:, kc, half:3872], in_=st.t[:, half:3872], func=AF.Copy, scale=nwt.t[:, kc:kc + 1]), R=[st, nwt], W=[Wb])

    kb.barrier()
    stk[0].close()
    stk[0] = ph1stk
    xf = [sb(f"xf{i}", [128, 8, 512]) for i in range(2)]
    xsem = [kb.dsem(f"xl{i}") for i in range(2)]
    xb = sb("xb", [128, 8, 512], BF16)
    sq = sb("sq", [128, 8, 512], BF16)
    rbc = sb("rbc", [128, 512])
    cbuf = [sb(f"cbuf{i}", [128, 515]) for i in range(12)]
    for c in cbuf:
        kb.op(P, lambda e, c=c: e.memset(c.t[:, 0:3], 0.0), W=[c])
    acc = [sb(f"acc{i}", [128, 512]) for i in range(4)]
    slb = [sb(f"slb{i}", [128, 512]) for i in range(4)]
    sq2 = [sb(f"sq2{i}", [128, 512]) for i in range(4)]
    rn = [sb(f"rn{i}", [128, 512]) for i in range(4)]
    ptmp = sb("ptmp", [128, 512])
    stg = [sb(f"stg{i}", [128, 512], BF16) for i in range(4)]
    stsem = [kb.dsem(f"st{i}") for i in range(4)]
    tmb = [sb(f"tmb{i}", [128, 1312]) for i in range(2)]
    tmsem = [kb.dsem(f"tm{i}") for i in range(2)]
    rcol = [sb(f"rcol{i}", [128, 2]) for i in range(2)]
    xT_v = xT.rearrange("(kc p) t -> p kc t", p=128)
    nstg = [0]

    def store_stage(src_tile_fn, dram_aps):
        i = nstg[0] % 4
        nstg[0] += 1
        st = stg[i]
        src_tile_fn(st)
        for (dap, p0, p1, bkey) in dram_aps:
            kb.dma("pool", dap, st.t[p0:p1, :], R=[st], sem=stsem[i])

    NB = (16 if PH >= 1 else 0) if NBLK is None else NBLK
    for tb in range(NB):
        t0 = tb * 512
        xt = xf[tb % 2]
        kb.dma("sp", xt.t[:], xT_v[:, :, t0:t0 + 512], W=[xt], sem=xsem[tb % 2])
        kb.op(V, lambda e: e.tensor_copy(out=xb.t[:, 0:4, :], in_=xt.t[:, 0:4, :]), R=[xt], W=[xb])
        kb.op(V, lambda e: e.tensor_copy(out=xb.t[:, 4:8, :], in_=xt.t[:, 4:8, :]), R=[xt], W=[xb])
        kb.op(A, lambda e: e.activation(out=sq.t[:], in_=xt.t[:], func=AF.Square), R=[xt], W=[sq])
        pr = psum[0]
        for kc in range(8):
            kb.op(PE, lambda e, kc=kc: e.matmul(pr.t[:], lhsT=onesb.t[:], rhs=sq.t[:, kc, :], start=(kc == 0), stop=(kc == 7)), R=[onesb, sq], W=[pr])
        kb.op(A, lambda e: e.activation(out=rbc.t[:], in_=pr.t[:], func=AF.Ln, bias=epsc, scale=1.0 / D), R=[pr, cst], W=[rbc])
        kb.op(A, lambda e: e.activation(out=rbc.t[:], in_=rbc.t[:], func=AF.Exp, scale=-0.5), R=[rbc], W=[rbc])
        pending = []
        for m in range(20):
            pm = psum[(1, 2, 3, 0)[m % 4]]
            for kc in range(8):
                kb.op(PE, lambda e, kc=kc, m=m, pm=pm: e.matmul(pm.t[:], lhsT=Wb.t[:, kc, m * 128:(m + 1) * 128], rhs=xb.t[:, kc, :], start=(kc == 0), stop=(kc == 7)), R=[Wb, xb], W=[pm])
            while len(pending) > 2:
                pending.pop(0)()
            if m < 12:
                cb = cbuf[m]
                ce = V
                a_ = acc[m % 4]; s_ = slb[m % 4]
                kb.op(V, lambda e, pm=pm, cb=cb: e.tensor_tensor(out=cb.t[:, 3:515], in0=pm.t[:], in1=rbc.t[:], op=ALU.mult), R=[pm, rbc], W=[cb])
                kb.op(ce, lambda e, cb=cb, a_=a_, m=m: e.tensor_scalar(out=a_.t[:], in0=cb.t[:, 3:515], scalar1=cw.t[:, m, 3:4], scalar2=None, op0=ALU.mult), R=[cb, cw], W=[a_])
                for j in (2, 1, 0):
                    if ce == V:
                        kb.op(ce, lambda e, cb=cb, a_=a_, m=m, j=j: e.scalar_tensor_tensor(out=a_.t[:], in0=cb.t[:, j:j + 512], scalar=cw.t[:, m, j:j + 1], in1=a_.t[:], op0=ALU.mult, op1=ALU.add), R=[cb, cw, a_], W=[a_])
                    else:
                        kb.op(ce, lambda e, cb=cb, m=m, j=j: e.tensor_scalar(out=ptmp.t[:], in0=cb.t[:, j:j + 512], scalar1=cw.t[:, m, j:j + 1], scalar2=None, op0=ALU.mult), R=[cb, cw], W=[ptmp])
                        kb.op(ce, lambda e, a_=a_: e.tensor_tensor(out=a_.t[:], in0=a_.t[:], in1=ptmp.t[:], op=ALU.add), R=[ptmp, a_], W=[a_])
                kb.op(P, lambda e, cb=cb: e.tensor_copy(out=cb.t[:, 0:3], in_=cb.t[:, 512:515]), R=[cb], W=[cb])
                if m < 8:
                    kb.op(A, lambda e, a_=a_, s_=s_: e.activation(out=s_.t[:], in_=a_.t[:], func=AF.Silu), R=[a_], W=[s_])
                    q2 = sq2[m % 4]; r2 = rn[m % 4]
                    kb.op(A, lambda e, s_=s_, q2=q2: e.activation(out=q2.t[:], in_=s_.t[:], func=AF.Square), R=[s_], W=[q2])
                    pn = psum[4 + (m % 2)]

                    def fin(m=m, s_=s_, q2=q2, r2=r2, pn=pn, t0=t0):
                        kb.op(PE, lambda e: e.matmul(pn.t[:], lhsT=ones_f, rhs=q2.t[:], start=True, stop=True), R=[cst, q2], W=[pn])
                        kb.op(A, lambda e: e.activation(out=r2.t[:], in_=pn.t[:], func=AF.Ln, bias=epsc, scale=1.0), R=[pn, cst], W=[r2])
                        kb.op(A, lambda e: e.activation(out=r2.t[:], in_=r2.t[:], func=AF.Exp, scale=-0.5), R=[r2], W=[r2])
                        scl = (128.0 ** -0.5) if m < 4 else 1.0
                        store_stage(lambda st: kb.op(V, lambda e: e.scalar_tensor_tensor(out=st.t[:], in0=s_.t[:], scalar=scl, in1=r2.t[:], op0=ALU.mult, op1=ALU.mult), R=[s_, r2], W=[st]),
                                    [(GQKV[m, :, t0:t0 + 512], 0, 128, "GQKV")])
                    pending.append(fin)
                else:
                    store_stage(lambda st, a_=a_: kb.op(A, lambda e: e.activation(out=st.t[:], in_=a_.t[:], func=AF.Silu), R=[a_], W=[st]),
                                [(GQKV[m, :, t0:t0 + 512], 0, 128, "GQKV")])
            elif m < 16:
                hp = m - 12
                store_stage(lambda st, pm=pm: kb.op(V, lambda e: e.scalar_tensor_tensor(out=st.t[:], in0=pm.t[:], scalar=0.125, in1=rbc.t[:], op0=ALU.mult, op1=ALU.mult), R=[pm, rbc], W=[st]),
                            [(NQ[2 * hp, :, t0:t0 + 512], 0, 64, "NQ"), (NQ[2 * hp + 1, :, t0:t0 + 512], 64, 128, "NQ")])
            else:
                kv = m - 16
                store_stage(lambda st, pm=pm: kb.op(V, lambda e: e.tensor_tensor(out=st.t[:], in0=pm.t[:], in1=rbc.t[:], op=ALU.mult), R=[pm, rbc], W=[st]),
                            [(KVT[kv, :, t0:t0 + 512], 0, 128, "KVT")])
        while pending:
            pending.pop(0)()
        for sub in range(4):
            tt = tb * 4 + sub
            tm_ = tmb[tt % 2]
            rc = rcol[tt % 2]
            pc = psum[6]
            kb.op(PE, lambda e, sub=sub: e.matmul(pc.t[:, 0:1], lhsT=rbc.t[0:1, sub * 128:(sub + 1) * 128], rhs=ones_f[0:1, 0:1], start=True, stop=True), R=[rbc, cst], W=[pc])
            kb.op(A, lambda e, rc=rc: e.copy(out=rc.t[:, 1:2], in_=pc.t[:, 0:1]), R=[pc], W=[rc])
            for part, (c0, c1) in enumerate(((0, 512), (512, 1024), (1024, 1312))):
                pm = psum[1 + part]
                for kc in range(8):
                    kb.op(PE, lambda e, kc=kc, sub=sub, pm=pm, c0=c0, c1=c1: e.matmul(pm.t[:, 0:c1 - c0], lhsT=xb.t[:, kc, sub * 128:(sub + 1) * 128], rhs=Wb.t[:, kc, 2560 + c0:2560 + c1], start=(kc == 0), stop=(kc == 7)), R=[Wb, xb], W=[pm])
                if part < 2:
                    kb.op(A, lambda e, pm=pm, c0=c0, c1=c1, tm_=tm_, rc=rc: e.activation(out=tm_.t[:, c0:c1], in_=pm.t[:, 0:c1 - c0], func=AF.Copy, scale=rc.t[:, 1:2]), R=[pm, rc], W=[tm_])
                else:
                    kb.op(V, lambda e, pm=pm, c0=c0, c1=c1, tm_=tm_, rc=rc: e.tensor_scalar(out=tm_.t[:, c0:c1], in0=pm.t[:, 0:c1 - c0], scalar1=rc.t[:, 1:2], scalar2=None, op0=ALU.mult), R=[pm, rc], W=[tm_])
            kb.op(P, lambda e, tm_=tm_, tt=tt: e.tensor_copy(out=gba.t[:, tt, :], in_=tm_.t[:, 1304:1312]), R=[tm_], W=[gba])
            kb.dma("pool", TM[tt * 128:(tt + 1) * 128, :], tm_.t[:], R=[tm_], sem=tmsem[tt % 2])

    if "gba" in dbg_out:
        dbg_dump("gba", gba.t[:].rearrange("p a b -> p (a b)"), gba)
    kb.barrier()
    stk[0].close()
    stk[0] = None
    for name in ("GQKV", "NQ", "KVT", "TM"):
        if name in dbg_out:
            src = {"GQKV": GQKV, "NQ": NQ, "KVT": KVT, "TM": TM}[name]
            d = kb.dsem(); out_sems.append(d)
            if name == "TM":
                for c in range(64):
                    kb.dma("sp", dbg_out[name][c * 128:(c + 1) * 128, :], src[c * 128:(c + 1) * 128, :], sem=d)
            else:
                for c in range(src.shape[0]):
                    kb.dma("sp", dbg_out[name][c], src[c], sem=d)


    def pq(bank, q0, q1):
        return psum[bank].t[:, q0 * 128:q1 * 128]

    pqb = [[Buf(bank=psum[i].b.bank) for _ in range(4)] for i in range(7)]
    if PH >= 2:
        stk[0] = ExitStack()
        sc = {}
        for nm in ("e1", "d1", "beta", "lnb", "z", "sp", "Aex", "g", "gc", "gl", "yy", "egc", "bg", "kd", "EGL0", "EGL1", "dtb", "alg"):
            sc[nm] = sb("sc_" + nm, [128, 256])
        gbv = gba.t[:, :, 0:4]
        gav = gba.t[:, :, 4:8]

        def v3(tl):
            return tl.t[:].rearrange("p (a b) -> p a b", b=4)

        ld2 = kb.dsem("ld2")
        kb.dma("sp", sc["dtb"].t[:], dtb_b, W=[sc["dtb"]], sem=ld2)
        kb.dma("sp", sc["alg"].t[:], alog_b, W=[sc["alg"]], sem=ld2)
        kb.seal(ld2, [sc["dtb"], sc["alg"]])
        kb.op(A, lambda e: e.activation(out=v3(sc["e1"]), in_=gbv, func=AF.Exp, scale=-1.0), R=[gba], W=[sc["e1"]])
        kb.op(V, lambda e: e.tensor_scalar(out=sc["d1"].t[:], in0=sc["e1"].t[:], scalar1=1.0, scalar2=None, op0=ALU.add), R=[sc["e1"]], W=[sc["d1"]])
        kb.op(V, lambda e: e.reciprocal(out=sc["beta"].t[:], in_=sc["d1"].t[:]), R=[sc["d1"]], W=[sc["beta"]])
        kb.op(A, lambda e: e.activation(out=sc["lnb"].t[:], in_=sc["d1"].t[:], func=AF.Ln), R=[sc["d1"]], W=[sc["lnb"]])
        kb.op(V, lambda e: e.tensor_tensor(out=v3(sc["z"]), in0=gav, in1=v3(sc["dtb"]), op=ALU.add), R=[gba, sc["dtb"]], W=[sc["z"]])
        kb.op(A, lambda e: e.activation(out=sc["z"].t[:], in_=sc["z"].t[:], func=AF.Exp), R=[sc["z"]], W=[sc["z"]])
        kb.op(V, lambda e: e.tensor_scalar(out=sc["z"].t[:], in0=sc["z"].t[:], scalar1=1.0, scalar2=None, op0=ALU.add), R=[sc["z"]], W=[sc["z"]])
        kb.op(A, lambda e: e.activation(out=sc["sp"].t[:], in_=sc["z"].t[:], func=AF.Ln), R=[sc["z"]], W=[sc["sp"]])
        kb.op(A, lambda e: e.activation(out=sc["Aex"].t[:], in_=sc["alg"].t[:], func=AF.Exp), R=[sc["alg"]], W=[sc["Aex"]])
        kb.op(V, lambda e: e.scalar_tensor_tensor(out=sc["g"].t[:], in0=sc["sp"].t[:], scalar=-1.0, in1=sc["Aex"].t[:], op0=ALU.mult, op1=ALU.mult), R=[sc["sp"], sc["Aex"]], W=[sc["g"]])
        p0 = psum[0]
        kb.op(PE, lambda e: e.matmul(p0.t[:, 0:256], lhsT=tri2, rhs=sc["g"].t[:], start=True, stop=True), R=[cst, sc["g"]], W=[p0])
        kb.op(PE, lambda e: e.matmul(p0.t[:, 256:512], lhsT=blk2, rhs=sc["g"].t[:], start=True, stop=True), R=[cst, sc["g"]], W=[p0])
        kb.op(V, lambda e: e.tensor_copy(out=sc["gc"].t[:], in_=p0.t[:, 0:256]), R=[p0], W=[sc["gc"]])
        kb.op(V, lambda e: e.tensor_copy(out=sc["gl"].t[:], in_=p0.t[:, 256:512]), R=[p0], W=[sc["gl"]])
        kb.op(V, lambda e: e.tensor_tensor(out=sc["yy"].t[:], in0=sc["gc"].t[:], in1=sc["lnb"].t[:], op=ALU.subtract), R=[sc["gc"], sc["lnb"]], W=[sc["yy"]])
        kb.op(A, lambda e: e.activation(out=sc["egc"].t[:], in_=sc["gc"].t[:], func=AF.Exp), R=[sc["gc"]], W=[sc["egc"]])
        kb.op(V, lambda e: e.tensor_tensor(out=sc["bg"].t[:], in0=sc["beta"].t[:], in1=sc["egc"].t[:], op=ALU.mult), R=[sc["beta"], sc["egc"]], W=[sc["bg"]])
        kb.op(V, lambda e: e.tensor_tensor(out=sc["kd"].t[:], in0=sc["gl"].t[:], in1=sc["gc"].t[:], op=ALU.subtract), R=[sc["gl"], sc["gc"]], W=[sc["kd"]])
        kb.op(A, lambda e: e.activation(out=sc["kd"].t[:], in_=sc["kd"].t[:], func=AF.Exp), R=[sc["kd"]], W=[sc["kd"]])
        p1 = psum[1]
        kb.op(PE, lambda e: e.matmul(p1.t[:, 0:256], lhsT=sel0, rhs=sc["gl"].t[:], start=True, stop=True), R=[cst, sc["gl"]], W=[p1])
        kb.op(PE, lambda e: e.matmul(p1.t[:, 256:512], lhsT=sel1, rhs=sc["gl"].t[:], start=True, stop=True), R=[cst, sc["gl"]], W=[p1])
        kb.op(A, lambda e: e.activation(out=sc["EGL0"].t[:], in_=p1.t[:, 0:256], func=AF.Exp), R=[p1], W=[sc["EGL0"]])
        kb.op(A, lambda e: e.activation(out=sc["EGL1"].t[:], in_=p1.t[:, 256:512], func=AF.Exp), R=[p1], W=[sc["EGL1"]])
        for nm in ("g", "beta", "gc"):
            dbg_dump("sc_" + nm, sc[nm].t[:], sc[nm])
        kb.barrier()

        kqv = [sb(f"kqv{i}", [128, 12, 128], BF16) for i in range(2)]
        kqsem = [kb.dsem(f"kq{i}") for i in range(2)]
        gzt = [sb(f"gzt{i}", [128, 512]) for i in range(2)]
        gzsem = [kb.dsem(f"gz{i}") for i in range(2)]
        gnw = sb("gnw", [128, 512])
        ld3 = kb.dsem("ld3")
        kb.dma("sp", gnw.t[:], gnw_b, W=[gnw], sem=ld3)
        Sst = [[sb(f"S{h}_{i}", [128, 128]) for i in range(2)] for h in range(4)]
        Sbf = [[sb(f"Sb{h}_{i}", [128, 128], BF16) for i in range(2)] for h in range(4)]
        for h in range(4):
            kb.op(P, lambda e, h=h: e.memset(Sst[h][0].t[:], 0.0), W=[Sst[h][0]])
            kb.op(P, lambda e, h=h: e.memset(Sbf[h][0].t[:], 0.0), W=[Sbf[h][0]])
        H4 = range(4)
        dg = [sb(f"dg{h}", [128, 256]) for h in H4]
        t1 = [sb(f"t1{h}", [128, 256]) for h in H4]
        Dd = [sb(f"Dd{h}", [128, 256]) for h in H4]
        MA = [sb(f"MA{h}", [128, 128]) for h in H4]
        ATb = [sb(f"ATb{h}", [128, 128], BF16) for h in H4]
        TTb = [sb(f"TTb{h}", [128, 128], BF16) for h in H4]
        XYb = [[sb(f"XYb{h}_{i}", [128, 256], BF16) for i in range(2)] for h in H4]
        Zb = [sb(f"Zb{h}", [128, 128], BF16) for h in H4]
        Zt = [sb(f"Z{h}", [128, 128]) for h in H4]
        egb = [sb(f"egb{h}", [128, 128]) for h in H4]
        rk = [sb(f"rk{h}", [128, 256], BF16) for h in H4]
        kdec = [sb(f"kdec{h}", [128, 128], BF16) for h in H4]
        WU = [sb(f"WU{h}", [128, 256], BF16) for h in H4]
        qd = [sb(f"qd{h}", [128, 128]) for h in H4]
        QtT = [sb(f"QtT{h}", [128, 128], BF16) for h in H4]
        NPh = [[sb(f"NPh{h}_{c}", [128, 128], BF16) for c in range(2)] for h in H4]
        ot = [sb(f"ot{i}", [128, 512]) for i in range(2)]
        osq = sb("osq", [128, 512])
        oss = sb("oss", [128, 8])
        gsg = sb("gsg", [128, 512])
        mixst = [sb(f"mixst{i}", [128, 512], BF16) for i in range(2)]
        mxsem = [kb.dsem(f"mx{i}") for i in range(2)]
        GQ_v = GQKV.rearrange("c p t -> p c t")
        NTG = NT if GT is None else GT
        cslot = [0]

        dslot = [0]

        def cps():
            q = cslot[0] % 4
            cslot[0] += 1
            return psum[4].t[:, q * 128:(q + 1) * 128], pqb[4][q]

        def cpd():
            i = dslot[0] % 8
            dslot[0] += 1
            bank = 5 + i // 4
            q = i % 4
            return psum[bank].t[:, q * 128:(q + 1) * 128], pqb[bank][q]

        import os as _os
        GST = int(_os.environ.get('GSTAGE', '99'))
        out_defer = []
        for t in range(NTG):
            kq = kqv[t % 2]
            kb.dma("sp", kq.t[:], GQ_v[:, :, t * 128:(t + 1) * 128], R=[scr_b["GQKV"]], W=[kq], sem=kqsem[t % 2])
            gz_ = gzt[t % 2]
            kb.dma("act", gz_.t[:], TM[t * 128:(t + 1) * 128, 0:512], R=[scr_b["TM"]], W=[gz_], sem=gzsem[t % 2])
            o_ = ot[t % 2]
            col = lambda nm, h: sc[nm].t[:, t * 4 + h:t * 4 + h + 1]
            B = lambda h, q: pqb[h][q]
            for h in H4:
                qT = kq.t[:, h, :]; kT = kq.t[:, 4 + h, :]
                kb.op(PE, lambda e, h=h, kT=kT: e.matmul(pq(h, 0, 1), lhsT=kT, rhs=kT, start=True, stop=True), R=[kq], W=[B(h, 0)])
                kb.op(PE, lambda e, h=h, kT=kT, qT=qT: e.matmul(pq(h, 1, 2), lhsT=kT, rhs=qT, start=True, stop=True), R=[kq], W=[B(h, 1)])
                kb.op(PE, lambda e, h=h: e.matmul(pq(h, 2, 3), lhsT=col("yy", h).to_broadcast([128, 128]), rhs=ident, start=True, stop=True), R=[cst, sc["yy"]], W=[B(h, 2)])
                kb.op(PE, lambda e, h=h: e.matmul(pq(h, 3, 4), lhsT=col("gc", h).to_broadcast([128, 128]), rhs=ident, start=True, stop=True), R=[cst, sc["gc"]], W=[B(h, 3)])
            if GST < 2:
                continue
            for h in H4:
                kb.op(V, lambda e, h=h: e.tensor_scalar(out=t1[h].t[:], in0=pq(h, 2, 4), scalar1=col("gc", h), scalar2=0.0, op0=ALU.subtract, op1=ALU.min), R=[B(h, 2), B(h, 3), sc["gc"]], W=[t1[h]])
            for h in H4:
                kb.op(A, lambda e, h=h: e.activation(out=egb[h].t[:], in_=pq(h, 3, 4), func=AF.Exp), R=[B(h, 3), t1[h]], W=[egb[h]])
                kb.op(A, lambda e, h=h: e.activation(out=t1[h].t[:], in_=t1[h].t[:], func=AF.Exp), R=[t1[h]], W=[t1[h]])
            for h in H4:
                kb.op(P, lambda e, h=h: e.tensor_tensor(out=Dd[h].t[:], in0=t1[h].t[:], in1=mask2, op=ALU.mult), R=[t1[h], cst], W=[Dd[h]])
            for h in H4:
                kb.op(V, lambda e, h=h: e.tensor_tensor(out=MA[h].t[:], in0=pq(h, 0, 1), in1=Dd[h].t[:, 0:128], op=ALU.mult), R=[B(h, 0), Dd[h]], W=[MA[h]])
                kb.op(V, lambda e, h=h: e.tensor_tensor(out=ATb[h].t[:], in0=pq(h, 1, 2), in1=Dd[h].t[:, 128:256], op=ALU.mult), R=[B(h, 1), Dd[h]], W=[ATb[h]])
            if GST < 3:
                continue
            for h in H4:
                kb.op(PE, lambda e, h=h: e.transpose(out=pq(h, 2, 3), in_=MA[h].t[:, 0:128], identity=ident), R=[MA[h], cst], W=[B(h, 2)])
                kb.op(A, lambda e, h=h: e.copy(out=XYb[h][0].t[:, 0:128], in_=pq(h, 2, 3)), R=[B(h, 2)], W=[XYb[h][0]])
                kb.op(P, lambda e, h=h: e.tensor_copy(out=XYb[h][0].t[:, 128:256], in_=MA[h].t[:, 0:128]), R=[MA[h]], W=[XYb[h][0]])
                kb.op(P, lambda e, h=h: e.tensor_tensor(out=Zt[h].t[:], in0=ident, in1=MA[h].t[:, 0:128], op=ALU.subtract), R=[cst, MA[h]], W=[Zt[h]])
            if GST < 4:
                continue
            for lv in range(5):
                for h in H4:
                    cur = XYb[h][lv % 2]
                    nxtb = XYb[h][(lv + 1) % 2]
                    kb.op(PE, lambda e, h=h, cur=cur: e.matmul(pq(h, 0, 1), lhsT=cur.t[:, 128:256], rhs=cur.t[:, 0:128], start=True, stop=True), R=[cur], W=[B(h, 0)])
                    if lv < 4:
                        kb.op(PE, lambda e, h=h, cur=cur: e.matmul(pq(h, 1, 2), lhsT=cur.t[:, 0:128], rhs=cur.t[:, 128:256], start=True, stop=True), R=[cur], W=[B(h, 1)])
                        kb.op(A, lambda e, h=h, nxtb=nxtb: e.copy(out=nxtb.t[:], in_=pq(h, 0, 2)), R=[B(h, 0), B(h, 1)], W=[nxtb])
                    else:
                        kb.op(A, lambda e, h=h, nxtb=nxtb: e.copy(out=nxtb.t[:, 0:128], in_=pq(h, 0, 1)), R=[B(h, 0)], W=[nxtb])
                    kb.op(A, lambda e, h=h: e.copy(out=Zb[h].t[:], in_=Zt[h].t[:]), R=[Zt[h]], W=[Zb[h]])
                for h in H4:
                    zq = 2 + (lv % 2)
                    nxtb = XYb[h][(lv + 1) % 2]
                    kb.op(PE, lambda e, h=h, nxtb=nxtb, zq=zq: e.matmul(pq(h, zq, zq + 1), lhsT=nxtb.t[:, 0:128], rhs=Zb[h].t[:], start=True, stop=True), R=[nxtb, Zb[h]], W=[B(h, zq)])
                    kb.op(V, lambda e, h=h, zq=zq: e.tensor_tensor(out=Zt[h].t[:], in0=Zt[h].t[:], in1=pq(h, zq, zq + 1), op=ALU.add), R=[B(h, zq), Zt[h]], W=[Zt[h]])
            while out_defer:
                out_defer.pop(0)()
            for h in H4:
                kT = kq.t[:, 4 + h, :]; vT = kq.t[:, 8 + h, :]
                kb.op(PE, lambda e, h=h, kT=kT: e.transpose(out=psb.t[:, h * 256:h * 256 + 128], in_=kT, identity=identb.t[:]), R=[kq, identb], W=[psb])
                kb.op(PE, lambda e, h=h, vT=vT: e.transpose(out=psb.t[:, h * 256 + 128:h * 256 + 256], in_=vT, identity=identb.t[:]), R=[kq, identb], W=[psb])
            for h in H4:
                kb.op(A, lambda e, h=h: e.activation(out=rk[h].t[:, 0:128], in_=psb.t[:, h * 256:h * 256 + 128], func=AF.Copy, scale=col("bg", h)), R=[psb, sc["bg"]], W=[rk[h]])
                kb.op(A, lambda e, h=h: e.activation(out=rk[h].t[:, 128:256], in_=psb.t[:, h * 256 + 128:h * 256 + 256], func=AF.Copy, scale=col("beta", h)), R=[psb, sc["beta"]], W=[rk[h]])
                kb.op(A, lambda e, h=h: e.activation(out=kdec[h].t[:], in_=psb.t[:, h * 256:h * 256 + 128], func=AF.Copy, scale=col("kd", h)), R=[psb, sc["kd"]], W=[kdec[h]])
                kb.op(V, lambda e, h=h: e.tensor_tensor(out=qd[h].t[:], in0=kq.t[:, h, :], in1=egb[h].t[:], op=ALU.mult), R=[kq, egb[h]], W=[qd[h]])
            if GST < 6:
                continue
            for h in H4:
                kb.op(A, lambda e, h=h: e.copy(out=TTb[h].t[:], in_=Zt[h].t[:]), R=[Zt[h]], W=[TTb[h]])
                kb.op(PE, lambda e, h=h: e.matmul(pq(h, 0, 2), lhsT=TTb[h].t[:], rhs=rk[h].t[:], start=True, stop=True), R=[TTb[h], rk[h]], W=[B(h, 0), B(h, 1)])
                kb.op(A, lambda e, h=h: e.copy(out=WU[h].t[:], in_=pq(h, 0, 2)), R=[B(h, 0), B(h, 1)], W=[WU[h]])
            if GST < 7:
                continue
            for h in H4:
                aap, abf = cpd()
                kb.op(PE, lambda e, h=h, aap=aap: e.matmul(aap, lhsT=WU[h].t[:, 0:128], rhs=ATb[h].t[:], start=True, stop=True), R=[WU[h], ATb[h]], W=[abf])
                kb.op(V, lambda e, h=h, aap=aap: e.tensor_tensor(out=QtT[h].t[:], in0=qd[h].t[:], in1=aap, op=ALU.subtract), R=[qd[h], abf], W=[QtT[h]])
                for c in range(int(_os.environ.get('GS7', '2'))):
                    r = slice(64 * c, 64 * c + 64)
                    kb.op(PE, lambda e, h=h, c=c, r=r: e.matmul(pq(h, c, c + 1), lhsT=WU[h].t[r, 0:128], rhs=kdec[h].t[r, :], start=True, stop=True), R=[WU[h], kdec[h]], W=[B(h, c)])
                    kb.op(A, lambda e, h=h, c=c: e.activation(out=NPh[h][c].t[:], in_=pq(h, c, c + 1), func=AF.Copy, scale=-1.0), R=[B(h, c)], W=[NPh[h][c]])
            if GST < 8:
                continue
            for c in range(2):
                r = slice(64 * c, 64 * c + 64)
                for h in H4:
                    Sc = Sst[h][c]; Sn = Sst[h][1 - c]; Scb = Sbf[h][c]; Snb = Sbf[h][1 - c]
                    oap, ob = cps()
                    kb.op(PE, lambda e, h=h, oap=oap, Scb=Scb: e.matmul(oap, lhsT=QtT[h].t[:], rhs=Scb.t[:], start=True, stop=False), R=[QtT[h], Scb], W=[ob])
                    kb.op(PE, lambda e, h=h, oap=oap, r=r: e.matmul(oap, lhsT=ATb[h].t[r, :], rhs=WU[h].t[r, 128:256], start=False, stop=True), R=[ATb[h], WU[h]], W=[ob])
                    kb.op(A, lambda e, h=h, oap=oap, r=r: e.copy(out=o_.t[r, h * 128:(h + 1) * 128], in_=oap[r, :]), R=[ob], W=[o_])
                    sap, sbf = cpd()
                    kb.op(PE, lambda e, h=h, sap=sap, r=r: e.matmul(sap, lhsT=kdec[h].t[r, :], rhs=WU[h].t[r, 128:256], start=True, stop=False), R=[kdec[h], WU[h]], W=[sbf])
                    kb.op(PE, lambda e, h=h, c=c, sap=sap, Scb=Scb: e.matmul(sap, lhsT=NPh[h][c].t[:], rhs=Scb.t[:], start=False, stop=True), R=[NPh[h][c], Scb], W=[sbf])
                    egl = sc["EGL%d" % c].t[:, t * 4 + h:t * 4 + h + 1]
                    kb.op(V, lambda e, sap=sap, Sc=Sc, Snb=Snb, egl=egl: e.scalar_tensor_tensor(out=Snb.t[:], in0=Sc.t[:], scalar=egl, in1=sap, op0=ALU.mult, op1=ALU.add), R=[Sc, sbf, sc["EGL%d" % c]], W=[Snb])
                    kb.op(V, lambda e, sap=sap, Sc=Sc, Sn=Sn, egl=egl: e.scalar_tensor_tensor(out=Sn.t[:], in0=Sc.t[:], scalar=egl, in1=sap, op0=ALU.mult, op1=ALU.add), R=[Sc, sbf, sc["EGL%d" % c]], W=[Sn])
            def out_stage(t=t, o_=o_, gz_=gz_):
                ms = mixst[t % 2]
                kb.op(A, lambda e: e.activation(out=osq.t[:], in_=o_.t[:], func=AF.Square), R=[o_], W=[osq])
                kb.op(V, lambda e: e.tensor_reduce(out=oss.t[:, 0:4], in_=osq.t[:].rearrange("p (h d) -> p h d", d=128), axis=mybir.AxisListType.X, op=ALU.add), R=[osq], W=[oss])
                kb.op(A, lambda e: e.activation(out=oss.t[:, 0:4], in_=oss.t[:, 0:4], func=AF.Ln, bias=epsc, scale=1.0 / 128), R=[oss, cst], W=[oss])
                kb.op(A, lambda e: e.activation(out=oss.t[:, 4:8], in_=oss.t[:, 0:4], func=AF.Exp, scale=-0.5), R=[oss], W=[oss])
                kb.op(A, lambda e: e.activation(out=gsg.t[:], in_=gz_.t[:], func=AF.Exp, scale=-1.0), R=[gz_], W=[gsg])
                kb.op(A, lambda e: e.activation(out=gsg.t[:], in_=gsg.t[:], func=AF.Ln, bias=ones_f[:, 0:1], scale=1.0), R=[gsg, cst], W=[gsg])
                kb.op(A, lambda e: e.activation(out=gsg.t[:], in_=gsg.t[:], func=AF.Exp, scale=-1.0), R=[gsg], W=[gsg])
                kb.op(P, lambda e: e.tensor_tensor(out=gsg.t[:], in0=gsg.t[:], in1=gz_.t[:], op=ALU.mult), R=[gsg, gz_], W=[gsg])
                kb.op(P, lambda e: e.tensor_tensor(out=gsg.t[:], in0=gsg.t[:], in1=gnw.t[:], op=ALU.mult), R=[gsg, gnw], W=[gsg])
                for h in H4:
                    kb.op(V, lambda e, h=h: e.scalar_tensor_tensor(out=ms.t[:, h * 128:(h + 1) * 128], in0=o_.t[:, h * 128:(h + 1) * 128], scalar=oss.t[:, 4 + h:5 + h], in1=gsg.t[:, h * 128:(h + 1) * 128], op0=ALU.mult, op1=ALU.mult), R=[o_, oss, gsg], W=[ms])
                kb.dma("pool", MIXG[t * 128:(t + 1) * 128, :], ms.t[:], R=[ms], sem=mxsem[t % 2])

            out_defer.append(out_stage)
        while out_defer:
            out_defer.pop(0)()
        kb.barrier()
        stk[0].close()
        stk[0] = None
        if "MIXG" in dbg_out:
            d = kb.dsem(); out_sems.append(d)
            for c in range(16):
                kb.dma("sp", dbg_out["MIXG"][c * 512:(c + 1) * 512, :], MIXG[c * 512:(c + 1) * 512, :], sem=d)


    if PH >= 3:
        stk[0] = ExitStack()
        EXP = AF.Exp
        ksT = [sb(f"ksT{g}", [128, S], BF16) for g in range(2)]
        Vs = [sb(f"Vs{g}", [128, 64, 65], BF16) for g in range(2)]
        tS = sb("tS", [128, 2, 9, 512])
        tWx = sb("tWx", [128, 2, 2, 512])
        tCf = sb("tCf", [128, 2, 512])
        Wo = sb("Wo", [128, 8, D], BF16)
        ovb = sb("ovb", [128, 4, 128], BF16)
        adjc = sb("adjc", [128, 256])
        j0c = sb("j0c", [128, 32])
        pwc = sb("pwc", [128, 2])
        ecx = sb("ecx", [128, 8])
        fnw = sb("fnw", [128, D])
        kcT = [sb(f"kcT{g}", [64, 512], BF16) for g in range(2)]
        vca = [sb(f"vca{g}", [128, 4, 65], BF16) for g in range(2)]
        l3 = kb.dsem("l3")
        mainstk = stk[0]
        stk[0] = ExitStack()
        stgA = sb("stgA", [128, 2048])
        for g in range(2):
            kb.dma("sp", ksT[g].t[0:64, :], KVT[2, 64 * g:64 * g + 64, :], R=[scr_b["KVT"]], W=[ksT[g]], sem=l3)
            for m in range(9):
                kb.dma("act", tS.t[:, g, m, :], tabS[g, m], W=[tS], sem=l3)
            for m in range(2):
                kb.dma("act", tWx.t[:, m, g, :], tabWx[m, g], W=[tWx], sem=l3)
            kb.dma("act", tCf.t[:, g, :], tabCfar[g], W=[tCf], sem=l3)
        for nm_, dst, src in (("adj", adjc, adjT), ("j0", j0c, j0b), ("pw", pwc, pw), ("ec", ecx, cexp), ("fnw", fnw, fnw_b)):
            kb.dma("sp", dst.t[:], src, W=[dst], sem=l3)
        kb.seal(l3, ksT + [tS, tWx, tCf, adjc, j0c, pwc, ecx, fnw])
        kb.op(A, lambda e: e.activation(out=ecx.t[:], in_=ecx.t[:], func=EXP), R=[ecx], W=[ecx])
        sgs = kb.dsem("sgs")
        for c in range(4):
            kb.dma("sp", stgA.t[64:128, :], estk[:, c * 2048:(c + 1) * 2048], W=[stgA], sem=sgs)
            for g in range(2):
                kb.op(V, lambda e, c=c, g=g: e.tensor_copy(out=ksT[g].t[64:128, c * 2048:(c + 1) * 2048], in_=stgA.t[64:128, :]), R=[stgA], W=[ksT[g]])
        stgB = sb("stgB", [128, 2048])
        sgs2 = kb.dsem("sgs2")
        TMv = TM[:, 1024:1152].rearrange("(k p) c -> p k c", p=128)
        for c in range(4):
            stg_, sem_, q_ = (stgA, sgs, "sp") if c % 2 == 0 else (stgB, sgs2, "act")
            kb.dma(q_, stg_.t[:], TMv[:, c * 16:(c + 1) * 16, :], R=[scr_b["TM"]], W=[stg_], sem=sem_)
            sv = stg_.t[:].rearrange("p (k c) -> p k c", c=128)
            for g in range(2):
                kb.op(V if g == 0 else A, lambda e, c=c, g=g, sv=sv: (e.tensor_copy(out=Vs[g].t[:, c * 16:(c + 1) * 16, 0:64], in_=sv[:, :, 64 * g:64 * g + 64]) if g == 0 else e.copy(out=Vs[g].t[:, c * 16:(c + 1) * 16, 0:64], in_=sv[:, :, 64 * g:64 * g + 64])), R=[stg_], W=[Vs[g]])
        for g in range(2):
            kb.op(P, lambda e, g=g: e.memset(Vs[g].t[:, :, 64:65], 1.0), W=[Vs[g]])
        wo_v = w_out.rearrange("(c p) n -> p c n", p=128)
        for c in range(8):
            kb.dma("sp", stgA.t[:, 0:1024], wo_v[:, c, :], W=[stgA], sem=sgs)
            kb.op(V, lambda e, c=c: e.tensor_copy(out=Wo.t[:, c, :], in_=stgA.t[:, 0:1024]), R=[stgA], W=[Wo])
        kb.dma("sp", stgA.t[:, 0:512], ovl.rearrange("p a b -> p (a b)"), W=[stgA], sem=sgs)
        kb.op(V, lambda e: e.tensor_copy(out=ovb.t[:].rearrange("p a b -> p (a b)"), in_=stgA.t[:, 0:512]), R=[stgA], W=[ovb])
        rawT = sb("rawT", [64, S], BF16)
        w1s = sb("w1s", [64, 4096])
        w1b = sb("w1b", [64, 32, 128], BF16)
        peb_ = sb("peb_", [64, 32])
        pebb = sb("pebb", [64, 32], BF16)
        w2s = sb("w2s", [128, 64])
        w2b = sb("w2b", [128, 64], BF16)
        pcol = sb("pcol", [128, 1])
        hb = sb("hb", [128, 512], BF16)
        kb.op(P, lambda e: e.memset(hb.t[:], 0.0), W=[hb])
        for kind, (w1d, ped, w2d) in enumerate(((w1k, pek, w2k), (w1v, pev, w2v))):
            kb.dma("sp", w1s.t[:], w1d.rearrange("p a b -> p (a b)"), W=[w1s], sem=sgs)
            kb.dma("sp", peb_.t[:], ped, W=[peb_], sem=sgs)
            kb.dma("sp", w2s.t[:], w2d, W=[w2s], sem=sgs)
            kb.seal(sgs, [w1s, peb_, w2s])
            kb.op(V, lambda e: e.tensor_copy(out=w1b.t[:].rearrange("p a b -> p (a b)"), in_=w1s.t[:]), R=[w1s], W=[w1b])
            kb.op(V, lambda e: e.tensor_copy(out=pebb.t[:], in_=peb_.t[:]), R=[peb_], W=[pebb])
            kb.op(V, lambda e: e.tensor_copy(out=w2b.t[:], in_=w2s.t[:]), R=[w2s], W=[w2b])
            for g in range(2):
                kb.dma("sp", rawT.t[:], KVT[kind, 64 * g:64 * g + 64, :], R=[scr_b["KVT"]], W=[rawT], sem=sgs)
                rv = rawT.t[:].rearrange("p (n r) -> p n r", r=16)
                ph_ = psum[0]; pp_ = psum[1]
                for l in range(32):
                    a_, r_ = l // 16, l % 16
                    kb.op(PE, lambda e, l=l, a_=a_, r_=r_: e.matmul(ph_.t[:, 0:511], lhsT=w1b.t[:, l, :], rhs=rv[:, a_:a_ + 511, r_], start=(l == 0), stop=(l == 31)), R=[w1b, rawT], W=[ph_])
                for l in range(32):
                    kb.op(PE, lambda e, l=l: e.matmul(pp_.t[:, 0:1], lhsT=w1b.t[:, l, :], rhs=pebb.t[:, l:l + 1], start=(l == 0), stop=(l == 31)), R=[w1b, pebb], W=[pp_])
                kb.op(V, lambda e: e.tensor_copy(out=pcol.t[:], in_=pp_.t[:, 0:1]), R=[pp_], W=[pcol])
                kb.op(A, lambda e: e.activation(out=hb.t[:, 0:511], in_=ph_.t[:, 0:511], func=AF.Silu, bias=pcol.t[:, 0:1], scale=1.0), R=[ph_, pcol], W=[hb])
                if kind == 0:
                    pk_ = psum[2]
                    kb.op(PE, lambda e: e.matmul(pk_.t[0:64, :], lhsT=w2b.t[:], rhs=hb.t[:], start=True, stop=True), R=[w2b, hb], W=[pk_])
                    kb.op(V, lambda e, g=g: e.tensor_copy(out=kcT[g].t[:], in_=pk_.t[0:64, :]), R=[pk_], W=[kcT[g]])
                else:
                    pk_ = psum[2]
                    for ct in range(4):
                        kb.op(PE, lambda e, ct=ct: e.matmul(pk_.t[:, ct * 64:(ct + 1) * 64], lhsT=hb.t[:, ct * 128:(ct + 1) * 128], rhs=w2b.t[:], start=True, stop=True), R=[w2b, hb], W=[pk_])
                    kb.op(V, lambda e, g=g: e.tensor_copy(out=vca[g].t[:, :, 0:64], in_=pk_.t[:, 0:256].rearrange("p (a b) -> p a b", b=64)), R=[pk_], W=[vca[g]])
                    kb.op(V, lambda e, g=g: e.tensor_copy(out=vca[g].t[:, :, 64:65], in_=nvalid.rearrange("p (a b) -> p a b", b=1)), R=[cst], W=[vca[g]])
        if "kcT" in dbg_out:
            for g in range(2):
                dbg_dump("kcT", None, None)
        kb.barrier()
        stk[0].close()
        stk[0] = mainstk
        NB2 = 1
        qp = [sb(f"qp{i}", [64, 8, 256], BF16) for i in range(NB2)]
        qT_ = [sb(f"qT{i}", [64, 8, 128], BF16) for i in range(NB2)]
        qtmp = sb("qtmp", [64, 8, 128], BF16)
        tm2 = [sb(f"tm2{i}", [128, 2, 536]) for i in range(NB2)]
        tmq = sb("tmq", [128, 536])
        mg2 = [sb(f"mg2{i}", [128, 2, 512], BF16) for i in range(NB2)]
        xrt = [sb(f"xrt{i}", [128, D]) for i in range(NB2)]
        kwt = [sb(f"kwt{i}", [64, 2, 768], BF16) for i in range(NB2)]
        vwst = [sb(f"vwst{i}", [128, 6, 128]) for i in range(NB2)]
        Vw = sb("Vw", [128, 6, 2, 65], BF16)
        kb.op(P, lambda e: e.memset(Vw.t[:, :, :, 64:65], 1.0), W=[Vw])
        slsem = [[kb.dsem(f"sl{k}_{i}") for i in range(NB2)] for k in range(6)]
        tcb = [sb(f"tcb{i}", [128, 512]) for i in range(2)]
        tcsem = [kb.dsem(f"tc{i}") for i in range(2)]
        sbt = [sb(f"sbt{i}", [128, 512]) for i in range(4)]
        Pt = [sb(f"Pt{i}", [128, 512], BF16) for i in range(6)]
        QS = [[sb(f"QS{g}_{i}", [128, 512], BF16) for i in range(2)] for g in range(2)]
        selm2 = [sb(f"selm2{g}", [128, 256]) for g in range(2)]
        gate = sb("gate", [128, 24])
        ocs = [sb(f"ocs{g}", [128, 260]) for g in range(2)]; cmb = sb("cmb", [128, 260]); ows = [sb(f"ows{g}", [128, 260]) for g in range(2)]
        den3 = [sb(f"den3{g}", [128, 12]) for g in range(2)]; cf3 = [sb(f"cf3{g}", [128, 12]) for g in range(2)]
        score = [sb(f"score{g}", [128, 128]) for g in range(2)]; work = sb("work", [128, 128])
        m8a = sb("m8a", [128, 8]); m8b = sb("m8b", [128, 8])
        o_n = sb("o_n", [128, 512]); sgn = sb("sgn", [128, 512])
        mix = sb("mix", [128, D], BF16); mixT = sb("mixT", [128, 8, 128], BF16)
        hbuf = sb("hbuf", [128, D]); hss = sb("hss", [128, 2])
        obuf = [sb("obuf0", [128, D])] * 2
        osem = [kb.dsem(f"os{i}") for i in range(2)]
        out_sems.extend(osem)
        NQv = NQ.rearrange("h d t -> d h t")
        print("sbuf remaining", nc.sbuf_bytes_remaining)
        nS = [0]; nP = [0]; nC = [0]; nTC = [0]
        SB_ = [psum[0], psum[1], psum[2], psum[6]]
        SBK = [SB_]
        ACCA, ACCB, ACCC, MISC = psum[3], psum[4], psum[5], psum[6]
        SL = range(32) if NSL is None else NSL
        for i in SL:
            bi = i % NB2
            t0 = 2 * i * 128
            kb.dma("sp", qp[bi].t[:], NQv[:, :, t0:t0 + 256], R=[scr_b["NQ"]], W=[qp[bi]], sem=slsem[0][bi])
            kb.dma("sp", tm2[bi].t[:, :, 0:512], TM[t0:t0 + 256, 512:1024].rearrange("(a p) c -> p a c", p=128), R=[scr_b["TM"]], W=[tm2[bi]], sem=slsem[1][bi])
            kb.dma("sp", tm2[bi].t[:, :, 512:536], TM[t0:t0 + 256, 1280:1304].rearrange("(a p) c -> p a c", p=128), R=[scr_b["TM"]], W=[tm2[bi]], sem=slsem[1][bi])
            kb.dma("act", mg2[bi].t[:], MIXG[t0:t0 + 256, :].rearrange("(a p) c -> p a c", p=128), R=[scr_b["MIXG"]], W=[mg2[bi]], sem=slsem[2][bi])
            kb.dma("act", xrt[bi].t[:], xr[i * 128:(i + 1) * 128, :], W=[xrt[bi]], sem=slsem[3][bi])
            kt_lo = max(0, 2 * i - 4); nkw = 2 * i + 2 - kt_lo
            for g in range(2):
                kb.dma("sp", kwt[bi].t[:, g, 0:nkw * 128], KVT[3, 64 * g:64 * g + 64, kt_lo * 128:(2 * i + 2) * 128], R=[scr_b["KVT"]], W=[kwt[bi]], sem=slsem[4][bi])
            kb.dma("act", vwst[bi].t[:, 0:nkw, :], TM[kt_lo * 128:(2 * i + 2) * 128, 1152:1280].rearrange("(k p) c -> p k c", p=128), R=[scr_b["TM"]], W=[vwst[bi]], sem=slsem[5][bi])
            w0 = pwc.t[:, 0:1]; w1_ = pwc.t[:, 1:2]
            kb.op(V, lambda e: e.tensor_scalar(out=qtmp.t[:], in0=qp[bi].t[:, :, 0:128], scalar1=w0[0:64, :], scalar2=None, op0=ALU.mult), R=[qp[bi], pwc], W=[qtmp])
            kb.op(V, lambda e: e.scalar_tensor_tensor(out=qT_[bi].t[:], in0=qp[bi].t[:, :, 128:256], scalar=w1_[0:64, :], in1=qtmp.t[:], op0=ALU.mult, op1=ALU.add), R=[qp[bi], pwc, qtmp], W=[qT_[bi]])
            kb.op(V, lambda e: e.tensor_scalar(out=tmq.t[:], in0=tm2[bi].t[:, 0, :], scalar1=w0, scalar2=None, op0=ALU.mult), R=[tm2[bi], pwc], W=[tmq])
            kb.op(V, lambda e: e.scalar_tensor_tensor(out=tmq.t[:], in0=tm2[bi].t[:, 1, :], scalar=w1_, in1=tmq.t[:], op0=ALU.mult, op1=ALU.add), R=[tm2[bi], pwc, tmq], W=[tmq])
            kb.op(V, lambda e: e.tensor_scalar(out=mix.t[:, 0:512], in0=mg2[bi].t[:, 0, :], scalar1=w0, scalar2=None, op0=ALU.mult), R=[mg2[bi], pwc], W=[mix])
            kb.op(V, lambda e: e.scalar_tensor_tensor(out=mix.t[:, 0:512], in0=mg2[bi].t[:, 1, :], scalar=w1_, in1=mix.t[:, 0:512], op0=ALU.mult, op1=ALU.add), R=[mg2[bi], pwc, mix], W=[mix])
            for g in range(2):
                kb.op(P, lambda e, g=g: e.tensor_copy(out=Vw.t[:, 0:nkw, g, 0:64], in_=vwst[bi].t[:, 0:nkw, 64 * g:64 * g + 64]), R=[vwst[bi]], W=[Vw])
            kb.op(A, lambda e: e.activation(out=gate.t[:], in_=tmq.t[:, 512:536], func=EXP, scale=-1.0), R=[tmq], W=[gate])
            kb.op(V, lambda e: e.tensor_scalar(out=gate.t[:], in0=gate.t[:], scalar1=1.0, scalar2=None, op0=ALU.add), R=[gate], W=[gate])
            kb.op(V, lambda e: e.reciprocal(out=gate.t[:], in_=gate.t[:]), R=[gate], W=[gate])
            qT = qT_[bi]

            def stageA(j, g):
                ps_ = SBK[0][nS[0] % len(SBK[0])]; nS[0] += 1
                if j.get("qs") is not None:
                    kb.op(PE, lambda e: e.matmul(ps_.t[:], lhsT=j["lhsT"], rhs=j["qs"].t[:], start=True, stop=True), R=[j["qs"]] + j["lt"], W=[ps_])
                else:
                    rhs = qT.t[:, 4 * g:4 * g + 4, :].rearrange("p a b -> p (a b)")
                    kb.op(PE, lambda e: e.matmul(ps_.t[:], lhsT=j["lhsT"], rhs=rhs, start=True, stop=True), R=[qT] + j["lt"], W=[ps_])
                pt = Pt[nP[0] % 6]; nP[0] += 1
                if j.get("tabdma") is not None:
                    tb_ = tcb[nTC[0] % 2]; ts_ = tcsem[nTC[0] % 2]; nTC[0] += 1
                    kb.dma("sp", tb_.t[:], j["tabdma"], W=[tb_], sem=ts_)
                    j["tab"], j["tt"] = tb_.t[:], [tb_]
                if j.get("tab") is not None:
                    st_ = sbt[nC[0] % 4]; nC[0] += 1
                    kb.op(V, lambda e: e.tensor_tensor(out=st_.t[:], in0=ps_.t[:], in1=j["tab"], op=ALU.add), R=[ps_] + j["tt"], W=[st_])
                    kb.op(A, lambda e: e.activation(out=pt.t[:], in_=st_.t[:], func=EXP), R=[st_], W=[pt])
                else:
                    kb.op(A, lambda e: e.activation(out=pt.t[:], in_=ps_.t[:], func=EXP), R=[ps_], W=[pt])
                j["pt"] = pt

            def stageB(j):
                pt = j["pt"]
                for (acc, v_ap, v_tiles, first, last, width, stride) in j["pv"]:
                    for h in range(4):
                        kb.op(PE, lambda e, h=h: e.matmul(acc.t[:, h * stride:h * stride + width], lhsT=pt.t[:, h * 128:(h + 1) * 128], rhs=v_ap, start=(first and h == 0), stop=(last and h == 3)), R=[pt] + v_tiles, W=[acc])

            def run_jobs(jobs, g, LA=3):
                for idx in range(len(jobs) + LA):
                    if idx < len(jobs):
                        stageA(jobs[idx], g)
                    if idx - LA >= 0:
                        stageB(jobs[idx - LA])

            def front(g):
                    nkt = 2 * i + 2
                    cts = [ct for ct in range(4) if 2 * i + 1 - 16 * ct >= 0]
                    jobs = []
                    for ci, ct in enumerate(cts):
                        j = dict(lhsT=kcT[g].t[:, ct * 128:(ct + 1) * 128], lt=[kcT[g]])
                        if 2 * i - 16 * ct >= 23:
                            j["tab"], j["tt"] = tCf.t[:, g, :], [tCf]
                        else:
                            j["tabdma"] = tabC[g, CIDX[(i, ct)]]
                        f_, l_ = ci == 0, ci == len(cts) - 1
                        j["pv"] = [(ACCA, vca[g].t[:, ct, :], [vca[g]], f_, l_, 65, 65), (ACCB, ovb.t[:, ct, :], [ovb], f_, l_, 128, 128)]
                        jobs.append(j)
                    run_jobs(jobs, g)
                    wk = [kt for kt in range(kt_lo, nkt)]
                    jobs = []
                    for ki, kt in enumerate(wk):
                        m = 2 * i + 1 - kt
                        sl_ = kt - kt_lo
                        tab = tS.t[:, g, m, :] if m <= 3 else tWx.t[:, m - 4, g, :]
                        jobs.append(dict(lhsT=kwt[bi].t[:, g, sl_ * 128:(sl_ + 1) * 128], lt=[kwt[bi]], tab=tab, tt=[tS, tWx],
                                         pv=[(ACCC, Vw.t[:, sl_, g, :], [Vw], ki == 0, ki == len(wk) - 1, 65, 65)]))
                    run_jobs(jobs, g)
                    kb.op(V, lambda e: e.tensor_copy(out=ows[g].t[:], in_=ACCC.t[:, 0:260]), R=[ACCC], W=[ows[g]])
                    kb.op(V, lambda e: e.tensor_copy(out=ocs[g].t[:], in_=ACCA.t[:, 0:260]), R=[ACCA], W=[ocs[g]])
                    o3 = ocs[g].t[:].rearrange("p (h c) -> p h c", c=65)
                    kb.op(V, lambda e: e.tensor_scalar(out=den3[g].t[:, 0:4], in0=o3[:, :, 64], scalar1=1e-30, scalar2=None, op0=ALU.max), R=[ocs[g]], W=[den3[g]])
                    kb.op(V, lambda e: e.reciprocal(out=cf3[g].t[:, 0:4], in_=den3[g].t[:, 0:4]), R=[den3[g]], W=[cf3[g]])
                    kb.op(V, lambda e: e.tensor_scalar(out=score[g].t[:], in0=ACCB.t[:, 0:128], scalar1=cf3[g].t[:, 0:1], scalar2=None, op0=ALU.mult), R=[ACCB, cf3[g]], W=[score[g]])
                    for h in range(1, 4):
                        kb.op(V, lambda e, h=h: e.scalar_tensor_tensor(out=score[g].t[:], in0=ACCB.t[:, h * 128:(h + 1) * 128], scalar=cf3[g].t[:, h:h + 1], in1=score[g].t[:], op0=ALU.mult, op1=ALU.add), R=[ACCB, cf3[g], score[g]], W=[score[g]])
                    a0 = 126 - 4 * i
                    kb.op(V, lambda e: e.tensor_tensor(out=score[g].t[:], in0=score[g].t[:], in1=adjc.t[:, a0:a0 + 128], op=ALU.add), R=[score[g], adjc], W=[score[g]])
                    kb.op(V, lambda e: e.tensor_tensor(out=score[g].t[:, 0:1], in0=score[g].t[:, 0:1], in1=j0c.t[:, i:i + 1], op=ALU.add), R=[score[g], j0c], W=[score[g]])
                    kb.op(V, lambda e: e.max(out=m8a.t[:], in_=score[g].t[:]), R=[score[g]], W=[m8a])
                    kb.op(V, lambda e: e.match_replace(out=work.t[:], in_to_replace=m8a.t[:], in_values=score[g].t[:], imm_value=-3.0e38), R=[score[g], m8a], W=[work])
                    kb.op(V, lambda e: e.max(out=m8b.t[:], in_=work.t[:]), R=[work], W=[m8b])
                    for hf in range(2):
                        kb.op(V, lambda e, hf=hf: e.tensor_scalar(out=selm2[g].t[:, hf * 128:(hf + 1) * 128], in0=score[g].t[:], scalar1=m8b.t[:, 7:8], scalar2=None, op0=ALU.is_ge), R=[score[g], m8b], W=[selm2[g]])
                    ngb = 2 if nkt > 32 else 1
                    kb.op(PE, lambda e: e.transpose(out=MISC.t[:, 0:128], in_=selm2[g].t[:, 64:192], identity=ident), R=[selm2[g], cst], W=[MISC])
                    if ngb == 2:
                        kb.op(PE, lambda e: e.transpose(out=MISC.t[:, 128:256], in_=selm2[g].t[:, 0:128], identity=ident), R=[selm2[g], cst], W=[MISC])
                    for gb in range(ngb):
                        kb.op(P, lambda e, gb=gb: e.tensor_copy(out=QS[g][gb].t[0:64, :], in_=qT.t[:, 4 * g:4 * g + 4, :].rearrange("p a b -> p (a b)")), R=[qT], W=[QS[g][gb]])
                        for h in range(4):
                            kb.op(V, lambda e, h=h, gb=gb: e.tensor_scalar(out=QS[g][gb].t[64:128, h * 128:(h + 1) * 128], in0=MISC.t[64:128, gb * 128:(gb + 1) * 128], scalar1=-1.0, scalar2=-MNEG, op0=ALU.add, op1=ALU.mult), R=[MISC], W=[QS[g][gb]])
                    if g == 0 and ("selm" in dbg_out) and i == DSLOT:
                        dbg_dump("selm", selm2[g].t[:, 0:128], selm2[g])
                        dbg_dump("score", score[g].t[:], score[g])
                        dbg_dump("ocs", ocs[g].t[:], ocs[g])
                        dbg_dump("cf3", cf3[g].t[:], cf3[g])

            def back(g):
                    nkt = 2 * i + 2
                    o3 = ocs[g].t[:].rearrange("p (h c) -> p h c", c=65)
                    far = [kt for kt in range(nkt) if 2 * i + 1 - kt >= 9]
                    near = [kt for kt in range(nkt) if 2 * i + 1 - kt < 9]
                    jobs = []
                    for ki, kt in enumerate(far):
                        jobs.append(dict(lhsT=ksT[g].t[:, kt * 128:(kt + 1) * 128], lt=[ksT[g]], qs=QS[g][kt // 32],
                                         pv=[(ACCB, Vs[g].t[:, kt, :], [Vs[g]], ki == 0, ki == len(far) - 1, 65, 65)]))
                    for ki, kt in enumerate(near):
                        m = 2 * i + 1 - kt
                        jobs.append(dict(lhsT=ksT[g].t[:, kt * 128:(kt + 1) * 128], lt=[ksT[g]], qs=QS[g][kt // 32], tab=tS.t[:, g, m, :], tt=[tS],
                                         pv=[(ACCA, Vs[g].t[:, kt, :], [Vs[g]], ki == 0, ki == len(near) - 1, 65, 65)]))
                    SBK[0] = SB_ + [ACCC]
                    run_jobs(jobs, g, LA=4)
                    SBK[0] = SB_
                    kb.op(V, lambda e: e.tensor_copy(out=cmb.t[:], in_=ACCA.t[:, 0:260]), R=[ACCA], W=[cmb])
                    if far:
                        for h in range(4):
                            kb.op(V, lambda e, h=h, g=g: e.scalar_tensor_tensor(out=cmb.t[:, h * 65:(h + 1) * 65], in0=ACCB.t[:, h * 65:(h + 1) * 65], scalar=ecx.t[:, 4 * g + h:4 * g + h + 1], in1=cmb.t[:, h * 65:(h + 1) * 65], op0=ALU.mult, op1=ALU.add), R=[ACCB, ecx, cmb], W=[cmb])
                    c3 = cmb.t[:].rearrange("p (h c) -> p h c", c=65)
                    w3 = ows[g].t[:].rearrange("p (h c) -> p h c", c=65)
                    kb.op(V, lambda e: e.tensor_scalar(out=den3[g].t[:, 4:8], in0=c3[:, :, 64], scalar1=1e-30, scalar2=None, op0=ALU.max), R=[cmb], W=[den3[g]])
                    kb.op(V, lambda e: e.tensor_scalar(out=den3[g].t[:, 8:12], in0=w3[:, :, 64], scalar1=1e-30, scalar2=None, op0=ALU.max), R=[ows[g]], W=[den3[g]])
                    kb.op(V, lambda e: e.reciprocal(out=cf3[g].t[:], in_=den3[g].t[:]), R=[den3[g]], W=[cf3[g]])
                    gv = gate.t[:].rearrange("p (x h) -> p x h", h=8)[:, :, 4 * g:4 * g + 4]
                    kb.op(V, lambda e, gv=gv: e.tensor_tensor(out=cf3[g].t[:].rearrange("p (x h) -> p x h", h=4), in0=cf3[g].t[:].rearrange("p (x h) -> p x h", h=4), in1=gv, op=ALU.mult), R=[cf3[g], gate], W=[cf3[g]])
                    for h in range(4):
                        oc_ = o_n.t[:, (4 * g + h) * 64:(4 * g + h + 1) * 64]
                        kb.op(V, lambda e, h=h, oc_=oc_: e.tensor_scalar(out=oc_, in0=o3[:, h, 0:64], scalar1=cf3[g].t[:, h:h + 1], scalar2=None, op0=ALU.mult), R=[ocs[g], cf3[g]], W=[o_n])
                        kb.op(V, lambda e, h=h, oc_=oc_: e.scalar_tensor_tensor(out=oc_, in0=c3[:, h, 0:64], scalar=cf3[g].t[:, 4 + h:5 + h], in1=oc_, op0=ALU.mult, op1=ALU.add), R=[cmb, cf3[g], o_n], W=[o_n])
                        kb.op(V, lambda e, h=h, oc_=oc_: e.scalar_tensor_tensor(out=oc_, in0=w3[:, h, 0:64], scalar=cf3[g].t[:, 8 + h:9 + h], in1=oc_, op0=ALU.mult, op1=ALU.add), R=[ows[g], cf3[g], o_n], W=[o_n])

            for g in range(2):
                front(g)
            for g in range(2):
                back(g)
            if ("o_n" in dbg_out) and i == DSLOT:
                dbg_dump("o_n", o_n.t[:], o_n)
            kb.op(A, lambda e: e.activation(out=sgn.t[:], in_=tmq.t[:, 0:512], func=EXP, scale=-1.0), R=[tmq], W=[sgn])
            kb.op(A, lambda e: e.activation(out=sgn.t[:], in_=sgn.t[:], func=AF.Ln, bias=ones_f[:, 0:1], scale=1.0), R=[sgn, cst], W=[sgn])
            kb.op(A, lambda e: e.activation(out=sgn.t[:], in_=sgn.t[:], func=EXP, scale=-1.0), R=[sgn], W=[sgn])
            kb.op(P, lambda e: e.tensor_tensor(out=sgn.t[:], in0=sgn.t[:], in1=tmq.t[:, 0:512], op=ALU.mult), R=[sgn, tmq], W=[sgn])
            kb.op(V, lambda e: e.tensor_tensor(out=mix.t[:, 512:1024], in0=o_n.t[:], in1=sgn.t[:], op=ALU.mult), R=[o_n, sgn], W=[mix])
            for c in range(8):
                kb.op(PE, lambda e, c=c: e.transpose(out=psb.t[:, c * 128:(c + 1) * 128], in_=mix.t[:, c * 128:(c + 1) * 128], identity=identb.t[:]), R=[mix, identb], W=[psb])
            kb.op(A, lambda e: e.copy(out=mixT.t[:].rearrange("p a b -> p (a b)"), in_=psb.t[:]), R=[psb], W=[mixT])
            for half, pY in ((0, MISC), (1, ACCC)):
                for c in range(8):
                    kb.op(PE, lambda e, c=c, half=half, pY=pY: e.matmul(pY.t[:], lhsT=mixT.t[:, c, :], rhs=Wo.t[:, c, half * 512:(half + 1) * 512], start=(c == 0), stop=(c == 7)), R=[mixT, Wo], W=[pY])
                kb.op(V, lambda e, half=half, pY=pY: e.tensor_tensor(out=hbuf.t[:, half * 512:(half + 1) * 512], in0=pY.t[:], in1=xrt[bi].t[:, half * 512:(half + 1) * 512], op=ALU.add), R=[pY, xrt[bi]], W=[hbuf])
            hsq = obuf[i % 2]
            kb.op(A, lambda e: e.activation(out=hsq.t[:], in_=hbuf.t[:], func=AF.Square), R=[hbuf], W=[hsq])
            kb.op(V, lambda e: e.tensor_reduce(out=hss.t[:, 0:1], in_=hsq.t[:], axis=mybir.AxisListType.X, op=ALU.add), R=[hsq], W=[hss])
            kb.op(A, lambda e: e.activation(out=hss.t[:, 0:1], in_=hss.t[:, 0:1], func=AF.Ln, bias=epsc, scale=1.0 / D), R=[hss, cst], W=[hss])
            kb.op(A, lambda e: e.activation(out=hss.t[:, 1:2], in_=hss.t[:, 0:1], func=EXP, scale=-0.5), R=[hss], W=[hss])
            ob = obuf[i % 2]
            kb.op(V, lambda e, ob=ob: e.scalar_tensor_tensor(out=ob.t[:], in0=hbuf.t[:], scalar=hss.t[:, 1:2], in1=fnw.t[:], op0=ALU.mult, op1=ALU.mult), R=[hbuf, hss, fnw], W=[ob])
            kb.dma("pool", y[i * 128:(i + 1) * 128, :], ob.t[:], R=[ob], sem=osem[i % 2])
        kb.barrier()
        stk[0].close()
        stk[0] = None

    kb.final_wait(out_sems)
    print('KB nops', kb.nops, kb.cnt)
    return nc, list(dbg_out.keys())


PERM = None


def _perm():
    gq, gk, gv, gz, gb, ga, nq, kc, vc, ks, vs, kw, vw, ng, nz = [np.arange(a, b) for a, b in zip(
        np.cumsum([0, 512, 512, 512, 512, 4, 4, 512, 128, 128, 128, 128, 128, 128, 24]),
        np.cumsum([512, 512, 512, 512, 4, 4, 512, 128, 128, 128, 128, 128, 128, 24, 512]))]
    return np.concatenate([gq, gk, gv, nq, kc, vc, ks, kw, gz, nz, vs, vw, ng, gb, ga])


def _consts():
    c = np.zeros((128, 1024), np.float32)
    j = np.arange(128)[:, None]; i = np.arange(128)[None, :]
    same = (j // 64) == (i // 64)
    c[:, 0:128] = np.eye(128)
    c[:, 128:256] = (same & (j < i))
    c[:, 256:384] = (same & (j <= i))
    c[:, 384:512] = same
    c[0, 512:640] = 1.0
    c[64, 640:768] = 1.0
    c[:, 768:772] = 1.0
    c[127, 771] = 0.0
    c[:, 772:900] = 1.0
    c[:, 900] = EPS
    return c


def _bucket(dist):
    n = np.maximum(dist, 0)
    large = 16 + (np.log(np.maximum(n, 16).astype(np.float32) / np.float32(16)) / np.float32(np.log(1024.0 / 16.0)) * np.float32(16)).astype(np.int32)
    return np.where(n < 16, n, np.minimum(large, 31)).astype(np.int64)


def _gather_bias(rel_ext, dist, valid, g):
    idx = np.where(valid, _bucket(dist), 32)
    t = rel_ext[idx][:, :, 4 * g:4 * g + 4]
    return np.ascontiguousarray(t.transpose(0, 2, 1).reshape(128, 512))


def prep_shared(inputs):
    sh = {}
    sh["w_in"] = np.ascontiguousarray(np.asarray(inputs["w_in"], np.float32)[0][:, _perm()])
    sh["nw"] = np.ascontiguousarray(np.asarray(inputs["norm_w"], np.float32)[0].reshape(8, 128).T)
    cwv = np.asarray(inputs["conv_w"], np.float32)[0]
    sh["convw"] = np.ascontiguousarray(cwv.reshape(4, 12, 128).transpose(2, 1, 0))
    sh["alog_b"] = np.ascontiguousarray(np.broadcast_to(np.tile(np.asarray(inputs["a_log"], np.float32)[0], 64)[None, :], (128, 256)))
    sh["dtb_b"] = np.ascontiguousarray(np.broadcast_to(np.tile(np.asarray(inputs["dt_bias"], np.float32)[0], 64)[None, :], (128, 256)))
    sh["gnw_b"] = np.ascontiguousarray(np.broadcast_to(np.tile(np.asarray(inputs["gdn_norm_w"], np.float32)[0], 4)[None, :], (128, 512)))
    sh["fnw_b"] = np.ascontiguousarray(np.broadcast_to(np.asarray(inputs["final_norm_w"], np.float32)[None, :], (128, D)))
    sh["w_out"] = np.ascontiguousarray(np.asarray(inputs["w_out"], np.float32)[0])
    for nm, k1, kp, k2 in (("k", "cmp_k_w1", "cmp_pe_k", "cmp_k_w2"), ("v", "cmp_v_w1", "cmp_pe_v", "cmp_v_w2")):
        sh["w1" + nm] = np.ascontiguousarray(np.asarray(inputs[k1], np.float32)[0].transpose(1, 0, 2))
        sh["pe" + nm] = np.ascontiguousarray(np.asarray(inputs[kp], np.float32)[0].T)
        sh["w2" + nm] = np.ascontiguousarray(np.asarray(inputs[k2], np.float32)[0])
    sh["consts"] = _consts()
    rel = np.asarray(inputs["rel_bias"], np.float32)
    rel_ext = np.concatenate([rel, np.full((1, 8), NEG, np.float32)], axis=0)
    sh["cexp"] = np.ascontiguousarray(np.broadcast_to(rel[31][None, :], (128, 8)))
    sh["tabCfar"] = np.ascontiguousarray(np.broadcast_to(np.repeat(rel[31].reshape(2, 4), 128, axis=1)[:, None, :], (2, 128, 512)))
    n_in = np.arange(128)[:, None, None]; ct = np.arange(4)[None, :, None]; j = np.arange(128)[None, None, :]
    n = 128 * ct + n_in
    sh["ovl"] = ((n <= 510) & (16 * n < 64 * j + 64) & (16 * n + 32 > 64 * j)).astype(np.float32)
    kk = np.arange(S)[None, :]
    sh["estk"] = (((kk // 64) % 64) == np.arange(64)[:, None]).astype(np.float32)
    ik = np.arange(128)[:, None]; iq = np.arange(128)[None, :]
    par = []
    for a in range(2):
        p = {}
        tS_ = np.empty((2, 9, 128, 512), np.float32)
        tW_ = np.empty((2, 2, 128, 512), np.float32)
        for g in range(2):
            for m in range(9):
                d = m + a - 1
                dist = 128 * d + iq - ik
                tS_[g, m] = _gather_bias(rel_ext, dist, dist >= 0, g)
            for mi in range(2):
                d = 4 + mi + a - 1
                dist = 128 * d + iq - ik
                tW_[mi, g] = _gather_bias(rel_ext, dist, (dist >= 0) & (dist < 512), g)
        p["tabS"] = tS_; p["tabWx"] = tW_
        tC_ = np.empty((2, len(CIDX), 128, 512), np.float32)
        for (i, ct_), idx in CIDX.items():
            e = 2 * i + a - 16 * ct_
            dist = 128 * e + iq - 16 * ik - 31
            for g in range(2):
                tC_[g, idx] = _gather_bias(rel_ext, dist, dist >= 0, g)
        p["tabC"] = tC_
        x_ = np.arange(256)[None, :]; hh = (np.arange(128) // 64)[:, None]
        relj = (x_ - 126) - 2 * a
        adj = np.where(relj > hh, np.float32(NEG), np.where((relj == hh) | (relj == hh - 1), np.float32(1e4), np.float32(0.0)))
        p["adjT"] = np.ascontiguousarray(adj.astype(np.float32))
        j0 = np.zeros((128, 32), np.float32)
        for i in range(32):
            if 2 * i + a >= 1:
                j0[:, i] = 1e4
        p["j0b"] = j0
        p["pw"] = np.ascontiguousarray(np.broadcast_to(np.array([1.0 - a, float(a)], np.float32)[None, :], (128, 2)))
        par.append(p)
    return sh, par


def prep_inputs(inputs, core, sh=None, par=None):
    if sh is None:
        sh, par = prep_shared(inputs)
    b = core // 2
    a = core % 2
    x = np.asarray(inputs["x"], np.float32)[b]
    m = dict(sh)
    m.update(par[a])
    m["xT"] = np.ascontiguousarray(x.T)
    m["xr"] = np.ascontiguousarray(x.reshape(64, 128, D)[a::2].reshape(S // 2, D))
    return m


_NC = [None]


def kernel(**inputs):
    if _NC[0] is None:
        _NC[0] = build()[0]
    nc = _NC[0]
    sh, par = prep_shared(inputs)
    in_maps = [prep_inputs(inputs, c, sh, par) for c in range(8)]
    res = run_bass_kernel_spmd(nc, in_maps, core_ids=list(range(8)))
    out = np.empty((4, S, D), np.float32)
    for c in range(8):
        b, a = c // 2, c % 2
        yc = np.asarray(res.results[c]["y"], np.float32).reshape(32, 128, D)
        out[b].reshape(64, 128, D)[a::2] = yc
    return out
```

```python
import numpy as np
import concourse.bass as bass
import concourse.mybir as mybir
from concourse.bass_utils import run_bass_kernel_spmd

F32 = mybir.dt.float32
BF16 = mybir.dt.bfloat16
ALU = mybir.AluOpType
AF = mybir.ActivationFunctionType

S = 8192
NT = 64
D = 1024
EPS = 1e-6
NEG = -1e30
MNEG = -30000.0


class Buf:
    __slots__ = ("w", "r", "bank")

    def __init__(self, bank=None):
        self.w = None
        self.r = {}
        self.bank = bank


class Tile:
    def __init__(self, t):
        self.t = t
        self.b = Buf()


class DSem:
    def __init__(self, nc, name):
        self.h = nc.alloc_semaphore(name)
        self.count = 0
        self.key = ("dma", name)


class KB:
    def __init__(self, nc):
        self.nc = nc
        self.eng = {"pe": nc.tensor, "dve": nc.vector, "act": nc.scalar, "pool": nc.gpsimd, "sp": nc.sync}
        self.semh = {}
        self.cnt = {}
        self.seen = {}
        for e in self.eng:
            self.semh[("eng", e)] = nc.alloc_semaphore("c_" + e)
            self.cnt[e] = 0
            self.seen[e] = {}
        self.dsems = []
        self.nd = 0
        self.nops = 0
        self.pe_bank_excl = True
        import os
        self.limit = int(os.environ.get("KBLIMIT", "1000000000"))

    def dsem(self, name=None):
        self.nd += 1
        d = DSem(self.nc, name or f"d{self.nd}")
        self.semh[d.key] = d.h
        self.dsems.append(d)
        return d

    def _wait(self, e, dep):
        key, val = dep
        if key == ("eng", "pe") and e == "pe":
            return
        if self.seen[e].get(key, 0) >= val:
            return
        self.eng[e].wait_ge(self.semh[key], val)
        self.seen[e][key] = val

    def _deps(self, e, R, W):
        for b in R:
            if b.w is not None:
                self._wait(e, b.w)
        for b in W:
            if b.w is not None:
                self._wait(e, b.w)
            for d in b.r.values():
                self._wait(e, d)

    def op(self, e, fn, R=(), W=()):
        self.nops += 1
        if self.nops > self.limit:
            return
        R = [x.b if isinstance(x, Tile) else x for x in R]
        W = [x.b if isinstance(x, Tile) else x for x in W]
        self._deps(e, R, W)
        if e != "pe":
            for b in R:
                if b.bank is not None:
                    for e2, tk in b.bank.items():
                        if e2 != e:
                            self._wait(e, tk)
        elif self.pe_bank_excl:
            for b in W:
                if b.bank is not None:
                    for e2, tk in b.bank.items():
                        self._wait(e, tk)
        ins = fn(self.eng[e])
        self.cnt[e] += 1
        ins.then_inc(self.semh[("eng", e)], 1)
        tok = (("eng", e), self.cnt[e])
        for b in R:
            b.r[tok[0]] = tok
            if b.bank is not None and e != "pe":
                b.bank[e] = tok
        for b in W:
            b.w = tok
            b.r = {}

    def dma(self, q, out, in_, R=(), W=(), sem=None):
        self.nops += 1
        if self.nops > self.limit:
            return
        R = [x.b if isinstance(x, Tile) else x for x in R]
        W = [x.b if isinstance(x, Tile) else x for x in W]
        self._deps(q, R, W)
        ins = self.eng[q].dma_start(out=out, in_=in_)
        sem.count += 16
        ins.then_inc(sem.h, 16)
        tok = (sem.key, sem.count)
        for b in R:
            b.r[tok[0]] = tok
        for b in W:
            b.w = tok
            b.r = {}

    def seal(self, sem, tiles):
        for t in tiles:
            b = t.b if isinstance(t, Tile) else t
            b.w = (sem.key, sem.count)

    def barrier(self):
        for e in self.eng:
            for e2 in self.eng:
                if e2 != e and self.cnt[e2] > 0:
                    self._wait(e, (("eng", e2), self.cnt[e2]))
            for d in self.dsems:
                if d.count > 0:
                    self._wait(e, (d.key, d.count))

    def final_wait(self, sems):
        for d in sems:
            if d.count > 0:
                self._wait("sp", (d.key, d.count))


CIDX = {}
for _ct in range(4):
    for _i in range(32):
        if 2 * _i + 1 - 16 * _ct >= 0 and 2 * _i - 16 * _ct < 23:
            CIDX[(_i, _ct)] = len(CIDX)


def build(PH=99, DBG=(), NBLK=None, GT=None, NSL=None, DSLOT=-1):
    nc = bass.Bass("TRN2", target_bir_lowering=False)
    try:
        nc.allow_low_precision("bf16 matmul operands, fp32 accumulation")
        nc.allow_non_contiguous_dma("strided scratch layouts")
    except Exception:
        pass
    kb = KB(nc)
    V, A, P, PE = "dve", "act", "pool", "pe"

    def din(name, shape, dt=F32):
        return nc.dram_tensor(name, list(shape), dt, kind="ExternalInput").ap()

    def dscr(name, shape, dt):
        return nc.dram_tensor(name, list(shape), dt).ap()

    from contextlib import ExitStack
    stk = [None]

    def sb(name, shape, dt=F32):
        if stk[0] is None:
            return Tile(nc.alloc_sbuf_tensor(name, list(shape), dt))
        return Tile(stk[0].enter_context(nc.sbuf_tensor(name, list(shape), dt)))

    xT = din("xT", [D, S])
    xr = din("xr", [S // 2, D])
    w_in = din("w_in", [D, 3872])
    nw = din("nw", [128, 8])
    convw = din("convw", [128, 12, 4])
    alog_b = din("alog_b", [128, 256])
    dtb_b = din("dtb_b", [128, 256])
    gnw_b = din("gnw_b", [128, 512])
    fnw_b = din("fnw_b", [128, D])
    w_out = din("w_out", [D, D])
    w1k = din("w1k", [64, 32, 128]); w1v = din("w1v", [64, 32, 128])
    pek = din("pek", [64, 32]); pev = din("pev", [64, 32])
    w2k = din("w2k", [128, 64]); w2v = din("w2v", [128, 64])
    tabS = din("tabS", [2, 9, 128, 512])
    tabWx = din("tabWx", [2, 2, 128, 512])
    tabCfar = din("tabCfar", [2, 128, 512])
    tabC = din("tabC", [2, 44, 128, 512])
    cexp = din("cexp", [128, 8])
    ovl = din("ovl", [128, 4, 128])
    adjT = din("adjT", [128, 256])
    j0b = din("j0b", [128, 32])
    pw = din("pw", [128, 2])
    consts = din("consts", [128, 1024])
    estk = din("estk", [64, S])
    y = nc.dram_tensor("y", [S // 2, D], F32, kind="ExternalOutput").ap()
    dbg_out = {}
    for name, shape, dt in DBG:
        dbg_out[name] = nc.dram_tensor("dbg_" + name, list(shape), dt, kind="ExternalOutput").ap()

    GQKV = dscr("GQKV", [12, 128, S], BF16)
    NQ = dscr("NQ", [8, 64, S], BF16)
    KVT = dscr("KVT", [4, 128, S], BF16)
    TM = dscr("TM", [S, 1312], F32)
    MIXG = dscr("MIXG", [S, 512], BF16)
    scr_b = {n: Buf() for n in ("GQKV", "NQ", "KVT", "TM", "MIXG")}

    cst = sb("cst", [128, 1024])
    ident = cst.t[:, 0:128]
    mask2 = cst.t[:, 128:384]
    tri2 = cst.t[:, 256:384]
    blk2 = cst.t[:, 384:512]
    sel0 = cst.t[:, 512:640]
    sel1 = cst.t[:, 640:768]
    nvalid = cst.t[:, 768:772]
    ones_f = cst.t[:, 772:900]
    epsc = cst.t[:, 900:901]
    identb = sb("identb", [128, 128], BF16)
    onesb = sb("onesb", [128, 128], BF16)
    gba = sb("gba", [128, 64, 8])
    ld = kb.dsem("ld0")
    kb.dma("sp", cst.t[:], consts, W=[cst], sem=ld)
    kb.op(V, lambda e: e.tensor_copy(out=identb.t[:], in_=ident), R=[cst], W=[identb])
    kb.op(V, lambda e: e.tensor_copy(out=onesb.t[:], in_=ones_f), R=[cst], W=[onesb])

    psum = [Tile(nc.alloc_psum_tensor(f"ps{i}", [128, 512], F32)) for i in range(7)]
    psb = Tile(nc.alloc_psum_tensor("psb", [128, 1024], BF16))
    for p_ in psum + [psb]:
        p_.b.bank = {}
    out_sems = []

    def dbg_dump(name, ap_src, tile, rows=128):
        if name in dbg_out:
            d = kb.dsem()
            out_sems.append(d)
            kb.dma("sp", dbg_out[name], ap_src, R=[tile], sem=d)

    stk[0] = ExitStack()
    Wb = sb("Wb", [128, 8, 3872], BF16)
    cw = sb("cw", [128, 12, 4])
    nwt = sb("nwt", [128, 8])
    ld1 = kb.dsem("ld1")
    kb.dma("sp", cw.t[:], convw, W=[cw], sem=ld1)
    kb.dma("sp", nwt.t[:], nw, W=[nwt], sem=ld1)
    kb.seal(ld1, [cw, nwt])
    ph1stk = stk[0]
    stk[0] = ExitStack()
    wst = [sb(f"wst{i}", [128, 3872]) for i in range(2)]
    wsem = [kb.dsem(f"wl{i}") for i in range(2)]
    w_v = w_in.rearrange("(kc p) c -> p kc c", p=128)
    for kc in range(8):
        st = wst[kc % 2]
        kb.dma("sp" if kc % 2 == 0 else "act", st.t[:], w_v[:, kc, :], W=[st], sem=wsem[kc % 2])
        half = 1936
        kb.op(V, lambda e, st=st, kc=kc: e.tensor_scalar(out=Wb.t[:, kc, 0:half], in0=st.t[:, 0:half], scalar1=nwt.t[:, kc:kc + 1], scalar2=None, op0=ALU.mult), R=[st, nwt], W=[Wb])
        kb.op(A, lambda e, st=st, kc=kc: e.activation(out=Wb.t[:, kc, half:3872], in_=st.t[:, half:3872], func=AF.Copy, scale=nwt.t[:, kc:kc + 1]), R=[st, nwt], W=[Wb])

    kb.barrier()
    stk[0].close()
    stk[0] = ph1stk
    xf = [sb(f"xf{i}", [128, 8, 512]) for i in range(2)]
    xsem = [kb.dsem(f"xl{i}") for i in range(2)]
    xb = sb("xb", [128, 8, 512], BF16)
    sq = sb("sq", [128, 8, 512], BF16)
    rbc = sb("rbc", [128, 512])
    cbuf = [sb(f"cbuf{i}", [128, 515]) for i in range(12)]
    for c in cbuf:
        kb.op(P, lambda e, c=c: e.memset(c.t[:, 0:3], 0.0), W=[c])
    acc = [sb(f"acc{i}", [128, 512]) for i in range(4)]
    slb = [sb(f"slb{i}", [128, 512]) for i in range(4)]
    sq2 = [sb(f"sq2{i}", [128, 512]) for i in range(4)]
    rn = [sb(f"rn{i}", [128, 512]) for i in range(4)]
    ptmp = sb("ptmp", [128, 512])
    stg = [sb(f"stg{i}", [128, 512], BF16) for i in range(4)]
    stsem = [kb.dsem(f"st{i}") for i in range(4)]
    tmb = [sb(f"tmb{i}", [128, 1312]) for i in range(2)]
    tmsem = [kb.dsem(f"tm{i}") for i in range(2)]
    rcol = [sb(f"rcol{i}", [128, 2]) for i in range(2)]
    xT_v = xT.rearrange("(kc p) t -> p kc t", p=128)
    nstg = [0]

    def store_stage(src_tile_fn, dram_aps):
        i = nstg[0] % 4
        nstg[0] += 1
        st = stg[i]
        src_tile_fn(st)
        for (dap, p0, p1, bkey) in dram_aps:
            kb.dma("pool", dap, st.t[p0:p1, :], R=[st], sem=stsem[i])

    NB = (16 if PH >= 1 else 0) if NBLK is None else NBLK
    for tb in range(NB):
        t0 = tb * 512
        xt = xf[tb % 2]
        kb.dma("sp", xt.t[:], xT_v[:, :, t0:t0 + 512], W=[xt], sem=xsem[tb % 2])
        kb.op(V, lambda e: e.tensor_copy(out=xb.t[:, 0:4, :], in_=xt.t[:, 0:4, :]), R=[xt], W=[xb])
        kb.op(V, lambda e: e.tensor_copy(out=xb.t[:, 4:8, :], in_=xt.t[:, 4:8, :]), R=[xt], W=[xb])
        kb.op(A, lambda e: e.activation(out=sq.t[:], in_=xt.t[:], func=AF.Square), R=[xt], W=[sq])
        pr = psum[0]
        for kc in range(8):
            kb.op(PE, lambda e, kc=kc: e.matmul(pr.t[:], lhsT=onesb.t[:], rhs=sq.t[:, kc, :], start=(kc == 0), stop=(kc == 7)), R=[onesb, sq], W=[pr])
        kb.op(A, lambda e: e.activation(out=rbc.t[:], in_=pr.t[:], func=AF.Ln, bias=epsc, scale=1.0 / D), R=[pr, cst], W=[rbc])
        kb.op(A, lambda e: e.activation(out=rbc.t[:], in_=rbc.t[:], func=AF.Exp, scale=-0.5), R=[rbc], W=[rbc])
        pending = []
        for m in range(20):
            pm = psum[(1, 2, 3, 0)[m % 4]]
            for kc in range(8):
                kb.op(PE, lambda e, kc=kc, m=m, pm=pm: e.matmul(pm.t[:], lhsT=Wb.t[:, kc, m * 128:(m + 1) * 128], rhs=xb.t[:, kc, :], start=(kc == 0), stop=(kc == 7)), R=[Wb, xb], W=[pm])
            while len(pending) > 2:
                pending.pop(0)()
            if m < 12:
                cb = cbuf[m]
                ce = V
                a_ = acc[m % 4]; s_ = slb[m % 4]
                kb.op(V, lambda e, pm=pm, cb=cb: e.tensor_tensor(out=cb.t[:, 3:515], in0=pm.t[:], in1=rbc.t[:], op=ALU.mult), R=[pm, rbc], W=[cb])
                kb.op(ce, lambda e, cb=cb, a_=a_, m=m: e.tensor_scalar(out=a_.t[:], in0=cb.t[:, 3:515], scalar1=cw.t[:, m, 3:4], scalar2=None, op0=ALU.mult), R=[cb, cw], W=[a_])
                for j in (2, 1, 0):
                    if ce == V:
                        kb.op(ce, lambda e, cb=cb, a_=a_, m=m, j=j: e.scalar_tensor_tensor(out=a_.t[:], in0=cb.t[:, j:j + 512], scalar=cw.t[:, m, j:j + 1], in1=a_.t[:], op0=ALU.mult, op1=ALU.add), R=[cb, cw, a_], W=[a_])
                    else:
                        kb.op(ce, lambda e, cb=cb, m=m, j=j: e.tensor_scalar(out=ptmp.t[:], in0=cb.t[:, j:j + 512], scalar1=cw.t[:, m, j:j + 1], scalar2=None, op0=ALU.mult), R=[cb, cw], W=[ptmp])
                        kb.op(ce, lambda e, a_=a_: e.tensor_tensor(out=a_.t[:], in0=a_.t[:], in1=ptmp.t[:], op=ALU.add), R=[ptmp, a_], W=[a_])
                kb.op(P, lambda e, cb=cb: e.tensor_copy(out=cb.t[:, 0:3], in_=cb.t[:, 512:515]), R=[cb], W=[cb])
                if m < 8:
                    kb.op(A, lambda e, a_=a_, s_=s_: e.activation(out=s_.t[:], in_=a_.t[:], func=AF.Silu), R=[a_], W=[s_])
                    q2 = sq2[m % 4]; r2 = rn[m % 4]
                    kb.op(A, lambda e, s_=s_, q2=q2: e.activation(out=q2.t[:], in_=s_.t[:], func=AF.Square), R=[s_], W=[q2])
                    pn = psum[4 + (m % 2)]

                    def fin(m=m, s_=s_, q2=q2, r2=r2, pn=pn, t0=t0):
                        kb.op(PE, lambda e: e.matmul(pn.t[:], lhsT=ones_f, rhs=q2.t[:], start=True, stop=True), R=[cst, q2], W=[pn])
                        kb.op(A, lambda e: e.activation(out=r2.t[:], in_=pn.t[:], func=AF.Ln, bias=epsc, scale=1.0), R=[pn, cst], W=[r2])
                        kb.op(A, lambda e: e.activation(out=r2.t[:], in_=r2.t[:], func=AF.Exp, scale=-0.5), R=[r2], W=[r2])
                        scl = (128.0 ** -0.5) if m < 4 else 1.0
                        store_stage(lambda st: kb.op(V, lambda e: e.scalar_tensor_tensor(out=st.t[:], in0=s_.t[:], scalar=scl, in1=r2.t[:], op0=ALU.mult, op1=ALU.mult), R=[s_, r2], W=[st]),
                                    [(GQKV[m, :, t0:t0 + 512], 0, 128, "GQKV")])
                    pending.append(fin)
                else:
                    store_stage(lambda st, a_=a_: kb.op(A, lambda e: e.activation(out=st.t[:], in_=a_.t[:], func=AF.Silu), R=[a_], W=[st]),
                                [(GQKV[m, :, t0:t0 + 512], 0, 128, "GQKV")])
            elif m < 16:
                hp = m - 12
                store_stage(lambda st, pm=pm: kb.op(V, lambda e: e.scalar_tensor_tensor(out=st.t[:], in0=pm.t[:], scalar=0.125, in1=rbc.t[:], op0=ALU.mult, op1=ALU.mult), R=[pm, rbc], W=[st]),
                            [(NQ[2 * hp, :, t0:t0 + 512], 0, 64, "NQ"), (NQ[2 * hp + 1, :, t0:t0 + 512], 64, 128, "NQ")])
            else:
                kv = m - 16
                store_stage(lambda st, pm=pm: kb.op(V, lambda e: e.tensor_tensor(out=st.t[:], in0=pm.t[:], in1=rbc.t[:], op=ALU.mult), R=[pm, rbc], W=[st]),
                            [(KVT[kv, :, t0:t0 + 512], 0, 128, "KVT")])
        while pending:
            pending.pop(0)()
        for sub in range(4):
            tt = tb * 4 + sub
            tm_ = tmb[tt % 2]
            rc = rcol[tt % 2]
            pc = psum[6]
            kb.op(PE, lambda e, sub=sub: e.matmul(pc.t[:, 0:1], lhsT=rbc.t[0:1, sub * 128:(sub + 1) * 128], rhs=ones_f[0:1, 0:1], start=True, stop=True), R=[rbc, cst], W=[pc])
            kb.op(A, lambda e, rc=rc: e.copy(out=rc.t[:, 1:2], in_=pc.t[:, 0:1]), R=[pc], W=[rc])
            for part, (c0, c1) in enumerate(((0, 512), (512, 1024), (1024, 1312))):
                pm = psum[((1, 2, 3), (0, 4, 5))[sub % 2][part]]
                for kc in range(8):
                    kb.op(PE, lambda e, kc=kc, sub=sub, pm=pm, c0=c0, c1=c1: e.matmul(pm.t[:, 0:c1 - c0], lhsT=xb.t[:, kc, sub * 128:(sub + 1) * 128], rhs=Wb.t[:, kc, 2560 + c0:2560 + c1], start=(kc == 0), stop=(kc == 7)), R=[Wb, xb], W=[pm])
                if part < 2:
                    kb.op(A, lambda e, pm=pm, c0=c0, c1=c1, tm_=tm_, rc=rc: e.activation(out=tm_.t[:, c0:c1], in_=pm.t[:, 0:c1 - c0], func=AF.Copy, scale=rc.t[:, 1:2]), R=[pm, rc], W=[tm_])
                else:
                    kb.op(V, lambda e, pm=pm, c0=c0, c1=c1, tm_=tm_, rc=rc: e.tensor_scalar(out=tm_.t[:, c0:c1], in0=pm.t[:, 0:c1 - c0], scalar1=rc.t[:, 1:2], scalar2=None, op0=ALU.mult), R=[pm, rc], W=[tm_])
            kb.op(P, lambda e, tm_=tm_, tt=tt: e.tensor_copy(out=gba.t[:, tt, :], in_=tm_.t[:, 1304:1312]), R=[tm_], W=[gba])
            kb.dma("pool", TM[tt * 128:(tt + 1) * 128, :], tm_.t[:], R=[tm_], sem=tmsem[tt % 2])

    if "gba" in dbg_out:
        dbg_dump("gba", gba.t[:].rearrange("p a b -> p (a b)"), gba)
    kb.barrier()
    stk[0].close()
    stk[0] = None
    for name in ("GQKV", "NQ", "KVT", "TM"):
        if name in dbg_out:
            src = {"GQKV": GQKV, "NQ": NQ, "KVT": KVT, "TM": TM}[name]
            d = kb.dsem(); out_sems.append(d)
            if name == "TM":
                for c in range(64):
                    kb.dma("sp", dbg_out[name][c * 128:(c + 1) * 128, :], src[c * 128:(c + 1) * 128, :], sem=d)
            else:
                for c in range(src.shape[0]):
                    kb.dma("sp", dbg_out[name][c], src[c], sem=d)


    def pq(bank, q0, q1):
        return psum[bank].t[:, q0 * 128:q1 * 128]

    pqb = [[Buf(bank=psum[i].b.bank) for _ in range(4)] for i in range(7)]
    if PH >= 2:
        stk[0] = ExitStack()
        sc = {}
        for nm in ("e1", "d1", "beta", "lnb", "z", "sp", "Aex", "g", "gc", "gl", "yy", "egc", "bg", "kd", "EGL0", "EGL1", "dtb", "alg"):
            sc[nm] = sb("sc_" + nm, [128, 256])
        gbv = gba.t[:, :, 0:4]
        gav = gba.t[:, :, 4:8]

        def v3(tl):
            return tl.t[:].rearrange("p (a b) -> p a b", b=4)

        ld2 = kb.dsem("ld2")
        kb.dma("sp", sc["dtb"].t[:], dtb_b, W=[sc["dtb"]], sem=ld2)
        kb.dma("sp", sc["alg"].t[:], alog_b, W=[sc["alg"]], sem=ld2)
        kb.seal(ld2, [sc["dtb"], sc["alg"]])
        kb.op(A, lambda e: e.activation(out=v3(sc["e1"]), in_=gbv, func=AF.Exp, scale=-1.0), R=[gba], W=[sc["e1"]])
        kb.op(V, lambda e: e.tensor_scalar(out=sc["d1"].t[:], in0=sc["e1"].t[:], scalar1=1.0, scalar2=None, op0=ALU.add), R=[sc["e1"]], W=[sc["d1"]])
        kb.op(V, lambda e: e.reciprocal(out=sc["beta"].t[:], in_=sc["d1"].t[:]), R=[sc["d1"]], W=[sc["beta"]])
        kb.op(A, lambda e: e.activation(out=sc["lnb"].t[:], in_=sc["d1"].t[:], func=AF.Ln), R=[sc["d1"]], W=[sc["lnb"]])
        kb.op(V, lambda e: e.tensor_tensor(out=v3(sc["z"]), in0=gav, in1=v3(sc["dtb"]), op=ALU.add), R=[gba, sc["dtb"]], W=[sc["z"]])
        kb.op(A, lambda e: e.activation(out=sc["z"].t[:], in_=sc["z"].t[:], func=AF.Exp), R=[sc["z"]], W=[sc["z"]])
        kb.op(V, lambda e: e.tensor_scalar(out=sc["z"].t[:], in0=sc["z"].t[:], scalar1=1.0, scalar2=None, op0=ALU.add), R=[sc["z"]], W=[sc["z"]])
        kb.op(A, lambda e: e.activation(out=sc["sp"].t[:], in_=sc["z"].t[:], func=AF.Ln), R=[sc["z"]], W=[sc["sp"]])
        kb.op(A, lambda e: e.activation(out=sc["Aex"].t[:], in_=sc["alg"].t[:], func=AF.Exp), R=[sc["alg"]], W=[sc["Aex"]])
        kb.op(V, lambda e: e.scalar_tensor_tensor(out=sc["g"].t[:], in0=sc["sp"].t[:], scalar=-1.0, in1=sc["Aex"].t[:], op0=ALU.mult, op1=ALU.mult), R=[sc["sp"], sc["Aex"]], W=[sc["g"]])
        p0 = psum[0]
        kb.op(PE, lambda e: e.matmul(p0.t[:, 0:256], lhsT=tri2, rhs=sc["g"].t[:], start=True, stop=True), R=[cst, sc["g"]], W=[p0])
        kb.op(PE, lambda e: e.matmul(p0.t[:, 256:512], lhsT=blk2, rhs=sc["g"].t[:], start=True, stop=True), R=[cst, sc["g"]], W=[p0])
        kb.op(V, lambda e: e.tensor_copy(out=sc["gc"].t[:], in_=p0.t[:, 0:256]), R=[p0], W=[sc["gc"]])
        kb.op(V, lambda e: e.tensor_copy(out=sc["gl"].t[:], in_=p0.t[:, 256:512]), R=[p0], W=[sc["gl"]])
        kb.op(V, lambda e: e.tensor_tensor(out=sc["yy"].t[:], in0=sc["gc"].t[:], in1=sc["lnb"].t[:], op=ALU.subtract), R=[sc["gc"], sc["lnb"]], W=[sc["yy"]])
        kb.op(A, lambda e: e.activation(out=sc["egc"].t[:], in_=sc["gc"].t[:], func=AF.Exp), R=[sc["gc"]], W=[sc["egc"]])
        kb.op(V, lambda e: e.tensor_tensor(out=sc["bg"].t[:], in0=sc["beta"].t[:], in1=sc["egc"].t[:], op=ALU.mult), R=[sc["beta"], sc["egc"]], W=[sc["bg"]])
        kb.op(V, lambda e: e.tensor_tensor(out=sc["kd"].t[:], in0=sc["gl"].t[:], in1=sc["gc"].t[:], op=ALU.subtract), R=[sc["gl"], sc["gc"]], W=[sc["kd"]])
        kb.op(A, lambda e: e.activation(out=sc["kd"].t[:], in_=sc["kd"].t[:], func=AF.Exp), R=[sc["kd"]], W=[sc["kd"]])
        p1 = psum[1]
        kb.op(PE, lambda e: e.matmul(p1.t[:, 0:256], lhsT=sel0, rhs=sc["gl"].t[:], start=True, stop=True), R=[cst, sc["gl"]], W=[p1])
        kb.op(PE, lambda e: e.matmul(p1.t[:, 256:512], lhsT=sel1, rhs=sc["gl"].t[:], start=True, stop=True), R=[cst, sc["gl"]], W=[p1])
        kb.op(A, lambda e: e.activation(out=sc["EGL0"].t[:], in_=p1.t[:, 0:256], func=AF.Exp), R=[p1], W=[sc["EGL0"]])
        kb.op(A, lambda e: e.activation(out=sc["EGL1"].t[:], in_=p1.t[:, 256:512], func=AF.Exp), R=[p1], W=[sc["EGL1"]])
        for nm in ("g", "beta", "gc"):
            dbg_dump("sc_" + nm, sc[nm].t[:], sc[nm])
        kb.barrier()

        kqv = [sb(f"kqv{i}", [128, 12, 128], BF16) for i in range(2)]
        kqsem = [kb.dsem(f"kq{i}") for i in range(2)]
        gzt = [sb(f"gzt{i}", [128, 512]) for i in range(2)]
        gzsem = [kb.dsem(f"gz{i}") for i in range(2)]
        gnw = sb("gnw", [128, 512])
        ld3 = kb.dsem("ld3")
        kb.dma("sp", gnw.t[:], gnw_b, W=[gnw], sem=ld3)
        Sst = [[sb(f"S{h}_{i}", [128, 128]) for i in range(2)] for h in range(4)]
        Sbf = [[sb(f"Sb{h}_{i}", [128, 128], BF16) for i in range(2)] for h in range(4)]
        for h in range(4):
            kb.op(P, lambda e, h=h: e.memset(Sst[h][0].t[:], 0.0), W=[Sst[h][0]])
            kb.op(P, lambda e, h=h: e.memset(Sbf[h][0].t[:], 0.0), W=[Sbf[h][0]])
        H4 = range(4)
        dg = [sb(f"dg{h}", [128, 256]) for h in H4]
        t1 = [sb(f"t1{h}", [128, 256]) for h in H4]
        Dd = [sb(f"Dd{h}", [128, 256]) for h in H4]
        MA = [sb(f"MA{h}", [128, 128]) for h in H4]
        ATb = [sb(f"ATb{h}", [128, 128], BF16) for h in H4]
        TTb = [sb(f"TTb{h}", [128, 128], BF16) for h in H4]
        XYb = [[sb(f"XYb{h}_{i}", [128, 256], BF16) for i in range(2)] for h in H4]
        Zb = [sb(f"Zb{h}", [128, 128], BF16) for h in H4]
        Zt = [sb(f"Z{h}", [128, 128]) for h in H4]
        egb = [sb(f"egb{h}", [128, 128]) for h in H4]
        rk = [sb(f"rk{h}", [128, 256], BF16) for h in H4]
        kdec = [sb(f"kdec{h}", [128, 128], BF16) for h in H4]
        WU = [sb(f"WU{h}", [128, 256], BF16) for h in H4]
        qd = [sb(f"qd{h}", [128, 128]) for h in H4]
        QtT = [sb(f"QtT{h}", [128, 128], BF16) for h in H4]
        NPh = [[sb(f"NPh{h}_{c}", [128, 128], BF16) for c in range(2)] for h in H4]
        ot = [sb(f"ot{i}", [128, 512]) for i in range(2)]
        osq = sb("osq", [128, 512])
        oss = sb("oss", [128, 8])
        gsg = sb("gsg", [128, 512])
        mixst = [sb(f"mixst{i}", [128, 512], BF16) for i in range(2)]
        mxsem = [kb.dsem(f"mx{i}") for i in range(2)]
        GQ_v = GQKV.rearrange("c p t -> p c t")
        NTG = NT if GT is None else GT
        cslot = [0]

        dslot = [0]

        def cps():
            q = cslot[0] % 4
            cslot[0] += 1
            return psum[4].t[:, q * 128:(q + 1) * 128], pqb[4][q]

        def cpd():
            i = dslot[0] % 8
            dslot[0] += 1
            bank = 5 + i // 4
            q = i % 4
            return psum[bank].t[:, q * 128:(q + 1) * 128], pqb[bank][q]

        import os as _os
        GST = int(_os.environ.get('GSTAGE', '99'))
        out_defer = []
        for t in range(NTG):
            kq = kqv[t % 2]
            kb.dma("sp", kq.t[:], GQ_v[:, :, t * 128:(t + 1) * 128], R=[scr_b["GQKV"]], W=[kq], sem=kqsem[t % 2])
            gz_ = gzt[t % 2]
            kb.dma("act", gz_.t[:], TM[t * 128:(t + 1) * 128, 0:512], R=[scr_b["TM"]], W=[gz_], sem=gzsem[t % 2])
            o_ = ot[t % 2]
            col = lambda nm, h: sc[nm].t[:, t * 4 + h:t * 4 + h + 1]
            B = lambda h, q: pqb[h][q]
            for h in H4:
                qT = kq.t[:, h, :]; kT = kq.t[:, 4 + h, :]
                kb.op(PE, lambda e, h=h, kT=kT: e.matmul(pq(h, 0, 1), lhsT=kT, rhs=kT, start=True, stop=True), R=[kq], W=[B(h, 0)])
                kb.op(PE, lambda e, h=h, kT=kT, qT=qT: e.matmul(pq(h, 1, 2), lhsT=kT, rhs=qT, start=True, stop=True), R=[kq], W=[B(h, 1)])
                kb.op(PE, lambda e, h=h: e.matmul(pq(h, 2, 3), lhsT=col("yy", h).to_broadcast([128, 128]), rhs=ident, start=True, stop=True), R=[cst, sc["yy"]], W=[B(h, 2)])
                kb.op(PE, lambda e, h=h: e.matmul(pq(h, 3, 4), lhsT=col("gc", h).to_broadcast([128, 128]), rhs=ident, start=True, stop=True), R=[cst, sc["gc"]], W=[B(h, 3)])
            if GST < 2:
                continue
            for h in H4:
                kb.op(V, lambda e, h=h: e.tensor_scalar(out=t1[h].t[:], in0=pq(h, 2, 4), scalar1=col("gc", h), scalar2=0.0, op0=ALU.subtract, op1=ALU.min), R=[B(h, 2), B(h, 3), sc["gc"]], W=[t1[h]])
            for h in H4:
                kb.op(A, lambda e, h=h: e.activation(out=egb[h].t[:], in_=pq(h, 3, 4), func=AF.Exp), R=[B(h, 3), t1[h]], W=[egb[h]])
                kb.op(A, lambda e, h=h: e.activation(out=t1[h].t[:], in_=t1[h].t[:], func=AF.Exp), R=[t1[h]], W=[t1[h]])
            for h in H4:
                kb.op(P, lambda e, h=h: e.tensor_tensor(out=Dd[h].t[:], in0=t1[h].t[:], in1=mask2, op=ALU.mult), R=[t1[h], cst], W=[Dd[h]])
            for h in H4:
                kb.op(V, lambda e, h=h: e.tensor_tensor(out=MA[h].t[:], in0=pq(h, 0, 1), in1=Dd[h].t[:, 0:128], op=ALU.mult), R=[B(h, 0), Dd[h]], W=[MA[h]])
                kb.op(V, lambda e, h=h: e.tensor_tensor(out=ATb[h].t[:], in0=pq(h, 1, 2), in1=Dd[h].t[:, 128:256], op=ALU.mult), R=[B(h, 1), Dd[h]], W=[ATb[h]])
            if GST < 3:
                continue
            for h in H4:
                kb.op(PE, lambda e, h=h: e.transpose(out=pq(h, 2, 3), in_=MA[h].t[:, 0:128], identity=ident), R=[MA[h], cst], W=[B(h, 2)])
                kb.op(A, lambda e, h=h: e.copy(out=XYb[h][0].t[:, 0:128], in_=pq(h, 2, 3)), R=[B(h, 2)], W=[XYb[h][0]])
                kb.op(P, lambda e, h=h: e.tensor_copy(out=XYb[h][0].t[:, 128:256], in_=MA[h].t[:, 0:128]), R=[MA[h]], W=[XYb[h][0]])
                kb.op(P, lambda e, h=h: e.tensor_tensor(out=Zt[h].t[:], in0=ident, in1=MA[h].t[:, 0:128], op=ALU.subtract), R=[cst, MA[h]], W=[Zt[h]])
            if GST < 4:
                continue
            for lv in range(5):
                for h in H4:
                    cur = XYb[h][lv % 2]
                    nxtb = XYb[h][(lv + 1) % 2]
                    kb.op(PE, lambda e, h=h, cur=cur: e.matmul(pq(h, 0, 1), lhsT=cur.t[:, 128:256], rhs=cur.t[:, 0:128], start=True, stop=True), R=[cur], W=[B(h, 0)])
                    if lv < 4:
                        kb.op(PE, lambda e, h=h, cur=cur: e.matmul(pq(h, 1, 2), lhsT=cur.t[:, 0:128], rhs=cur.t[:, 128:256], start=True, stop=True), R=[cur], W=[B(h, 1)])
                        kb.op(A, lambda e, h=h, nxtb=nxtb: e.copy(out=nxtb.t[:], in_=pq(h, 0, 2)), R=[B(h, 0), B(h, 1)], W=[nxtb])
                    else:
                        kb.op(A, lambda e, h=h, nxtb=nxtb: e.copy(out=nxtb.t[:, 0:128], in_=pq(h, 0, 1)), R=[B(h, 0)], W=[nxtb])
                    kb.op(A, lambda e, h=h: e.copy(out=Zb[h].t[:], in_=Zt[h].t[:]), R=[Zt[h]], W=[Zb[h]])
                for h in H4:
                    zq = 2 + (lv % 2)
                    nxtb = XYb[h][(lv + 1) % 2]
                    kb.op(PE, lambda e, h=h, nxtb=nxtb, zq=zq: e.matmul(pq(h, zq, zq + 1), lhsT=nxtb.t[:, 0:128], rhs=Zb[h].t[:], start=True, stop=True), R=[nxtb, Zb[h]], W=[B(h, zq)])
                    kb.op(V, lambda e, h=h, zq=zq: e.tensor_tensor(out=Zt[h].t[:], in0=Zt[h].t[:], in1=pq(h, zq, zq + 1), op=ALU.add), R=[B(h, zq), Zt[h]], W=[Zt[h]])
            while out_defer:
                out_defer.pop(0)()
            for h in H4:
                kT = kq.t[:, 4 + h, :]; vT = kq.t[:, 8 + h, :]
                kb.op(PE, lambda e, h=h, kT=kT: e.transpose(out=psb.t[:, h * 256:h * 256 + 128], in_=kT, identity=identb.t[:]), R=[kq, identb], W=[psb])
                kb.op(PE, lambda e, h=h, vT=vT: e.transpose(out=psb.t[:, h * 256 + 128:h * 256 + 256], in_=vT, identity=identb.t[:]), R=[kq, identb], W=[psb])
            for h in H4:
                kb.op(A, lambda e, h=h: e.activation(out=rk[h].t[:, 0:128], in_=psb.t[:, h * 256:h * 256 + 128], func=AF.Copy, scale=col("bg", h)), R=[psb, sc["bg"]], W=[rk[h]])
                kb.op(A, lambda e, h=h: e.activation(out=rk[h].t[:, 128:256], in_=psb.t[:, h * 256 + 128:h * 256 + 256], func=AF.Copy, scale=col("beta", h)), R=[psb, sc["beta"]], W=[rk[h]])
                kb.op(A, lambda e, h=h: e.activation(out=kdec[h].t[:], in_=psb.t[:, h * 256:h * 256 + 128], func=AF.Copy, scale=col("kd", h)), R=[psb, sc["kd"]], W=[kdec[h]])
                kb.op(V, lambda e, h=h: e.tensor_tensor(out=qd[h].t[:], in0=kq.t[:, h, :], in1=egb[h].t[:], op=ALU.mult), R=[kq, egb[h]], W=[qd[h]])
            if GST < 6:
                continue
            for h in H4:
                kb.op(A, lambda e, h=h: e.copy(out=TTb[h].t[:], in_=Zt[h].t[:]), R=[Zt[h]], W=[TTb[h]])
                kb.op(PE, lambda e, h=h: e.matmul(pq(h, 0, 2), lhsT=TTb[h].t[:], rhs=rk[h].t[:], start=True, stop=True), R=[TTb[h], rk[h]], W=[B(h, 0), B(h, 1)])
                kb.op(A, lambda e, h=h: e.copy(out=WU[h].t[:], in_=pq(h, 0, 2)), R=[B(h, 0), B(h, 1)], W=[WU[h]])
            if GST < 7:
                continue
            for h in H4:
                aap, abf = cpd()
                kb.op(PE, lambda e, h=h, aap=aap: e.matmul(aap, lhsT=WU[h].t[:, 0:128], rhs=ATb[h].t[:], start=True, stop=True), R=[WU[h], ATb[h]], W=[abf])
                kb.op(V, lambda e, h=h, aap=aap: e.tensor_tensor(out=QtT[h].t[:], in0=qd[h].t[:], in1=aap, op=ALU.subtract), R=[qd[h], abf], W=[QtT[h]])
                for c in range(int(_os.environ.get('GS7', '2'))):
                    r = slice(64 * c, 64 * c + 64)
                    kb.op(PE, lambda e, h=h, c=c, r=r: e.matmul(pq(h, c, c + 1), lhsT=WU[h].t[r, 0:128], rhs=kdec[h].t[r, :], start=True, stop=True), R=[WU[h], kdec[h]], W=[B(h, c)])
                    kb.op(A, lambda e, h=h, c=c: e.activation(out=NPh[h][c].t[:], in_=pq(h, c, c + 1), func=AF.Copy, scale=-1.0), R=[B(h, c)], W=[NPh[h][c]])
            if GST < 8:
                continue
            for c in range(2):
                r = slice(64 * c, 64 * c + 64)
                for h in H4:
                    Sc = Sst[h][c]; Sn = Sst[h][1 - c]; Scb = Sbf[h][c]; Snb = Sbf[h][1 - c]
                    oap, ob = cps()
                    kb.op(PE, lambda e, h=h, oap=oap, Scb=Scb: e.matmul(oap, lhsT=QtT[h].t[:], rhs=Scb.t[:], start=True, stop=False), R=[QtT[h], Scb], W=[ob])
                    kb.op(PE, lambda e, h=h, oap=oap, r=r: e.matmul(oap, lhsT=ATb[h].t[r, :], rhs=WU[h].t[r, 128:256], start=False, stop=True), R=[ATb[h], WU[h]], W=[ob])
                    kb.op(A, lambda e, h=h, oap=oap, r=r: e.copy(out=o_.t[r, h * 128:(h + 1) * 128], in_=oap[r, :]), R=[ob], W=[o_])
                    sap, sbf = cpd()
                    kb.op(PE, lambda e, h=h, sap=sap, r=r: e.matmul(sap, lhsT=kdec[h].t[r, :], rhs=WU[h].t[r, 128:256], start=True, stop=False), R=[kdec[h], WU[h]], W=[sbf])
                    kb.op(PE, lambda e, h=h, c=c, sap=sap, Scb=Scb: e.matmul(sap, lhsT=NPh[h][c].t[:], rhs=Scb.t[:], start=False, stop=True), R=[NPh[h][c], Scb], W=[sbf])
                    egl = sc["EGL%d" % c].t[:, t * 4 + h:t * 4 + h + 1]
                    kb.op(V, lambda e, sap=sap, Sc=Sc, Snb=Snb, egl=egl: e.scalar_tensor_tensor(out=Snb.t[:], in0=Sc.t[:], scalar=egl, in1=sap, op0=ALU.mult, op1=ALU.add), R=[Sc, sbf, sc["EGL%d" % c]], W=[Snb])
                    kb.op(V, lambda e, sap=sap, Sc=Sc, Sn=Sn, egl=egl: e.scalar_tensor_tensor(out=Sn.t[:], in0=Sc.t[:], scalar=egl, in1=sap, op0=ALU.mult, op1=ALU.add), R=[Sc, sbf, sc["EGL%d" % c]], W=[Sn])
            def out_stage(t=t, o_=o_, gz_=gz_):
                ms = mixst[t % 2]
                kb.op(A, lambda e: e.activation(out=osq.t[:], in_=o_.t[:], func=AF.Square), R=[o_], W=[osq])
                kb.op(V, lambda e: e.tensor_reduce(out=oss.t[:, 0:4], in_=osq.t[:].rearrange("p (h d) -> p h d", d=128), axis=mybir.AxisListType.X, op=ALU.add), R=[osq], W=[oss])
                kb.op(A, lambda e: e.activation(out=oss.t[:, 0:4], in_=oss.t[:, 0:4], func=AF.Ln, bias=epsc, scale=1.0 / 128), R=[oss, cst], W=[oss])
                kb.op(A, lambda e: e.activation(out=oss.t[:, 4:8], in_=oss.t[:, 0:4], func=AF.Exp, scale=-0.5), R=[oss], W=[oss])
                kb.op(A, lambda e: e.activation(out=gsg.t[:], in_=gz_.t[:], func=AF.Exp, scale=-1.0), R=[gz_], W=[gsg])
                kb.op(A, lambda e: e.activation(out=gsg.t[:], in_=gsg.t[:], func=AF.Ln, bias=ones_f[:, 0:1], scale=1.0), R=[gsg, cst], W=[gsg])
                kb.op(A, lambda e: e.activation(out=gsg.t[:], in_=gsg.t[:], func=AF.Exp, scale=-1.0), R=[gsg], W=[gsg])
                kb.op(P, lambda e: e.tensor_tensor(out=gsg.t[:], in0=gsg.t[:], in1=gz_.t[:], op=ALU.mult), R=[gsg, gz_], W=[gsg])
                kb.op(P, lambda e: e.tensor_tensor(out=gsg.t[:], in0=gsg.t[:], in1=gnw.t[:], op=ALU.mult), R=[gsg, gnw], W=[gsg])
                for h in H4:
                    kb.op(V, lambda e, h=h: e.scalar_tensor_tensor(out=ms.t[:, h * 128:(h + 1) * 128], in0=o_.t[:, h * 128:(h + 1) * 128], scalar=oss.t[:, 4 + h:5 + h], in1=gsg.t[:, h * 128:(h + 1) * 128], op0=ALU.mult, op1=ALU.mult), R=[o_, oss, gsg], W=[ms])
                kb.dma("pool", MIXG[t * 128:(t + 1) * 128, :], ms.t[:], R=[ms], sem=mxsem[t % 2])

            out_defer.append(out_stage)
        while out_defer:
            out_defer.pop(0)()
        kb.barrier()
        stk[0].close()
        stk[0] = None
        if "MIXG" in dbg_out:
            d = kb.dsem(); out_sems.append(d)
            for c in range(16):
                kb.dma("sp", dbg_out["MIXG"][c * 512:(c + 1) * 512, :], MIXG[c * 512:(c + 1) * 512, :], sem=d)


    if PH >= 3:
        stk[0] = ExitStack()
        EXP = AF.Exp
        ksT = [sb(f"ksT{g}", [128, S], BF16) for g in range(2)]
        Vs = [sb(f"Vs{g}", [128, 64, 65], BF16) for g in range(2)]
        tS = sb("tS", [128, 2, 9, 512])
        tWx = sb("tWx", [128, 2, 2, 512])
        tCf = sb("tCf", [128, 2, 512])
        Wo = sb("Wo", [128, 8, D], BF16)
        ovb = sb("ovb", [128, 4, 128], BF16)
        adjc = sb("adjc", [128, 256])
        j0c = sb("j0c", [128, 32])
        pwc = sb("pwc", [128, 2])
        ecx = sb("ecx", [128, 8])
        fnw = sb("fnw", [128, D])
        kcT = [sb(f"kcT{g}", [64, 512], BF16) for g in range(2)]
        vca = [sb(f"vca{g}", [128, 4, 65], BF16) for g in range(2)]
        l3 = kb.dsem("l3")
        mainstk = stk[0]
        stk[0] = ExitStack()
        stgA = sb("stgA", [128, 2048])
        for g in range(2):
            kb.dma("sp", ksT[g].t[0:64, :], KVT[2, 64 * g:64 * g + 64, :], R=[scr_b["KVT"]], W=[ksT[g]], sem=l3)
            for m in range(9):
                kb.dma("act", tS.t[:, g, m, :], tabS[g, m], W=[tS], sem=l3)
            for m in range(2):
                kb.dma("act", tWx.t[:, m, g, :], tabWx[m, g], W=[tWx], sem=l3)
            kb.dma("act", tCf.t[:, g, :], tabCfar[g], W=[tCf], sem=l3)
        for nm_, dst, src in (("adj", adjc, adjT), ("j0", j0c, j0b), ("pw", pwc, pw), ("ec", ecx, cexp), ("fnw", fnw, fnw_b)):
            kb.dma("sp", dst.t[:], src, W=[dst], sem=l3)
        kb.seal(l3, ksT + [tS, tWx, tCf, adjc, j0c, pwc, ecx, fnw])
        kb.op(A, lambda e: e.activation(out=ecx.t[:], in_=ecx.t[:], func=EXP), R=[ecx], W=[ecx])
        sgs = kb.dsem("sgs")
        for c in range(4):
            kb.dma("sp", stgA.t[64:128, :], estk[:, c * 2048:(c + 1) * 2048], W=[stgA], sem=sgs)
            for g in range(2):
                kb.op(V, lambda e, c=c, g=g: e.tensor_copy(out=ksT[g].t[64:128, c * 2048:(c + 1) * 2048], in_=stgA.t[64:128, :]), R=[stgA], W=[ksT[g]])
        stgB = sb("stgB", [128, 2048])
        sgs2 = kb.dsem("sgs2")
        TMv = TM[:, 1024:1152].rearrange("(k p) c -> p k c", p=128)
        for c in range(4):
            stg_, sem_, q_ = (stgA, sgs, "sp") if c % 2 == 0 else (stgB, sgs2, "act")
            kb.dma(q_, stg_.t[:], TMv[:, c * 16:(c + 1) * 16, :], R=[scr_b["TM"]], W=[stg_], sem=sem_)
            sv = stg_.t[:].rearrange("p (k c) -> p k c", c=128)
            for g in range(2):
                kb.op(V if g == 0 else A, lambda e, c=c, g=g, sv=sv: (e.tensor_copy(out=Vs[g].t[:, c * 16:(c + 1) * 16, 0:64], in_=sv[:, :, 64 * g:64 * g + 64]) if g == 0 else e.copy(out=Vs[g].t[:, c * 16:(c + 1) * 16, 0:64], in_=sv[:, :, 64 * g:64 * g + 64])), R=[stg_], W=[Vs[g]])
        for g in range(2):
            kb.op(P, lambda e, g=g: e.memset(Vs[g].t[:, :, 64:65], 1.0), W=[Vs[g]])
        wo_v = w_out.rearrange("(c p) n -> p c n", p=128)
        for c in range(8):
            kb.dma("sp", stgA.t[:, 0:1024], wo_v[:, c, :], W=[stgA], sem=sgs)
            kb.op(V, lambda e, c=c: e.tensor_copy(out=Wo.t[:, c, :], in_=stgA.t[:, 0:1024]), R=[stgA], W=[Wo])
        kb.dma("sp", stgA.t[:, 0:512], ovl.rearrange("p a b -> p (a b)"), W=[stgA], sem=sgs)
        kb.op(V, lambda e: e.tensor_copy(out=ovb.t[:].rearrange("p a b -> p (a b)"), in_=stgA.t[:, 0:512]), R=[stgA], W=[ovb])
        rawT = sb("rawT", [64, S], BF16)
        w1s = sb("w1s", [64, 4096])
        w1b = sb("w1b", [64, 32, 128], BF16)
        peb_ = sb("peb_", [64, 32])
        pebb = sb("pebb", [64, 32], BF16)
        w2s = sb("w2s", [128, 64])
        w2b = sb("w2b", [128, 64], BF16)
        pcol = sb("pcol", [128, 1])
        hb = sb("hb", [128, 512], BF16)
        kb.op(P, lambda e: e.memset(hb.t[:], 0.0), W=[hb])
        for kind, (w1d, ped, w2d) in enumerate(((w1k, pek, w2k), (w1v, pev, w2v))):
            kb.dma("sp", w1s.t[:], w1d.rearrange("p a b -> p (a b)"), W=[w1s], sem=sgs)
            kb.dma("sp", peb_.t[:], ped, W=[peb_], sem=sgs)
            kb.dma("sp", w2s.t[:], w2d, W=[w2s], sem=sgs)
            kb.seal(sgs, [w1s, peb_, w2s])
            kb.op(V, lambda e: e.tensor_copy(out=w1b.t[:].rearrange("p a b -> p (a b)"), in_=w1s.t[:]), R=[w1s], W=[w1b])
            kb.op(V, lambda e: e.tensor_copy(out=pebb.t[:], in_=peb_.t[:]), R=[peb_], W=[pebb])
            kb.op(V, lambda e: e.tensor_copy(out=w2b.t[:], in_=w2s.t[:]), R=[w2s], W=[w2b])
            for g in range(2):
                kb.dma("sp", rawT.t[:], KVT[kind, 64 * g:64 * g + 64, :], R=[scr_b["KVT"]], W=[rawT], sem=sgs)
                rv = rawT.t[:].rearrange("p (n r) -> p n r", r=16)
                ph_ = psum[0]; pp_ = psum[1]
                for l in range(32):
                    a_, r_ = l // 16, l % 16
                    kb.op(PE, lambda e, l=l, a_=a_, r_=r_: e.matmul(ph_.t[:, 0:511], lhsT=w1b.t[:, l, :], rhs=rv[:, a_:a_ + 511, r_], start=(l == 0), stop=(l == 31)), R=[w1b, rawT], W=[ph_])
                for l in range(32):
                    kb.op(PE, lambda e, l=l: e.matmul(pp_.t[:, 0:1], lhsT=w1b.t[:, l, :], rhs=pebb.t[:, l:l + 1], start=(l == 0), stop=(l == 31)), R=[w1b, pebb], W=[pp_])
                kb.op(V, lambda e: e.tensor_copy(out=pcol.t[:], in_=pp_.t[:, 0:1]), R=[pp_], W=[pcol])
                kb.op(A, lambda e: e.activation(out=hb.t[:, 0:511], in_=ph_.t[:, 0:511], func=AF.Silu, bias=pcol.t[:, 0:1], scale=1.0), R=[ph_, pcol], W=[hb])
                if kind == 0:
                    pk_ = psum[2]
                    kb.op(PE, lambda e: e.matmul(pk_.t[0:64, :], lhsT=w2b.t[:], rhs=hb.t[:], start=True, stop=True), R=[w2b, hb], W=[pk_])
                    kb.op(V, lambda e, g=g: e.tensor_copy(out=kcT[g].t[:], in_=pk_.t[0:64, :]), R=[pk_], W=[kcT[g]])
                else:
                    pk_ = psum[2]
                    for ct in range(4):
                        kb.op(PE, lambda e, ct=ct: e.matmul(pk_.t[:, ct * 64:(ct + 1) * 64], lhsT=hb.t[:, ct * 128:(ct + 1) * 128], rhs=w2b.t[:], start=True, stop=True), R=[w2b, hb], W=[pk_])
                    kb.op(V, lambda e, g=g: e.tensor_copy(out=vca[g].t[:, :, 0:64], in_=pk_.t[:, 0:256].rearrange("p (a b) -> p a b", b=64)), R=[pk_], W=[vca[g]])
                    kb.op(V, lambda e, g=g: e.tensor_copy(out=vca[g].t[:, :, 64:65], in_=nvalid.rearrange("p (a b) -> p a b", b=1)), R=[cst], W=[vca[g]])
        if "kcT" in dbg_out:
            for g in range(2):
                dbg_dump("kcT", None, None)
        kb.barrier()
        stk[0].close()
        stk[0] = mainstk
        NB2 = 1
        qp = [sb(f"qp{i}", [64, 8, 256], BF16) for i in range(NB2)]
        qT_ = [sb(f"qT{i}", [64, 8, 128], BF16) for i in range(NB2)]
        qtmp = sb("qtmp", [64, 8, 128], BF16)
        tm2 = [sb(f"tm2{i}", [128, 2, 536]) for i in range(NB2)]
        tmq = sb("tmq", [128, 536])
        mg2 = [sb(f"mg2{i}", [128, 2, 512], BF16) for i in range(NB2)]
        xrt = [sb(f"xrt{i}", [128, D]) for i in range(NB2)]
        kwt = [sb(f"kwt{i}", [64, 2, 768], BF16) for i in range(NB2)]
        vwst = [sb(f"vwst{i}", [128, 6, 128]) for i in range(NB2)]
        Vw = sb("Vw", [128, 6, 2, 65], BF16)
        kb.op(P, lambda e: e.memset(Vw.t[:, :, :, 64:65], 1.0), W=[Vw])
        slsem = [[kb.dsem(f"sl{k}_{i}") for i in range(NB2)] for k in range(6)]
        tcb = [sb(f"tcb{i}", [128, 512]) for i in range(2)]
        tcsem = [kb.dsem(f"tc{i}") for i in range(2)]
        sbt = [sb(f"sbt{i}", [128, 512]) for i in range(4)]
        Pt = [sb(f"Pt{i}", [128, 512], BF16) for i in range(6)]
        QS = [[sb(f"QS{g}_{i}", [128, 512], BF16) for i in range(2)] for g in range(2)]
        selm2 = [sb(f"selm2{g}", [128, 256]) for g in range(2)]
        gate = sb("gate", [128, 24])
        ocs = [sb(f"ocs{g}", [128, 260]) for g in range(2)]; cmb = sb("cmb", [128, 260]); ows = [sb(f"ows{g}", [128, 260]) for g in range(2)]
        den3 = [sb(f"den3{g}", [128, 12]) for g in range(2)]; cf3 = [sb(f"cf3{g}", [128, 12]) for g in range(2)]
        score = [sb(f"score{g}", [128, 128]) for g in range(2)]; work = sb("work", [128, 128])
        m8a = sb("m8a", [128, 8]); m8b = sb("m8b", [128, 8])
        o_n = sb("o_n", [128, 512]); sgn = sb("sgn", [128, 512])
        mix = sb("mix", [128, D], BF16); mixT = sb("mixT", [128, 8, 128], BF16)
        hbuf = sb("hbuf", [128, D]); hss = sb("hss", [128, 2])
        obuf = [sb("obuf0", [128, D])] * 2
        osem = [kb.dsem(f"os{i}") for i in range(2)]
        out_sems.extend(osem)
        NQv = NQ.rearrange("h d t -> d h t")
        print("sbuf remaining", nc.sbuf_bytes_remaining)
        nS = [0]; nP = [0]; nC = [0]; nTC = [0]
        SB_ = [psum[0], psum[1], psum[2], psum[6]]
        SBK = [SB_]
        ACCA, ACCB, ACCC, MISC = psum[3], psum[4], psum[5], psum[6]
        SL = range(32) if NSL is None else NSL
        for i in SL:
            bi = i % NB2
            t0 = 2 * i * 128
            kb.dma("sp", qp[bi].t[:], NQv[:, :, t0:t0 + 256], R=[scr_b["NQ"]], W=[qp[bi]], sem=slsem[0][bi])
            kb.dma("sp", tm2[bi].t[:, :, 0:512], TM[t0:t0 + 256, 512:1024].rearrange("(a p) c -> p a c", p=128), R=[scr_b["TM"]], W=[tm2[bi]], sem=slsem[1][bi])
            kb.dma("sp", tm2[bi].t[:, :, 512:536], TM[t0:t0 + 256, 1280:1304].rearrange("(a p) c -> p a c", p=128), R=[scr_b["TM"]], W=[tm2[bi]], sem=slsem[1][bi])
            kb.dma("act", mg2[bi].t[:], MIXG[t0:t0 + 256, :].rearrange("(a p) c -> p a c", p=128), R=[scr_b["MIXG"]], W=[mg2[bi]], sem=slsem[2][bi])
            kb.dma("act", xrt[bi].t[:], xr[i * 128:(i + 1) * 128, :], W=[xrt[bi]], sem=slsem[3][bi])
            kt_lo = max(0, 2 * i - 4); nkw = 2 * i + 2 - kt_lo
            for g in range(2):
                kb.dma("sp", kwt[bi].t[:, g, 0:nkw * 128], KVT[3, 64 * g:64 * g + 64, kt_lo * 128:(2 * i + 2) * 128], R=[scr_b["KVT"]], W=[kwt[bi]], sem=slsem[4][bi])
            kb.dma("act", vwst[bi].t[:, 0:nkw, :], TM[kt_lo * 128:(2 * i + 2) * 128, 1152:1280].rearrange("(k p) c -> p k c", p=128), R=[scr_b["TM"]], W=[vwst[bi]], sem=slsem[5][bi])
            w0 = pwc.t[:, 0:1]; w1_ = pwc.t[:, 1:2]
            kb.op(V, lambda e: e.tensor_scalar(out=qtmp.t[:], in0=qp[bi].t[:, :, 0:128], scalar1=w0[0:64, :], scalar2=None, op0=ALU.mult), R=[qp[bi], pwc], W=[qtmp])
            kb.op(V, lambda e: e.scalar_tensor_tensor(out=qT_[bi].t[:], in0=qp[bi].t[:, :, 128:256], scalar=w1_[0:64, :], in1=qtmp.t[:], op0=ALU.mult, op1=ALU.add), R=[qp[bi], pwc, qtmp], W=[qT_[bi]])
            kb.op(V, lambda e: e.tensor_scalar(out=tmq.t[:], in0=tm2[bi].t[:, 0, :], scalar1=w0, scalar2=None, op0=ALU.mult), R=[tm2[bi], pwc], W=[tmq])
            kb.op(V, lambda e: e.scalar_tensor_tensor(out=tmq.t[:], in0=tm2[bi].t[:, 1, :], scalar=w1_, in1=tmq.t[:], op0=ALU.mult, op1=ALU.add), R=[tm2[bi], pwc, tmq], W=[tmq])
            kb.op(V, lambda e: e.tensor_scalar(out=mix.t[:, 0:512], in0=mg2[bi].t[:, 0, :], scalar1=w0, scalar2=None, op0=ALU.mult), R=[mg2[bi], pwc], W=[mix])
            kb.op(V, lambda e: e.scalar_tensor_tensor(out=mix.t[:, 0:512], in0=mg2[bi].t[:, 1, :], scalar=w1_, in1=mix.t[:, 0:512], op0=ALU.mult, op1=ALU.add), R=[mg2[bi], pwc, mix], W=[mix])
            for g in range(2):
                kb.op(P, lambda e, g=g: e.tensor_copy(out=Vw.t[:, 0:nkw, g, 0:64], in_=vwst[bi].t[:, 0:nkw, 64 * g:64 * g + 64]), R=[vwst[bi]], W=[Vw])
            kb.op(A, lambda e: e.activation(out=gate.t[:], in_=tmq.t[:, 512:536], func=EXP, scale=-1.0), R=[tmq], W=[gate])
            kb.op(V, lambda e: e.tensor_scalar(out=gate.t[:], in0=gate.t[:], scalar1=1.0, scalar2=None, op0=ALU.add), R=[gate], W=[gate])
            kb.op(V, lambda e: e.reciprocal(out=gate.t[:], in_=gate.t[:]), R=[gate], W=[gate])
            qT = qT_[bi]

            def stageA(j, g):
                ps_ = SBK[0][nS[0] % len(SBK[0])]; nS[0] += 1
                if j.get("qs") is not None:
                    kb.op(PE, lambda e: e.matmul(ps_.t[:], lhsT=j["lhsT"], rhs=j["qs"].t[:], start=True, stop=True), R=[j["qs"]] + j["lt"], W=[ps_])
                else:
                    rhs = qT.t[:, 4 * g:4 * g + 4, :].rearrange("p a b -> p (a b)")
                    kb.op(PE, lambda e: e.matmul(ps_.t[:], lhsT=j["lhsT"], rhs=rhs, start=True, stop=True), R=[qT] + j["lt"], W=[ps_])
                pt = Pt[nP[0] % 6]; nP[0] += 1
                if j.get("tabdma") is not None:
                    tb_ = tcb[nTC[0] % 2]; ts_ = tcsem[nTC[0] % 2]; nTC[0] += 1
                    kb.dma("sp", tb_.t[:], j["tabdma"], W=[tb_], sem=ts_)
                    j["tab"], j["tt"] = tb_.t[:], [tb_]
                if j.get("tab") is not None:
                    st_ = sbt[nC[0] % 4]; nC[0] += 1
                    kb.op(V, lambda e: e.tensor_tensor(out=st_.t[:], in0=ps_.t[:], in1=j["tab"], op=ALU.add), R=[ps_] + j["tt"], W=[st_])
                    kb.op(A, lambda e: e.activation(out=pt.t[:], in_=st_.t[:], func=EXP), R=[st_], W=[pt])
                else:
                    kb.op(A, lambda e: e.activation(out=pt.t[:], in_=ps_.t[:], func=EXP), R=[ps_], W=[pt])
                j["pt"] = pt

            def stageB(j):
                pt = j["pt"]
                for (acc, v_ap, v_tiles, first, last, width, stride) in j["pv"]:
                    for h in range(4):
                        kb.op(PE, lambda e, h=h: e.matmul(acc.t[:, h * stride:h * stride + width], lhsT=pt.t[:, h * 128:(h + 1) * 128], rhs=v_ap, start=(first and h == 0), stop=(last and h == 3)), R=[pt] + v_tiles, W=[acc])

            def run_jobs(jobs, g, LA=3):
                for idx in range(len(jobs) + LA):
                    if idx < len(jobs):
                        stageA(jobs[idx], g)
                    if idx - LA >= 0:
                        stageB(jobs[idx - LA])

            def front(g):
                    nkt = 2 * i + 2
                    cts = [ct for ct in range(4) if 2 * i + 1 - 16 * ct >= 0]
                    jobs = []
                    for ci, ct in enumerate(cts):
                        j = dict(lhsT=kcT[g].t[:, ct * 128:(ct + 1) * 128], lt=[kcT[g]])
                        if 2 * i - 16 * ct >= 23:
                            j["tab"], j["tt"] = tCf.t[:, g, :], [tCf]
                        else:
                            j["tabdma"] = tabC[g, CIDX[(i, ct)]]
                        f_, l_ = ci == 0, ci == len(cts) - 1
                        j["pv"] = [(ACCA, vca[g].t[:, ct, :], [vca[g]], f_, l_, 65, 65), (ACCB, ovb.t[:, ct, :], [ovb], f_, l_, 128, 128)]
                        jobs.append(j)
                    run_jobs(jobs, g)
                    wk = [kt for kt in range(kt_lo, nkt)]
                    jobs = []
                    for ki, kt in enumerate(wk):
                        m = 2 * i + 1 - kt
                        sl_ = kt - kt_lo
                        tab = tS.t[:, g, m, :] if m <= 3 else tWx.t[:, m - 4, g, :]
                        jobs.append(dict(lhsT=kwt[bi].t[:, g, sl_ * 128:(sl_ + 1) * 128], lt=[kwt[bi]], tab=tab, tt=[tS, tWx],
                                         pv=[(ACCC, Vw.t[:, sl_, g, :], [Vw], ki == 0, ki == len(wk) - 1, 65, 65)]))
                    run_jobs(jobs, g)
                    kb.op(V, lambda e: e.tensor_copy(out=ows[g].t[:], in_=ACCC.t[:, 0:260]), R=[ACCC], W=[ows[g]])
                    kb.op(V, lambda e: e.tensor_copy(out=ocs[g].t[:], in_=ACCA.t[:, 0:260]), R=[ACCA], W=[ocs[g]])
                    o3 = ocs[g].t[:].rearrange("p (h c) -> p h c", c=65)
                    kb.op(V, lambda e: e.tensor_scalar(out=den3[g].t[:, 0:4], in0=o3[:, :, 64], scalar1=1e-30, scalar2=None, op0=ALU.max), R=[ocs[g]], W=[den3[g]])
                    kb.op(V, lambda e: e.reciprocal(out=cf3[g].t[:, 0:4], in_=den3[g].t[:, 0:4]), R=[den3[g]], W=[cf3[g]])
                    kb.op(V, lambda e: e.tensor_scalar(out=score[g].t[:], in0=ACCB.t[:, 0:128], scalar1=cf3[g].t[:, 0:1], scalar2=None, op0=ALU.mult), R=[ACCB, cf3[g]], W=[score[g]])
                    for h in range(1, 4):
                        kb.op(V, lambda e, h=h: e.scalar_tensor_tensor(out=score[g].t[:], in0=ACCB.t[:, h * 128:(h + 1) * 128], scalar=cf3[g].t[:, h:h + 1], in1=score[g].t[:], op0=ALU.mult, op1=ALU.add), R=[ACCB, cf3[g], score[g]], W=[score[g]])
                    a0 = 126 - 4 * i
                    kb.op(V, lambda e: e.tensor_tensor(out=score[g].t[:], in0=score[g].t[:], in1=adjc.t[:, a0:a0 + 128], op=ALU.add), R=[score[g], adjc], W=[score[g]])
                    kb.op(V, lambda e: e.tensor_tensor(out=score[g].t[:, 0:1], in0=score[g].t[:, 0:1], in1=j0c.t[:, i:i + 1], op=ALU.add), R=[score[g], j0c], W=[score[g]])
                    kb.op(V, lambda e: e.max(out=m8a.t[:], in_=score[g].t[:]), R=[score[g]], W=[m8a])
                    kb.op(V, lambda e: e.match_replace(out=work.t[:], in_to_replace=m8a.t[:], in_values=score[g].t[:], imm_value=-3.0e38), R=[score[g], m8a], W=[work])
                    kb.op(V, lambda e: e.max(out=m8b.t[:], in_=work.t[:]), R=[work], W=[m8b])
                    for hf in range(2):
                        kb.op(V, lambda e, hf=hf: e.tensor_scalar(out=selm2[g].t[:, hf * 128:(hf + 1) * 128], in0=score[g].t[:], scalar1=m8b.t[:, 7:8], scalar2=None, op0=ALU.is_ge), R=[score[g], m8b], W=[selm2[g]])
                    ngb = 2 if nkt > 32 else 1
                    kb.op(PE, lambda e: e.transpose(out=MISC.t[:, 0:128], in_=selm2[g].t[:, 64:192], identity=ident), R=[selm2[g], cst], W=[MISC])
                    if ngb == 2:
                        kb.op(PE, lambda e: e.transpose(out=MISC.t[:, 128:256], in_=selm2[g].t[:, 0:128], identity=ident), R=[selm2[g], cst], W=[MISC])
                    for gb in range(ngb):
                        kb.op(P, lambda e, gb=gb: e.tensor_copy(out=QS[g][gb].t[0:64, :], in_=qT.t[:, 4 * g:4 * g + 4, :].rearrange("p a b -> p (a b)")), R=[qT], W=[QS[g][gb]])
                        for h in range(4):
                            kb.op(V, lambda e, h=h, gb=gb: e.tensor_scalar(out=QS[g][gb].t[64:128, h * 128:(h + 1) * 128], in0=MISC.t[64:128, gb * 128:(gb + 1) * 128], scalar1=-1.0, scalar2=-MNEG, op0=ALU.add, op1=ALU.mult), R=[MISC], W=[QS[g][gb]])
                    if g == 0 and ("selm" in dbg_out) and i == DSLOT:
                        dbg_dump("selm", selm2[g].t[:, 0:128], selm2[g])
                        dbg_dump("score", score[g].t[:], score[g])
                        dbg_dump("ocs", ocs[g].t[:], ocs[g])
                        dbg_dump("cf3", cf3[g].t[:], cf3[g])

            def back(g):
                    nkt = 2 * i + 2
                    o3 = ocs[g].t[:].rearrange("p (h c) -> p h c", c=65)
                    far = [kt for kt in range(nkt) if 2 * i + 1 - kt >= 9]
                    near = [kt for kt in range(nkt) if 2 * i + 1 - kt < 9]
                    jobs = []
                    for ki, kt in enumerate(far):
                        jobs.append(dict(lhsT=ksT[g].t[:, kt * 128:(kt + 1) * 128], lt=[ksT[g]], qs=QS[g][kt // 32],
                                         pv=[(ACCB, Vs[g].t[:, kt, :], [Vs[g]], ki == 0, ki == len(far) - 1, 65, 65)]))
                    for ki, kt in enumerate(near):
                        m = 2 * i + 1 - kt
                        jobs.append(dict(lhsT=ksT[g].t[:, kt * 128:(kt + 1) * 128], lt=[ksT[g]], qs=QS[g][kt // 32], tab=tS.t[:, g, m, :], tt=[tS],
                                         pv=[(ACCA, Vs[g].t[:, kt, :], [Vs[g]], ki == 0, ki == len(near) - 1, 65, 65)]))
                    SBK[0] = SB_ + [ACCC]
                    run_jobs(jobs, g, LA=4)
                    SBK[0] = SB_
                    kb.op(V, lambda e: e.tensor_copy(out=cmb.t[:], in_=ACCA.t[:, 0:260]), R=[ACCA], W=[cmb])
                    if far:
                        for h in range(4):
                            kb.op(V, lambda e, h=h, g=g: e.scalar_tensor_tensor(out=cmb.t[:, h * 65:(h + 1) * 65], in0=ACCB.t[:, h * 65:(h + 1) * 65], scalar=ecx.t[:, 4 * g + h:4 * g + h + 1], in1=cmb.t[:, h * 65:(h + 1) * 65], op0=ALU.mult, op1=ALU.add), R=[ACCB, ecx, cmb], W=[cmb])
                    c3 = cmb.t[:].rearrange("p (h c) -> p h c", c=65)
                    w3 = ows[g].t[:].rearrange("p (h c) -> p h c", c=65)
                    kb.op(V, lambda e: e.tensor_scalar(out=den3[g].t[:, 4:8], in0=c3[:, :, 64], scalar1=1e-30, scalar2=None, op0=ALU.max), R=[cmb], W=[den3[g]])
                    kb.op(V, lambda e: e.tensor_scalar(out=den3[g].t[:, 8:12], in0=w3[:, :, 64], scalar1=1e-30, scalar2=None, op0=ALU.max), R=[ows[g]], W=[den3[g]])
                    kb.op(V, lambda e: e.reciprocal(out=cf3[g].t[:], in_=den3[g].t[:]), R=[den3[g]], W=[cf3[g]])
                    gv = gate.t[:].rearrange("p (x h) -> p x h", h=8)[:, :, 4 * g:4 * g + 4]
                    kb.op(V, lambda e, gv=gv: e.tensor_tensor(out=cf3[g].t[:].rearrange("p (x h) -> p x h", h=4), in0=cf3[g].t[:].rearrange("p (x h) -> p x h", h=4), in1=gv, op=ALU.mult), R=[cf3[g], gate], W=[cf3[g]])
                    for h in range(4):
                        oc_ = o_n.t[:, (4 * g + h) * 64:(4 * g + h + 1) * 64]
                        kb.op(V, lambda e, h=h, oc_=oc_: e.tensor_scalar(out=oc_, in0=o3[:, h, 0:64], scalar1=cf3[g].t[:, h:h + 1], scalar2=None, op0=ALU.mult), R=[ocs[g], cf3[g]], W=[o_n])
                        kb.op(V, lambda e, h=h, oc_=oc_: e.scalar_tensor_tensor(out=oc_, in0=c3[:, h, 0:64], scalar=cf3[g].t[:, 4 + h:5 + h], in1=oc_, op0=ALU.mult, op1=ALU.add), R=[cmb, cf3[g], o_n], W=[o_n])
                        kb.op(V, lambda e, h=h, oc_=oc_: e.scalar_tensor_tensor(out=oc_, in0=w3[:, h, 0:64], scalar=cf3[g].t[:, 8 + h:9 + h], in1=oc_, op0=ALU.mult, op1=ALU.add), R=[ows[g], cf3[g], o_n], W=[o_n])

            for g in range(2):
                front(g)
            for g in range(2):
                back(g)
            if ("o_n" in dbg_out) and i == DSLOT:
                dbg_dump("o_n", o_n.t[:], o_n)
            kb.op(A, lambda e: e.activation(out=sgn.t[:], in_=tmq.t[:, 0:512], func=EXP, scale=-1.0), R=[tmq], W=[sgn])
            kb.op(A, lambda e: e.activation(out=sgn.t[:], in_=sgn.t[:], func=AF.Ln, bias=ones_f[:, 0:1], scale=1.0), R=[sgn, cst], W=[sgn])
            kb.op(A, lambda e: e.activation(out=sgn.t[:], in_=sgn.t[:], func=EXP, scale=-1.0), R=[sgn], W=[sgn])
            kb.op(P, lambda e: e.tensor_tensor(out=sgn.t[:], in0=sgn.t[:], in1=tmq.t[:, 0:512], op=ALU.mult), R=[sgn, tmq], W=[sgn])
            kb.op(V, lambda e: e.tensor_tensor(out=mix.t[:, 512:1024], in0=o_n.t[:], in1=sgn.t[:], op=ALU.mult), R=[o_n, sgn], W=[mix])
            for c in range(8):
                kb.op(PE, lambda e, c=c: e.transpose(out=psb.t[:, c * 128:(c + 1) * 128], in_=mix.t[:, c * 128:(c + 1) * 128], identity=identb.t[:]), R=[mix, identb], W=[psb])
            kb.op(A, lambda e: e.copy(out=mixT.t[:].rearrange("p a b -> p (a b)"), in_=psb.t[:]), R=[psb], W=[mixT])
            for half, pY in ((0, MISC), (1, ACCC)):
                for c in range(8):
                    kb.op(PE, lambda e, c=c, half=half, pY=pY: e.matmul(pY.t[:], lhsT=mixT.t[:, c, :], rhs=Wo.t[:, c, half * 512:(half + 1) * 512], start=(c == 0), stop=(c == 7)), R=[mixT, Wo], W=[pY])
                kb.op(V, lambda e, half=half, pY=pY: e.tensor_tensor(out=hbuf.t[:, half * 512:(half + 1) * 512], in0=pY.t[:], in1=xrt[bi].t[:, half * 512:(half + 1) * 512], op=ALU.add), R=[pY, xrt[bi]], W=[hbuf])
            hsq = obuf[i % 2]
            kb.op(A, lambda e: e.activation(out=hsq.t[:], in_=hbuf.t[:], func=AF.Square), R=[hbuf], W=[hsq])
            kb.op(V, lambda e: e.tensor_reduce(out=hss.t[:, 0:1], in_=hsq.t[:], axis=mybir.AxisListType.X, op=ALU.add), R=[hsq], W=[hss])
            kb.op(A, lambda e: e.activation(out=hss.t[:, 0:1], in_=hss.t[:, 0:1], func=AF.Ln, bias=epsc, scale=1.0 / D), R=[hss, cst], W=[hss])
            kb.op(A, lambda e: e.activation(out=hss.t[:, 1:2], in_=hss.t[:, 0:1], func=EXP, scale=-0.5), R=[hss], W=[hss])
            ob = obuf[i % 2]
            kb.op(V, lambda e, ob=ob: e.scalar_tensor_tensor(out=ob.t[:], in0=hbuf.t[:], scalar=hss.t[:, 1:2], in1=fnw.t[:], op0=ALU.mult, op1=ALU.mult), R=[hbuf, hss, fnw], W=[ob])
            kb.dma("pool", y[i * 128:(i + 1) * 128, :], ob.t[:], R=[ob], sem=osem[i % 2])
        kb.barrier()
        stk[0].close()
        stk[0] = None

    kb.final_wait(out_sems)
    print('KB nops', kb.nops, kb.cnt)
    return nc, list(dbg_out.keys())


PERM = None


def _perm():
    gq, gk, gv, gz, gb, ga, nq, kc, vc, ks, vs, kw, vw, ng, nz = [np.arange(a, b) for a, b in zip(
        np.cumsum([0, 512, 512, 512, 512, 4, 4, 512, 128, 128, 128, 128, 128, 128, 24]),
        np.cumsum([512, 512, 512, 512, 4, 4, 512, 128, 128, 128, 128, 128, 128, 24, 512]))]
    return np.concatenate([gq, gk, gv, nq, kc, vc, ks, kw, gz, nz, vs, vw, ng, gb, ga])


def _consts():
    c = np.zeros((128, 1024), np.float32)
    j = np.arange(128)[:, None]; i = np.arange(128)[None, :]
    same = (j // 64) == (i // 64)
    c[:, 0:128] = np.eye(128)
    c[:, 128:256] = (same & (j < i))
    c[:, 256:384] = (same & (j <= i))
    c[:, 384:512] = same
    c[0, 512:640] = 1.0
    c[64, 640:768] = 1.0
    c[:, 768:772] = 1.0
    c[127, 771] = 0.0
    c[:, 772:900] = 1.0
    c[:, 900] = EPS
    return c


def _bucket(dist):
    n = np.maximum(dist, 0)
    large = 16 + (np.log(np.maximum(n, 16).astype(np.float32) / np.float32(16)) / np.float32(np.log(1024.0 / 16.0)) * np.float32(16)).astype(np.int32)
    return np.where(n < 16, n, np.minimum(large, 31)).astype(np.int64)


def _gather_bias(rel_ext, dist, valid, g):
    idx = np.where(valid, _bucket(dist), 32)
    t = rel_ext[idx][:, :, 4 * g:4 * g + 4]
    return np.ascontiguousarray(t.transpose(0, 2, 1).reshape(128, 512))


def prep_shared(inputs):
    sh = {}
    sh["w_in"] = np.ascontiguousarray(np.asarray(inputs["w_in"], np.float32)[0][:, _perm()])
    sh["nw"] = np.ascontiguousarray(np.asarray(inputs["norm_w"], np.float32)[0].reshape(8, 128).T)
    cwv = np.asarray(inputs["conv_w"], np.float32)[0]
    sh["convw"] = np.ascontiguousarray(cwv.reshape(4, 12, 128).transpose(2, 1, 0))
    sh["alog_b"] = np.ascontiguousarray(np.broadcast_to(np.tile(np.asarray(inputs["a_log"], np.float32)[0], 64)[None, :], (128, 256)))
    sh["dtb_b"] = np.ascontiguousarray(np.broadcast_to(np.tile(np.asarray(inputs["dt_bias"], np.float32)[0], 64)[None, :], (128, 256)))
    sh["gnw_b"] = np.ascontiguousarray(np.broadcast_to(np.tile(np.asarray(inputs["gdn_norm_w"], np.float32)[0], 4)[None, :], (128, 512)))
    sh["fnw_b"] = np.ascontiguousarray(np.broadcast_to(np.asarray(inputs["final_norm_w"], np.float32)[None, :], (128, D)))
    sh["w_out"] = np.ascontiguousarray(np.asarray(inputs["w_out"], np.float32)[0])
    for nm, k1, kp, k2 in (("k", "cmp_k_w1", "cmp_pe_k", "cmp_k_w2"), ("v", "cmp_v_w1", "cmp_pe_v", "cmp_v_w2")):
        sh["w1" + nm] = np.ascontiguousarray(np.asarray(inputs[k1], np.float32)[0].transpose(1, 0, 2))
        sh["pe" + nm] = np.ascontiguousarray(np.asarray(inputs[kp], np.float32)[0].T)
        sh["w2" + nm] = np.ascontiguousarray(np.asarray(inputs[k2], np.float32)[0])
    sh["consts"] = _consts()
    rel = np.asarray(inputs["rel_bias"], np.float32)
    rel_ext = np.concatenate([rel, np.full((1, 8), NEG, np.float32)], axis=0)
    sh["cexp"] = np.ascontiguousarray(np.broadcast_to(rel[31][None, :], (128, 8)))
    sh["tabCfar"] = np.ascontiguousarray(np.broadcast_to(np.repeat(rel[31].reshape(2, 4), 128, axis=1)[:, None, :], (2, 128, 512)))
    n_in = np.arange(128)[:, None, None]; ct = np.arange(4)[None, :, None]; j = np.arange(128)[None, None, :]
    n = 128 * ct + n_in
    sh["ovl"] = ((n <= 510) & (16 * n < 64 * j + 64) & (16 * n + 32 > 64 * j)).astype(np.float32)
    kk = np.arange(S)[None, :]
    sh["estk"] = (((kk // 64) % 64) == np.arange(64)[:, None]).astype(np.float32)
    ik = np.arange(128)[:, None]; iq = np.arange(128)[None, :]
    par = []
    for a in range(2):
        p = {}
        tS_ = np.empty((2, 9, 128, 512), np.float32)
        tW_ = np.empty((2, 2, 128, 512), np.float32)
        for g in range(2):
            for m in range(9):
                d = m + a - 1
                dist = 128 * d + iq - ik
                tS_[g, m] = _gather_bias(rel_ext, dist, dist >= 0, g)
            for mi in range(2):
                d = 4 + mi + a - 1
                dist = 128 * d + iq - ik
                tW_[mi, g] = _gather_bias(rel_ext, dist, (dist >= 0) & (dist < 512), g)
        p["tabS"] = tS_; p["tabWx"] = tW_
        tC_ = np.empty((2, len(CIDX), 128, 512), np.float32)
        for (i, ct_), idx in CIDX.items():
            e = 2 * i + a - 16 * ct_
            dist = 128 * e + iq - 16 * ik - 31
            for g in range(2):
                tC_[g, idx] = _gather_bias(rel_ext, dist, dist >= 0, g)
        p["tabC"] = tC_
        x_ = np.arange(256)[None, :]; hh = (np.arange(128) // 64)[:, None]
        relj = (x_ - 126) - 2 * a
        adj = np.where(relj > hh, np.float32(NEG), np.where((relj == hh) | (relj == hh - 1), np.float32(1e4), np.float32(0.0)))
        p["adjT"] = np.ascontiguousarray(adj.astype(np.float32))
        j0 = np.zeros((128, 32), np.float32)
        for i in range(32):
            if 2 * i + a >= 1:
                j0[:, i] = 1e4
        p["j0b"] = j0
        p["pw"] = np.ascontiguousarray(np.broadcast_to(np.array([1.0 - a, float(a)], np.float32)[None, :], (128, 2)))
        par.append(p)
    return sh, par


def prep_inputs(inputs, core, sh=None, par=None):
    if sh is None:
        sh, par = prep_shared(inputs)
    b = core // 2
    a = core % 2
    x = np.asarray(inputs["x"], np.float32)[b]
    m = dict(sh)
    m.update(par[a])
    m["xT"] = np.ascontiguousarray(x.T)
    m["xr"] = np.ascontiguousarray(x.reshape(64, 128, D)[a::2].reshape(S // 2, D))
    return m


_NC = [None]


def kernel(**inputs):
    if _NC[0] is None:
        _NC[0] = build()[0]
    nc = _NC[0]
    sh, par = prep_shared(inputs)
    in_maps = [prep_inputs(inputs, c, sh, par) for c in range(8)]
    res = run_bass_kernel_spmd(nc, in_maps, core_ids=list(range(8)))
    out = np.empty((4, S, D), np.float32)
    for c in range(8):
        b, a = c // 2, c % 2
        yc = np.asarray(res.results[c]["y"], np.float32).reshape(32, 128, D)
        out[b].reshape(64, 128, D)[a::2] = yc
    return out
```
